# Optimizing a Trainium2 kernel written in Bass

```python
import math
import numpy as np
import jax
import jax.numpy as jnp
from jax import lax

D_MODEL = 1024
BATCH = 4
SEQ = 8192
DEPTH = 2

GRID_W = 64
CTX_LEN = 256
CHUNK = 64
S5_W = D_MODEL // 4
S5_HC = 16
S5_G = S5_W // S5_HC
S5_P = 64
GLA_W = 3 * D_MODEL // 8
GLA_H = 4
GLA_DV = GLA_W // GLA_H
GLA_DK = GLA_DV // 2
GLA_RANK = 16
GLA_TAU = 16.0
ML_W = 3 * D_MODEL // 8
ML_H = 4
ML_D = ML_W // ML_H
D_FF = ((8 * D_MODEL // 3 + 127) // 128) * 128
ALPHA = (2.0 * DEPTH) ** 0.25
BETA = (8.0 * DEPTH) ** -0.25
LN_EPS = 1e-5
IN_SPLITS = (S5_W, GLA_H * GLA_DK, GLA_H * GLA_DK, GLA_W, GLA_W, 2 * GLA_RANK, ML_W, ML_W, ML_W, ML_W, 2 * ML_H, 2 * ML_H)
D_IN = sum(IN_SPLITS)
IN_OFFSETS = tuple(int(o) for o in np.cumsum(IN_SPLITS)[:-1])

kernel_name = 'hybrid_s5_gla_mlstm_prefix_dit'


def _ln(x):
    xf = x.astype(jnp.float32)
    mu = jnp.mean(xf, axis=-1, keepdims=True)
    var = jnp.mean(jnp.square(xf - mu), axis=-1, keepdims=True)
    return (xf - mu) * lax.rsqrt(var + LN_EPS)


def _modulate(h, shift, scale):
    return (_ln(h) * (1.0 + scale) + shift).astype(h.dtype)


def _post_ln(y, g, b):
    return (_ln(y) * g + b).astype(y.dtype)


def _identity(t):
    return t


def _flip(t):
    return jnp.flip(t, axis=1)


def _bidir(run, ctx_in, lat_in, state0, ctx_out):
    y_x, y_c = None, None
    for d, fl in enumerate((_identity, _flip)):
        yc, st = run(tuple(fl(t) for t in ctx_in[d]), d, state0, ctx_out)
        yx, _ = run(tuple(fl(t) for t in lat_in[d]), d, st, True)
        y_x = fl(yx) if y_x is None else y_x + fl(yx)
        if ctx_out:
            y_c = fl(yc) if y_c is None else y_c + fl(yc)
    return y_x, y_c


def _cplx_combine(e1, e2):
    a1r, a1i, b1r, b1i = e1
    a2r, a2i, b2r, b2i = e2
    ar = a2r * a1r - a2i * a1i
    ai = a2r * a1i + a2i * a1r
    br = a2r * b1r - a2i * b1i + b2r
    bi = a2r * b1i + a2i * b1r + b2i
    return ar, ai, br, bi


def _s5_scan(u, a_re, a_im, log_dt, b_re, b_im, s0):
    dt = jnp.exp(log_dt)[:, None]
    mag = jnp.exp(a_re * dt)
    lam_re = mag * jnp.cos(a_im * dt)
    lam_im = mag * jnp.sin(a_im * dt)
    den = a_re * a_re + a_im * a_im
    z_re = ((lam_re - 1.0) * a_re + lam_im * a_im) / den
    z_im = (lam_im * a_re - (lam_re - 1.0) * a_im) / den
    bb_re = z_re[..., None] * b_re - z_im[..., None] * b_im
    bb_im = z_re[..., None] * b_im + z_im[..., None] * b_re
    bu_re = jnp.einsum('gph,blgh->blgp', bb_re, u)
    bu_im = jnp.einsum('gph,blgh->blgp', bb_im, u)
    s_re, s_im = s0
    bu_re = bu_re.at[:, 0].add(lam_re * s_re - lam_im * s_im)
    bu_im = bu_im.at[:, 0].add(lam_re * s_im + lam_im * s_re)
    la_re = jnp.broadcast_to(lam_re, bu_re.shape)
    la_im = jnp.broadcast_to(lam_im, bu_im.shape)
    _, _, x_re, x_im = lax.associative_scan(_cplx_combine, (la_re, la_im, bu_re, bu_im), axis=1)
    return x_re, x_im


def _s5_mixer(ux, uc, a_re, a_im, log_dt, b_re, b_im, c_re, c_im, d_skip, w_glu, b_glu, ctx_out):
    f32 = jnp.float32
    a_re, a_im, log_dt, b_re, b_im, c_re, c_im, d_skip = (
        t.astype(f32) for t in (a_re, a_im, log_dt, b_re, b_im, c_re, c_im, d_skip))

    def grp(u):
        return u.astype(f32).reshape(u.shape[0], u.shape[1], S5_G, S5_HC)

    gx, gc = grp(ux), grp(uc)

    def run(inp, d, s0, want_out):
        (u,) = inp
        x_re, x_im = _s5_scan(u, a_re[d], a_im[d], log_dt[d], b_re[d], b_im[d], s0)
        final = (x_re[:, -1], x_im[:, -1])
        if not want_out:
            return None, final
        y = jnp.einsum('ghp,blgp->blgh', c_re[d], x_re) - jnp.einsum('ghp,blgp->blgh', c_im[d], x_im)
        return y, final

    zeros = jnp.zeros((gc.shape[0], S5_G, S5_P), f32)
    yx, yc = _bidir(run, [(gc,), (gc,)], [(gx,), (gx,)], (zeros, zeros), ctx_out)

    def post(y, u):
        B_, L = y.shape[:2]
        y = jax.nn.gelu((y + d_skip.reshape(S5_G, S5_HC) * u).reshape(B_, L, S5_W))
        return y * jax.nn.sigmoid(y @ w_glu + b_glu)

    return post(yx, gx), (post(yc, gc) if ctx_out else None)


def _gla_dir(q, k, v, loga, S0, want_out):
    B_, L, H, DK = q.shape
    DV = v.shape[-1]
    N = L // CHUNK
    q, k, loga = (t.reshape(B_, N, CHUNK, H, DK) for t in (q, k, loga))
    v = v.reshape(B_, N, CHUNK, H, DV)
    b = jnp.cumsum(loga, axis=2)
    b_last = b[:, :, -1]
    dS = jnp.einsum('bnshk,bnshv->bnhkv', k * jnp.exp(b_last[:, :, None] - b), v)

    def step(S, inp):
        decay, ds = inp
        return decay[..., None] * S + ds, S

    S_final, S_enter = lax.scan(step, S0, (jnp.moveaxis(jnp.exp(b_last), 1, 0), jnp.moveaxis(dS, 1, 0)))
    if not want_out:
        return None, S_final
    S_enter = jnp.moveaxis(S_enter, 0, 1)
    qd = q * jnp.exp(b)
    causal = jnp.tril(jnp.ones((CHUNK, CHUNK), dtype=bool))
    att = jnp.einsum('bnthk,bnshk->bnhts', qd, k * jnp.exp(-b))
    att = jnp.where(causal, att, 0.0)
    o = jnp.einsum('bnthk,bnhkv->bnthv', qd, S_enter) + jnp.einsum('bnhts,bnshv->bnthv', att, v)
    return o.reshape(B_, L, H, DV), S_final


def _gla_mixer(parts_x, parts_c, w_a2, b_a, g, ctx_out):
    f32 = jnp.float32
    w_a2, b_a, g = (t.astype(f32) for t in (w_a2, b_a, g))

    def prep(parts):
        q, k, v, r, lr = (t.astype(f32) for t in parts)
        B_, L, _ = q.shape
        q = q.reshape(B_, L, GLA_H, GLA_DK) * GLA_DK ** -0.5
        k = k.reshape(B_, L, GLA_H, GLA_DK)
        v = v.reshape(B_, L, GLA_H, GLA_DV)
        z = jnp.einsum('blzr,zrk->blzk', lr.reshape(B_, L, 2, GLA_RANK), w_a2) + b_a
        loga = (jax.nn.log_sigmoid(z) / GLA_TAU).reshape(B_, L, 2, GLA_H, GLA_DK)
        return [(q, k, v, loga[:, :, d]) for d in range(2)], r

    ins_x, r_x = prep(parts_x)
    ins_c, r_c = prep(parts_c)

    def run(inp, d, s0, want_out):
        return _gla_dir(*inp, s0, want_out)

    S0 = jnp.zeros((r_c.shape[0], GLA_H, GLA_DK, GLA_DV), f32)
    ox, oc = _bidir(run, ins_c, ins_x, S0, ctx_out)

    def post(o, r):
        B_, L = r.shape[:2]
        return jax.nn.silu(r) * (_ln(o) * g.reshape(GLA_H, GLA_DV)).reshape(B_, L, GLA_W)

    return post(ox, r_x), (post(oc, r_c) if ctx_out else None)


def _mlstm_dir(q, k, v, ig, lf, state0, want_out):
    B_, L, H, DK = q.shape
    DV = v.shape[-1]
    N = L // CHUNK
    q, k = (t.reshape(B_, N, CHUNK, H, DK) for t in (q, k))
    v = v.reshape(B_, N, CHUNK, H, DV)
    ig, lf = (t.reshape(B_, N, CHUNK, H) for t in (ig, lf))
    F = jnp.cumsum(lf, axis=2)
    F_last = F[:, :, -1]
    g = F_last[:, :, None] - F + ig
    m_loc = jnp.max(g, axis=2)
    w = jnp.exp(g - m_loc[:, :, None])
    dC = jnp.einsum('bnsh,bnshk,bnshv->bnhkv', w, k, v)
    dn = jnp.einsum('bnsh,bnshk->bnhk', w, k)

    def step(carry, inp):
        C, n, m = carry
        fl, ml, dc, dnn = inp
        m_new = jnp.maximum(fl + m, ml)
        a = jnp.exp(fl + m - m_new)
        bb = jnp.exp(ml - m_new)
        C_new = a[..., None, None] * C + bb[..., None, None] * dc
        n_new = a[..., None] * n + bb[..., None] * dnn
        return (C_new, n_new, m_new), (C, n, m)

    xs = tuple(jnp.moveaxis(t, 1, 0) for t in (F_last, m_loc, dC, dn))
    final, enter = lax.scan(step, state0, xs)
    if not want_out:
        return None, final
    C_e, n_e, m_e = (jnp.moveaxis(t, 0, 1) for t in enter)
    causal = jnp.tril(jnp.ones((CHUNK, CHUNK), dtype=bool))
    Dlog = F[:, :, :, None, :] - F[:, :, None, :, :] + ig[:, :, None, :, :]
    Dlog = jnp.where(causal[None, None, :, :, None], Dlog, -jnp.inf)
    inter = F + m_e[:, :, None]
    m_t = jnp.maximum(inter, jnp.max(Dlog, axis=3))
    w_inter = jnp.exp(inter - m_t)
    P = jnp.exp(Dlog - m_t[:, :, :, None]) * jnp.einsum('bnthk,bnshk->bntsh', q, k)
    num = w_inter[..., None] * jnp.einsum('bnthk,bnhkv->bnthv', q, C_e) + jnp.einsum('bntsh,bnshv->bnthv', P, v)
    den = w_inter * jnp.einsum('bnthk,bnhk->bnth', q, n_e) + jnp.sum(P, axis=3)
    h = num / jnp.maximum(jnp.abs(den), jnp.exp(-m_t))[..., None]
    return h.reshape(B_, L, H, DV), final


def _mlstm_mixer(parts_x, parts_c, i_bias, f_bias, g, ctx_out):
    f32 = jnp.float32
    i_bias, f_bias, g = (t.astype(f32) for t in (i_bias, f_bias, g))

    def prep(parts):
        q, k, v, o, ig, fg = (t.astype(f32) for t in parts)
        B_, L, _ = q.shape
        q = q.reshape(B_, L, ML_H, ML_D)
        k = k.reshape(B_, L, ML_H, ML_D) * ML_D ** -0.5
        v = v.reshape(B_, L, ML_H, ML_D)
        ig = ig.reshape(B_, L, 2, ML_H) + i_bias
        lf = jax.nn.log_sigmoid(fg.reshape(B_, L, 2, ML_H) + f_bias)
        return [(q, k, v, ig[:, :, d], lf[:, :, d]) for d in range(2)], o

    ins_x, o_x = prep(parts_x)
    ins_c, o_c = prep(parts_c)
    Bc = o_c.shape[0]
    state0 = (jnp.zeros((Bc, ML_H, ML_D, ML_D), f32), jnp.zeros((Bc, ML_H, ML_D), f32), jnp.zeros((Bc, ML_H), f32))

    def run(inp, d, s0, want_out):
        return _mlstm_dir(*inp, s0, want_out)

    hx, hc = _bidir(run, ins_c, ins_x, state0, ctx_out)

    def post(h, o):
        B_, L = o.shape[:2]
        return jax.nn.sigmoid(o) * (_ln(h) * g.reshape(ML_H, ML_D)).reshape(B_, L, ML_W)

    return post(hx, o_x), (post(hc, o_c) if ctx_out else None)


def _token_mixers(zx, zc, s5_a_re, s5_a_im, s5_log_dt, s5_b_re, s5_b_im, s5_c_re, s5_c_im, s5_d,
                  s5_w_glu, s5_b_glu, gla_w_a2, gla_b_a, gla_g, ml_i_bias, ml_f_bias, ml_g, ctx_out):
    px = jnp.split(zx, IN_OFFSETS, axis=-1)
    pc = jnp.split(zc, IN_OFFSETS, axis=-1)
    s5x, s5c = _s5_mixer(px[0], pc[0], s5_a_re, s5_a_im, s5_log_dt, s5_b_re, s5_b_im, s5_c_re, s5_c_im,
                         s5_d, s5_w_glu, s5_b_glu, ctx_out)
    glx, glc = _gla_mixer(px[1:6], pc[1:6], gla_w_a2, gla_b_a, gla_g, ctx_out)
    mlx, mlc = _mlstm_mixer(px[6:12], pc[6:12], ml_i_bias, ml_f_bias, ml_g, ctx_out)
    out_x = jnp.concatenate([s5x, glx, mlx], axis=-1).astype(zx.dtype)
    out_c = jnp.concatenate([s5c, glc, mlc], axis=-1).astype(zc.dtype) if ctx_out else None
    return out_x, out_c


def _conv_ffn(h, w_up, w_dconv, b_dconv, w_down, rows):
    B_, L, _ = h.shape
    a, v = jnp.split(h @ w_up, 2, axis=-1)
    a = a.reshape(B_, rows, L // rows, D_FF)
    a = lax.conv_general_dilated(a, w_dconv[:, :, None, :], (1, 1), 'SAME',
                                 dimension_numbers=('NHWC', 'HWIO', 'NHWC'), feature_group_count=D_FF)
    a = a.reshape(B_, L, D_FF) + b_dconv
    return (jax.nn.gelu(a) * v) @ w_down


def setup_inputs(seed: int = 0) -> dict:
    key = jax.random.key(seed)
    ks = iter(jax.random.split(key, 48))
    f32 = jnp.float32

    def nrm(shape, scale):
        return scale * jax.random.normal(next(ks), shape, f32)

    Lh, D = DEPTH, D_MODEL
    n_idx = jnp.arange(S5_P, dtype=f32)
    return {
        'x': nrm((BATCH, SEQ, D), 1.0),
        'c': nrm((BATCH, D), 1.0),
        'ctx': nrm((BATCH, CTX_LEN, D), 1.0),
        'c_ctx': nrm((D,), 1.0),
        'w_ada': nrm((Lh, D, 6 * D), 0.5 * D ** -0.5),
        'b_ada': nrm((Lh, 6 * D), 0.02),
        'w_in': nrm((Lh, D, D_IN), D ** -0.5),
        's5_a_re': -0.5 * jnp.exp(nrm((Lh, 2, S5_G, S5_P), 0.05)),
        's5_a_im': math.pi * n_idx + nrm((Lh, 2, S5_G, S5_P), 0.05),
        's5_log_dt': jax.random.uniform(next(ks), (Lh, 2, S5_G), f32, math.log(1e-3), math.log(1e-1)),
        's5_b_re': nrm((Lh, 2, S5_G, S5_P, S5_HC), (2.0 * S5_HC) ** -0.5),
        's5_b_im': nrm((Lh, 2, S5_G, S5_P, S5_HC), (2.0 * S5_HC) ** -0.5),
        's5_c_re': nrm((Lh, 2, S5_G, S5_HC, S5_P), S5_P ** -0.5),
        's5_c_im': nrm((Lh, 2, S5_G, S5_HC, S5_P), S5_P ** -0.5),
        's5_d': nrm((Lh, S5_W), 1.0),
        's5_w_glu': nrm((Lh, S5_W, S5_W), S5_W ** -0.5),
        's5_b_glu': nrm((Lh, S5_W), 0.02),
        'gla_w_a2': nrm((Lh, 2, GLA_RANK, GLA_H * GLA_DK), GLA_RANK ** -0.5),
        'gla_b_a': nrm((Lh, 2, GLA_H * GLA_DK), 0.1),
        'gla_g': 1.0 + nrm((Lh, GLA_W), 0.02),
        'ml_i_bias': nrm((Lh, 2, ML_H), 0.1),
        'ml_f_bias': jnp.linspace(3.0, 6.0, ML_H, dtype=f32) + nrm((Lh, 2, ML_H), 0.1),
        'ml_g': 1.0 + nrm((Lh, ML_W), 0.02),
        'w_out': nrm((Lh, D, D), BETA * D ** -0.5),
        'ln1_g': 1.0 + nrm((Lh, D), 0.02),
        'ln1_b': nrm((Lh, D), 0.02),
        'w_up': nrm((Lh, D, 2 * D_FF), D ** -0.5),
        'w_dconv': nrm((Lh, 3, 3, D_FF), 1.0 / 3.0),
        'b_dconv': nrm((Lh, D_FF), 0.02),
        'w_down': nrm((Lh, D_FF, D), BETA * D_FF ** -0.5),
        'ln2_g': 1.0 + nrm((Lh, D), 0.02),
        'ln2_b': nrm((Lh, D), 0.02),
    }


def reference(x, c, ctx, c_ctx, w_ada, b_ada, w_in, s5_a_re, s5_a_im, s5_log_dt, s5_b_re, s5_b_im,
              s5_c_re, s5_c_im, s5_d, s5_w_glu, s5_b_glu, gla_w_a2, gla_b_a, gla_g, ml_i_bias, ml_f_bias,
              ml_g, w_out, ln1_g, ln1_b, w_up, w_dconv, b_dconv, w_down, ln2_g, ln2_b):
    rows = x.shape[1] // GRID_W
    hx, hc = x, ctx
    for l in range(DEPTH):
        last = l == DEPTH - 1
        mod_x = jax.nn.silu(c) @ w_ada[l] + b_ada[l]
        mod_c = jax.nn.silu(c_ctx) @ w_ada[l] + b_ada[l]
        sh1, sc1, g1, sh2, sc2, g2 = (m[:, None, :] for m in jnp.split(mod_x, 6, axis=-1))
        csh1, csc1, cg1, csh2, csc2, cg2 = jnp.split(mod_c, 6, axis=-1)
        zx = _modulate(hx, sh1, sc1) @ w_in[l]
        zc = _modulate(hc, csh1, csc1) @ w_in[l]
        mx, mc = _token_mixers(zx, zc, s5_a_re[l], s5_a_im[l], s5_log_dt[l], s5_b_re[l], s5_b_im[l],
                               s5_c_re[l], s5_c_im[l], s5_d[l], s5_w_glu[l], s5_b_glu[l], gla_w_a2[l],
                               gla_b_a[l], gla_g[l], ml_i_bias[l], ml_f_bias[l], ml_g[l], not last)
        hx = _post_ln(ALPHA * hx + g1 * (mx @ w_out[l]), ln1_g[l], ln1_b[l])
        fx = _conv_ffn(_modulate(hx, sh2, sc2), w_up[l], w_dconv[l], b_dconv[l], w_down[l], rows)
        hx = _post_ln(ALPHA * hx + g2 * fx, ln2_g[l], ln2_b[l])
        if not last:
            hc = _post_ln(ALPHA * hc + cg1 * (mc @ w_out[l]), ln1_g[l], ln1_b[l])
            fc = _conv_ffn(_modulate(hc, csh2, csc2), w_up[l], w_dconv[l], b_dconv[l], w_down[l], 1)
            hc = _post_ln(ALPHA * hc + cg2 * fc, ln2_g[l], ln2_b[l])
    return hx
```

```python
import bisect
import numpy as np
import concourse.bass as bass
import concourse.mybir as mybir
from concourse.bass_utils import run_bass_kernel_spmd

F32, BF16 = mybir.dt.float32, mybir.dt.bfloat16
AF = mybir.ActivationFunctionType
ALU = mybir.AluOpType
AX = mybir.AxisListType

D = 1024
CTX = 256
GRID_W = 64
S5_W, S5_G, S5_P, S5_HC = 256, 16, 64, 16
GLA_H, GLA_DK, GLA_DV, GLA_RANK = 4, 48, 96, 16
ML_H, ML_D = 4, 96
D_FF = 2816
DEPTH = 2
ALPHA = (2.0 * DEPTH) ** 0.25
LN_EPS = 1e-5
IN_SPLITS = (256, 192, 192, 384, 384, 32, 384, 384, 384, 384, 8, 8)
IN_OFF = [0] + list(np.cumsum(IN_SPLITS))
NFM = 15
NTM = 1536
TWO_PI = 2.0 * np.pi


class Dep:
    __slots__ = ("w", "r")

    def __init__(self):
        self.w = None
        self.r = {}


class Buf:
    def __init__(self, t):
        self.t = t
        self.d = Dep()

    def __getitem__(self, k):
        return self.t[k]


class Eng:
    def __init__(self, K, name, raw, is_pe=False):
        self.K, self.name, self.raw, self.is_pe = K, name, raw, is_pe
        self.sem = K.nc.semaphore(name + "_sem").__enter__()
        self.seq = 0
        self.count = 0
        self.inc_seq = []
        self.inc_cnt = []
        self.last = None
        self.last_seq = -1
        self.waited = {}

    def resolve(self, seq):
        i = bisect.bisect_left(self.inc_seq, seq)
        if i < len(self.inc_seq):
            return self.inc_cnt[i]
        assert self.last is not None and self.last_seq >= seq
        self.count += 1
        self.last.then_inc(self.sem, 1)
        self.inc_seq.append(self.last_seq)
        self.inc_cnt.append(self.count)
        return self.count

    def wait_tok(self, tok):
        if tok[0] == "c":
            e = tok[1]
            val = e.resolve(tok[2])
            key, sem = e.name, e.sem
        else:
            key, sem, val = ("dma", tok[1]), self.K.dsem[tok[1]], tok[2]
        if self.waited.get(key, 0) >= val:
            return
        self.raw.wait_ge(sem, val)
        self.waited[key] = val


class Kern:
    NDSEM = 24

    def __init__(self):
        self.nc = bass.Bass("TRN2", target_bir_lowering=False)
        nc = self.nc
        self.pe = Eng(self, "pe", nc.tensor, True)
        self.act = Eng(self, "act", nc.scalar)
        self.dve = Eng(self, "dve", nc.vector)
        self.pool = self.dve
        self.sp = Eng(self, "sp", nc.sync)
        self.engs = [self.pe, self.act, self.dve, self.sp]
        self.dsem = [nc.semaphore("dsem%d" % i).__enter__() for i in range(self.NDSEM)]
        self.dcnt = [0] * self.NDSEM
        self.dnext = 0
        self.sb_bytes = 0
        self.n_instr = 0

    def sb(self, name, shape, dt=F32):
        t = self.nc.sbuf_tensor(name, list(shape), dt).__enter__()
        n = 1
        for s in shape[1:]:
            n *= s
        self.sb_bytes += n * (4 if dt == F32 else 2)
        return Buf(t)

    def ps(self, name, shape, dt=F32):
        return Buf(self.nc.psum_tensor(name, list(shape), dt).__enter__())

    def dram(self, name, shape, dt=F32, kind="Internal"):
        return self.nc.dram_tensor(name, list(shape), dt, kind=kind).ap()

    def _deps(self, reads, writes):
        toks = []
        for d in reads:
            d = d.d if isinstance(d, Buf) else d
            if d.w is not None:
                toks.append(d.w)
        for d in writes:
            d = d.d if isinstance(d, Buf) else d
            if d.w is not None:
                toks.append(d.w)
            toks.extend(d.r.values())
        return toks

    def op(self, eng, fn, reads=(), writes=()):
        for t in self._deps(reads, writes):
            if eng.is_pe and t[0] == "c" and t[1] is eng:
                continue
            eng.wait_tok(t)
        ins = fn(eng.raw)
        seq = eng.seq
        eng.seq += 1
        eng.last, eng.last_seq = ins, seq
        tok = ("c", eng, seq)
        for d in reads:
            d = d.d if isinstance(d, Buf) else d
            d.r[eng.name] = tok
        for d in writes:
            d = d.d if isinstance(d, Buf) else d
            d.w = tok
            d.r = {}
        self.n_instr += 1
        return ins

    def dma(self, out, in_, reads=(), writes=(), q=None, **kw):
        q = q or self.sp
        for t in self._deps(reads, writes):
            q.wait_tok(t)
        s = self.dnext
        self.dnext = (self.dnext + 1) % self.NDSEM
        if self.dcnt[s] > 0:
            q.wait_tok(("d", s, 16 * self.dcnt[s]))
        self.dcnt[s] += 1
        q.raw.dma_start(out=out, in_=in_, **kw).then_inc(self.dsem[s], 16)
        tok = ("d", s, 16 * self.dcnt[s])
        for d in reads:
            d = d.d if isinstance(d, Buf) else d
            d.r[("dma", s, self.dcnt[s])] = tok
        for d in writes:
            d = d.d if isinstance(d, Buf) else d
            d.w = tok
            d.r = {}
        self.n_instr += 1
        return tok

    def finish(self):
        for e in self.engs:
            if e is self.sp or e.last is None:
                continue
            self.sp.wait_tok(("c", e, e.last_seq))
        for s in range(self.NDSEM):
            if self.dcnt[s] > 0:
                self.sp.wait_tok(("d", s, 16 * self.dcnt[s]))

    def mm(self, out, lhsT, rhs, start, stop, reads, writes):
        return self.op(self.pe, lambda e: e.matmul(out, lhsT=lhsT, rhs=rhs, start=start, stop=stop),
                       reads, writes)

    def tr(self, out, in_, ident, reads, writes):
        return self.op(self.pe, lambda e: e.transpose(out, in_, ident), reads, writes)


def _layout_w_in(w_in):
    nl = w_in.shape[0]
    wF = np.zeros((nl, w_in.shape[1], NFM * 128), np.float32)
    o = IN_OFF
    wF[:, :, 0:256] = w_in[:, :, 0:256]
    for h in range(4):
        t, s = 2 + h // 2, 64 * (h % 2)
        wF[:, :, t * 128 + s: t * 128 + s + 48] = w_in[:, :, o[1] + 48 * h: o[1] + 48 * (h + 1)]
        t = 4 + h // 2
        wF[:, :, t * 128 + s: t * 128 + s + 48] = w_in[:, :, o[2] + 48 * h: o[2] + 48 * (h + 1)]
    b = 6 * 128
    wF[:, :, b + 0: b + 16] = w_in[:, :, o[5]: o[5] + 16]
    wF[:, :, b + 32: b + 48] = w_in[:, :, o[5] + 16: o[5] + 32]
    wF[:, :, b + 64: b + 72] = w_in[:, :, o[10]: o[10] + 8]
    wF[:, :, b + 72: b + 80] = w_in[:, :, o[11]: o[11] + 8]
    for h in range(4):
        wF[:, :, (7 + h) * 128: (7 + h) * 128 + 96] = w_in[:, :, o[6] + 96 * h: o[6] + 96 * (h + 1)]
        wF[:, :, (11 + h) * 128: (11 + h) * 128 + 96] = w_in[:, :, o[7] + 96 * h: o[7] + 96 * (h + 1)]
    wT = np.concatenate([w_in[:, :, o[3]:o[4]], w_in[:, :, o[4]:o[5]],
                         w_in[:, :, o[8]:o[9]], w_in[:, :, o[9]:o[10]]], axis=2)
    return np.ascontiguousarray(wF), np.ascontiguousarray(wT)


FM_ROWS = [128, 128, 128, 128, 128, 128, 80] + [96] * 8
FM_SCALE = [1.0, 1.0, GLA_DK ** -0.5, GLA_DK ** -0.5, 1.0, 1.0, 1.0] + [1.0] * 4 + [ML_D ** -0.5] * 4


import contextlib


class Phase:
    def __init__(self, K):
        self.K = K
        self.stack = contextlib.ExitStack()

    UID = [0]

    def sb(self, name, shape, dt=F32):
        Phase.UID[0] += 1
        t = self.stack.enter_context(self.K.nc.sbuf_tensor("%s_u%d" % (name, Phase.UID[0]), list(shape), dt))
        return Buf(t)

    def ps(self, name, shape=(128, 512), dt=F32):
        Phase.UID[0] += 1
        t = self.stack.enter_context(self.K.nc.psum_tensor("%s_u%d" % (name, Phase.UID[0]), list(shape), dt))
        return Buf(t)

    def close(self):
        self.K.barrier()
        self.stack.close()


def _barrier(K):
    toks = []
    for e in K.engs:
        if e.last is not None:
            toks.append(("c", e, e.last_seq))
    for s in range(K.NDSEM):
        if K.dcnt[s] > 0:
            toks.append(("d", s, 16 * K.dcnt[s]))
    for e in K.engs:
        for t in toks:
            if t[0] == "c" and t[1] is e:
                continue
            e.wait_tok(t)


Kern.barrier = _barrier


def _cvt(K, i, out, in_, reads, writes, scale=None):
    e = (K.act, K.dve, K.pool)[i % 3]
    if e is K.act:
        if scale is None:
            return K.op(e, lambda r: r.copy(out=out, in_=in_), reads, writes)
        return K.op(e, lambda r: r.mul(out=out, in_=in_, mul=scale), reads, writes)
    if scale is None:
        return K.op(e, lambda r: r.tensor_copy(out=out, in_=in_), reads, writes)
    return K.op(e, lambda r: r.tensor_scalar(out=out, in0=in_, scalar1=scale, scalar2=None, op0=ALU.mult),
                reads, writes)


def _evac(K, i, out, in_, reads, writes, scale=None):
    e = (K.act, K.dve)[i % 2]
    if e is K.act:
        if scale is None:
            return K.op(e, lambda r: r.copy(out=out, in_=in_), reads, writes)
        return K.op(e, lambda r: r.mul(out=out, in_=in_, mul=scale), reads, writes)
    if scale is None:
        return K.op(e, lambda r: r.tensor_copy(out=out, in_=in_), reads, writes)
    return K.op(e, lambda r: r.tensor_scalar(out=out, in0=in_, scalar1=scale, scalar2=None, op0=ALU.mult),
                reads, writes)


def load_weight_bf16(K, ph, dst, src, ncols, stage, cnt=[0]):
    for k in range(src.shape[0] // 128):
        sw = stage[0].t.shape[1]
        for c0 in range(0, ncols, sw):
            c1 = min(ncols, c0 + sw)
            st = stage[cnt[0] % len(stage)]
            K.dma(st[:, 0:c1 - c0], src[k * 128:(k + 1) * 128, c0:c1], writes=[st])
            _cvt(K, cnt[0], dst[:, k, c0:c1], st[:, 0:c1 - c0], [st], [dst])
            cnt[0] += 1


def phase0(K, G, l):
    ph = Phase(K)
    sc = ph.sb("p0_sc", [128, 16])
    screp = ph.sb("p0_screp", [128, 16, 128])
    cv = ph.sb("p0_cv", [128, 16])
    bF = ph.sb("p0_bF", [128, 48])
    wst = [ph.sb("p0_w%d" % i, [128, 6144]) for i in range(2)]
    brow = ph.sb("p0_brow", [128, 1024])
    psF = ph.ps("p0_psF")
    psR = [[ph.ps("p0_psR%d%d" % (j, hf)) for hf in range(2)] for j in range(2)]
    K.dma(cv[:, :], G["cvec"][:, :], writes=[cv])
    K.dma(bF[:, :], G["b_adaF"][l], writes=[bF])
    K.op(K.act, lambda r: r.activation(out=sc[:, :], in_=cv[:, :], func=AF.Silu), [cv], [sc])
    for i in range(16):
        K.op(K.dve, lambda r: r.tensor_copy(out=screp[:, i, :], in_=sc[:, i:i + 1].to_broadcast([128, 128])),
             [sc], [screp])
    n = 0
    for cb in range(8):
        st = wst[n % 2]
        n += 1
        K.dma(st[:, :].rearrange("p (k c) -> p k c", k=8),
              G["w_ada"][l, :, cb * 768:(cb + 1) * 768].rearrange("(k p) c -> p k c", p=128), writes=[st])
        for c in range(6):
            cc = cb * 6 + c
            for k in range(8):
                K.mm(psF[:, 2 * cc:2 * cc + 2], st[:, k * 768 + c * 128: k * 768 + (c + 1) * 128],
                     sc[:, 2 * k:2 * k + 2], k == 0, k == 7, [st, sc], [psF])
    modF = G["modF"]
    K.op(K.dve, lambda r: r.tensor_tensor(
        out=modF[:, :, :], in0=psF[:, 0:96].rearrange("p (c j) -> p c j", j=2),
        in1=bF[:, :].unsqueeze(2).to_broadcast([128, 48, 2]), op=ALU.add), [psF, bF], [modF])
    for c0 in (8, 32):
        K.op(K.dve, lambda r: r.tensor_scalar(out=modF[:, c0:c0 + 8, :], in0=modF[:, c0:c0 + 8, :],
                                              scalar1=1.0, scalar2=None, op0=ALU.add), [modF], [modF])
    for w, c0 in enumerate((2048, 5120)):
        K.dma(brow[:, :], G["b_ada"][l:l + 1, c0:c0 + 1024].to_broadcast([128, 1024]), writes=[brow])
        for k in range(8):
            st = wst[n % 2]
            n += 1
            K.dma(st[:, 0:1024], G["w_ada"][l, k * 128:(k + 1) * 128, c0:c0 + 1024], writes=[st])
            for j in range(2):
                for hf in range(2):
                    K.mm(psR[j][hf][:, :], screp[:, 2 * k + j, :], st[:, hf * 512:(hf + 1) * 512],
                         k == 0, k == 7, [st, screp], [psR[j][hf]])
        for j in range(2):
            for hf in range(2):
                dst = G["modR"][w][j]
                K.op(K.dve, lambda r: r.tensor_tensor(out=dst[:, hf * 512:(hf + 1) * 512], in0=psR[j][hf][:, :],
                                                      in1=brow[:, hf * 512:(hf + 1) * 512], op=ALU.add),
                     [psR[j][hf], brow], [dst])
    ph.close()


G_EPS = [None]


def ln_stats(K, xi, bn, mv, rs, nb):
    K.op(K.dve, lambda r: r.bn_stats(out=bn[:, 0:6], in_=xi[:, 0:512]), [xi], [bn])
    K.op(K.dve, lambda r: r.bn_stats(out=bn[:, 6:12], in_=xi[:, 512:1024]), [xi], [bn])
    K.op(K.dve, lambda r: r.bn_aggr(out=mv[:, :], in_=bn[:, :]), [bn], [mv])
    K.op(K.act, lambda r: r.activation(out=rs[:, :], in_=mv[:, 1:2], func=AF.Sqrt, bias=G_EPS[0][:, 0:1]),
         [mv, G_EPS[0]], [rs])
    K.op(K.dve, lambda r: r.reciprocal(out=rs[:, :], in_=rs[:, :]), [rs], [rs])
    K.op(K.dve, lambda r: r.tensor_scalar(out=nb[:, :], in0=mv[:, 0:1], scalar1=rs[:, 0:1], scalar2=-1.0,
                                          op0=ALU.mult, op1=ALU.mult), [mv, rs], [nb])


def phaseA(K, G, l, Hin, segs):
    ph = Phase(K)
    wFb = ph.sb("a_wF", [128, 8, NFM * 128], BF16)
    wTb = ph.sb("a_wT", [128, 8, NTM], BF16)
    stage = [ph.sb("a_st%d" % i, [128, 2048]) for i in range(2)]
    load_weight_bf16(K, ph, wFb, G["wF"][l], NFM * 128, stage)
    load_weight_bf16(K, ph, wTb, G["wT"][l], NTM, stage)
    xin = [ph.sb("a_xin%d" % i, [128, 1024]) for i in range(2)]
    xn = [ph.sb("a_xn%d" % i, [128, 1024]) for i in range(2)]
    bn = [ph.sb("a_bn%d" % i, [128, 12]) for i in range(2)]
    mv = [ph.sb("a_mv%d" % i, [128, 2]) for i in range(2)]
    rs = [ph.sb("a_rs%d" % i, [128, 1]) for i in range(2)]
    nb = [ph.sb("a_nb%d" % i, [128, 1]) for i in range(2)]
    xmT = ph.sb("a_xmT", [128, 8, 512], BF16)
    stF = [ph.sb("a_stF%d" % i, [128, 512]) for i in range(3)]
    stT = [ph.sb("a_stT%d" % i, [128, NTM]) for i in range(2)]
    psT = [ph.ps("a_psT%d" % i, [128, 1024]) for i in range(2)]
    psM = [ph.ps("a_psM%d" % i) for i in range(4)]
    ident = G["identf"]
    modF = G["modF"]
    cnt = 0
    ne = 0
    for (t0, n, is_ctx) in segs:
        j = 1 if is_ctx else 0
        nt = n // 128
        for ti in range(nt):
            s = cnt % 2
            cnt += 1
            K.dma(xin[s][:, :], Hin[t0 + 128 * ti: t0 + 128 * (ti + 1), :], writes=[xin[s]])
            ln_stats(K, xin[s], bn[s], mv[s], rs[s], nb[s])
            K.op(K.act, lambda r: r.activation(out=xn[s][:, :], in_=xin[s][:, :], func=AF.Identity,
                                               scale=rs[s][:, 0:1], bias=nb[s][:, 0:1]),
                 [xin[s], rs[s], nb[s]], [xn[s]])
            pt = psT[s % 2]
            for k in range(8):
                K.tr(pt[:, k * 128:(k + 1) * 128], xn[s][:, k * 128:(k + 1) * 128], ident[:, :],
                     [xn[s], ident], [pt])
            for k in range(8):
                src = pt[:, k * 128:(k + 1) * 128]
                dst = xmT[:, k, ti * 128:(ti + 1) * 128]
                if k % 2 == 0:
                    K.op(K.act, lambda r: r.activation(out=dst, in_=src, func=AF.Identity,
                                                       scale=modF[:, 8 + k, j:j + 1], bias=modF[:, k, j:j + 1]),
                         [pt, modF], [xmT])
                else:
                    K.op(K.dve, lambda r: r.tensor_scalar(out=dst, in0=src, scalar1=modF[:, 8 + k, j:j + 1],
                                                          scalar2=modF[:, k, j:j + 1], op0=ALU.mult, op1=ALU.add),
                         [pt, modF], [xmT])
        for m in range(NFM):
            M = FM_ROWS[m]
            p = psM[m % 4]
            for k in range(8):
                K.mm(p[0:M, 0:n], wFb[:, k, m * 128: m * 128 + M], xmT[:, k, 0:n], k == 0, k == 7, [wFb, xmT], [p])
            s = stF[m % 3]
            _evac(K, ne, s[0:M, 0:n], p[0:M, 0:n], [p], [s], None if FM_SCALE[m] == 1.0 else FM_SCALE[m])
            ne += 1
            K.dma(G["zF"][m * 128: m * 128 + M, t0:t0 + n], s[0:M, 0:n], reads=[s])
        for ti in range(nt):
            s = stT[ti % 2]
            for c in range(3):
                p = psM[(c + ti * 3 + 3) % 4]
                for k in range(8):
                    K.mm(p[:, :], xmT[:, k, ti * 128:(ti + 1) * 128], wTb[:, k, c * 512:(c + 1) * 512],
                         k == 0, k == 7, [wTb, xmT], [p])
                _evac(K, ne, s[:, c * 512:(c + 1) * 512], p[:, :], [p], [s])
                ne += 1
            K.dma(G["zT"][t0 + ti * 128: t0 + (ti + 1) * 128, :], s[:, :], reads=[s])
    ph.close()


def make_segs(L):
    return [(0, CTX, True)] + [(CTX + 512 * i, 512, False) for i in range(L // 512)]


def _layout_s5(a_re, a_im, log_dt, b_re, b_im, c_re, c_im):
    nl = a_re.shape[0]
    par = np.zeros((nl, 2, 128, 24), np.float32)
    BT = np.zeros((nl, 2, 2, 128, 8, 128), np.float32)
    CT = np.zeros((nl, 2, 2, 128, 8, 128), np.float32)
    for b in range(8):
        for gl in range(2):
            g = 2 * b + gl
            g8 = g % 8
            par[:, :, gl * 64:(gl + 1) * 64, b] = a_re[:, :, g, :]
            par[:, :, gl * 64:(gl + 1) * 64, 8 + b] = a_im[:, :, g, :]
            par[:, :, gl * 64:(gl + 1) * 64, 16 + b] = log_dt[:, :, g][:, :, None]
            for x, (bb, cc) in enumerate(((b_re, c_re), (b_im, c_im))):
                BT[:, :, x, g8 * 16:(g8 + 1) * 16, b, gl * 64:(gl + 1) * 64] = bb[:, :, g].transpose(0, 1, 3, 2)
                CT[:, :, x, gl * 64:(gl + 1) * 64, b, g8 * 16:(g8 + 1) * 16] = cc[:, :, g].transpose(0, 1, 3, 2)
    return par, BT.reshape(nl, 2, 2, 128, 1024), CT.reshape(nl, 2, 2, 128, 1024)


def mixer_inputs(p):
    par, BT, CT = _layout_s5(p["s5_a_re"], p["s5_a_im"], p["s5_log_dt"], p["s5_b_re"], p["s5_b_im"],
                             p["s5_c_re"], p["s5_c_im"])
    nl = par.shape[0]
    out = dict(s5_par=par, s5_BT=BT, s5_CT=CT,
               s5_d2=np.ascontiguousarray(p["s5_d"].reshape(nl, 2, 128).transpose(0, 2, 1)),
               s5_bg2=np.ascontiguousarray(p["s5_b_glu"].reshape(nl, 2, 128).transpose(0, 2, 1)),
               s5_w_glu=np.ascontiguousarray(p["s5_w_glu"]),
               jjrow=np.ascontiguousarray(np.broadcast_to(np.arange(1, 513, dtype=np.float32), (128, 512))))
    wa2 = np.zeros((nl, 48, 256), np.float32)
    ba = np.zeros((nl, 128, 4), np.float32)
    for d in range(2):
        for h in range(4):
            wa2[:, 32 * d:32 * d + 16, 64 * h:64 * h + 48] = p["gla_w_a2"][:, d, :, 48 * h:48 * (h + 1)]
            ba[:, 64 * (h % 2):64 * (h % 2) + 48, 2 * d + h // 2] = p["gla_b_a"][:, d, 48 * h:48 * (h + 1)]
    gb = np.zeros((nl, 128, 1), np.float32)
    gb[:, 64:72, 0] = p["ml_i_bias"].reshape(nl, 8)
    gb[:, 72:80, 0] = p["ml_f_bias"].reshape(nl, 8)
    gsel = np.zeros((128, 3, 2, 4, 96), np.float32)
    for d in range(2):
        for h in range(4):
            gsel[64 + 4 * d + h, 0, d, h, :] = 1.0
            gsel[72 + 4 * d + h, 1, d, h, :] = 1.0
            gsel[72 + 4 * d + h, 2, d, h, :] = -1.0
    s_, t_ = np.arange(128)[:, None], np.arange(128)[None, :]
    masks = np.stack([(s_ <= t_), (s_ >= t_)]).astype(np.float32)
    out.update(gla_wa2=wa2, gla_ba=ba, ml_gb=gb, gsel=gsel.reshape(128, -1), masks=masks,
               gmix=np.concatenate([p["gla_g"], p["ml_g"]], axis=1))
    return {k: np.ascontiguousarray(v, dtype=np.float32) for k, v in out.items()}


DBG = {"on": False, "n": 0}


def _dbg(K, name, buf, ap, shape, dt=F32):
    if not DBG["on"]:
        return
    t = K.dram("dbg_%s_%d" % (name, DBG["n"]), list(shape), dt, "ExternalOutput")
    DBG["n"] += 1
    K.dma(t, ap, reads=[buf])


def run_pipelined(items, fn, width):
    active = []
    items = list(items)
    pos = 0
    while pos < len(items) or active:
        while pos < len(items) and len(active) < width:
            if items[pos] is None:
                if active:
                    break
                pos += 1
                continue
            active.append(fn(items[pos]))
            pos += 1
        for g in list(active):
            try:
                next(g)
            except StopIteration:
                active.remove(g)


def _range_reduce_sin(K, u, kint, tmp, shift, deps):
    if shift != 0.0:
        K.op(K.dve, lambda r: r.tensor_scalar(out=tmp, in0=u, scalar1=shift, scalar2=None, op0=ALU.add), deps, deps)
        src = tmp
    else:
        src = u
    K.op(K.dve, lambda r: r.tensor_scalar(out=kint, in0=src, scalar1=1.0 / TWO_PI, scalar2=None, op0=ALU.mult),
         deps, deps)
    K.op(K.dve, lambda r: r.scalar_tensor_tensor(out=tmp, in0=kint, scalar=-TWO_PI, in1=src, op0=ALU.mult,
                                                 op1=ALU.add), deps, deps)
    K.op(K.dve, lambda r: r.tensor_scalar(out=tmp, in0=tmp, scalar1=-3.1415925, scalar2=3.1415925, op0=ALU.max,
                                          op1=ALU.min), deps, deps)


def phaseB1(K, G, l, segs, last):
    ph = Phase(K)
    I32 = mybir.dt.int32
    jj = ph.sb("s_jj", [128, 512])
    d2 = ph.sb("s_d2", [128, 2])
    bg = ph.sb("s_bg", [128, 2])
    wg = ph.sb("s_wg", [128, 2, 256], BF16)
    stage = [ph.sb("s_st%d" % i, [128, 1024]) for i in range(2)]
    K.dma(jj[:, :], G["jjrow"][:, :], writes=[jj])
    K.dma(d2[:, :], G["s5_d2"][l], writes=[d2])
    K.dma(bg[:, :], G["s5_bg2"][l], writes=[bg])
    load_weight_bf16(K, ph, wg, G["s5_w_glu"][l], 256, stage)
    BTr = ph.sb("s_BTr", [128, 8, 128], BF16)
    BTi = ph.sb("s_BTi", [128, 8, 128], BF16)
    CTr = ph.sb("s_CTr", [128, 8, 128], BF16)
    CTrn = ph.sb("s_CTrn", [128, 8, 128], BF16)
    CTin = ph.sb("s_CTin", [128, 8, 128], BF16)
    par = ph.sb("s_par", [128, 24])
    Tre = ph.sb("s_Tre", [128, 8, 512])
    Tim = ph.sb("s_Tim", [128, 8, 512])
    Cc = ph.sb("s_Cc", [128, 8, 512])
    Ss = ph.sb("s_Ss", [128, 8, 512])
    sm = {nm: ph.sb("s_" + nm, [128, 8]) for nm in
          ("dt", "r", "th", "sin", "cos", "lre", "lim", "den", "t1", "t2", "zre", "zim", "cre", "cim")}
    smi = ph.sb("s_smi", [128, 8], I32)
    kint = ph.sb("s_kint", [128, 512], I32)
    ub_ = [ph.sb("s_ub%d" % i, [128, 2, 512], BF16) for i in range(2)]
    u32_ = [ph.sb("s_u32%d" % i, [128, 2, 512]) for i in range(2)]
    wk = {nm: [ph.sb("s_%s%d" % (nm, i), [128, 512]) for i in range(2)] for nm in
          ("a1", "a2", "a3", "a4", "vre", "vim", "wre", "wim")}
    xb = [[ph.sb("s_xb%d%d" % (q, i), [128, 512], BF16) for i in range(2)] for q in range(4)]
    tcs = [[ph.sb("s_tc%d%d" % (i, j), [128, 1]) for j in range(2)] for i in range(2)]
    cre_d = [Dep() for _ in range(8)]
    cim_d = [Dep() for _ in range(8)]
    ysb = [ph.sb("s_ysb%d" % i, [128, 512]) for i in range(2)]
    yfw = [ph.sb("s_yfw%d" % i, [128, 512]) for i in range(2)]
    yg = [ph.sb("s_yg%d" % i, [128, 512]) for i in range(2)]
    ygb = [ph.sb("s_ygb%d" % i, [128, 512], BF16) for i in range(2)]
    sig = [ph.sb("s_sig%d" % i, [128, 512]) for i in range(2)]
    mxo = [ph.sb("s_mxo%d" % i, [128, 512], BF16) for i in range(2)]
    pP = [[ph.ps("s_pP%d%d" % (x, i)) for i in range(2)] for x in range(2)]
    pY = [ph.ps("s_pY%d" % i) for i in range(2)]
    pG = [ph.ps("s_pG%d" % i) for i in range(2)]
    ydep = {}
    it = 0
    for d in range(2):
        K.dma(par[:, :], G["s5_par"][l, d], writes=[par])
        for x, (dstb, scale) in enumerate(((BTr, None), (BTi, None))):
            st = stage[x]
            K.dma(st[:, 0:1024], G["s5_BT"][l, d, x], writes=[st])
            _cvt(K, x, dstb[:, :, :].rearrange("p b m -> p (b m)"), st[:, 0:1024], [st], [dstb])
        st = stage[0]
        K.dma(st[:, 0:1024], G["s5_CT"][l, d, 0], writes=[st])
        _cvt(K, 1, CTr[:, :, :].rearrange("p b m -> p (b m)"), st[:, 0:1024], [st], [CTr])
        _cvt(K, 1, CTrn[:, :, :].rearrange("p b m -> p (b m)"), st[:, 0:1024], [st], [CTrn], scale=-1.0)
        st = stage[1]
        K.dma(st[:, 0:1024], G["s5_CT"][l, d, 1], writes=[st])
        _cvt(K, 1, CTin[:, :, :].rearrange("p b m -> p (b m)"), st[:, 0:1024], [st], [CTin], scale=-1.0)
        are, aim, ldt = par[:, 0:8], par[:, 8:16], par[:, 16:24]
        S = {k: v[:, :] for k, v in sm.items()}
        allsm = list(sm.values()) + [smi, par]

        def dv(fn):
            K.op(K.dve, fn, allsm, allsm)

        K.op(K.act, lambda r: r.activation(out=S["dt"], in_=ldt, func=AF.Exp), allsm, allsm)
        dv(lambda r: r.tensor_tensor(out=S["t1"], in0=are, in1=S["dt"], op=ALU.mult))
        K.op(K.act, lambda r: r.activation(out=S["r"], in_=S["t1"], func=AF.Exp), allsm, allsm)
        dv(lambda r: r.tensor_tensor(out=S["th"], in0=aim, in1=S["dt"], op=ALU.mult))
        for nm, shift in (("sin", 0.0), ("cos", 0.5 * np.pi)):
            _range_reduce_sin(K, S["th"], smi[:, :], S["t2"], shift, allsm)
            K.op(K.act, lambda r: r.activation(out=S[nm], in_=S["t2"], func=AF.Sin), allsm, allsm)
        dv(lambda r: r.tensor_tensor(out=S["lre"], in0=S["r"], in1=S["cos"], op=ALU.mult))
        dv(lambda r: r.tensor_tensor(out=S["lim"], in0=S["r"], in1=S["sin"], op=ALU.mult))
        dv(lambda r: r.tensor_tensor(out=S["den"], in0=are, in1=are, op=ALU.mult))
        dv(lambda r: r.tensor_tensor(out=S["t1"], in0=aim, in1=aim, op=ALU.mult))
        dv(lambda r: r.tensor_tensor(out=S["den"], in0=S["den"], in1=S["t1"], op=ALU.add))
        dv(lambda r: r.reciprocal(out=S["den"], in_=S["den"]))
        dv(lambda r: r.tensor_scalar(out=S["lre"], in0=S["lre"], scalar1=-1.0, scalar2=None, op0=ALU.add))
        dv(lambda r: r.tensor_tensor(out=S["t1"], in0=S["lre"], in1=are, op=ALU.mult))
        dv(lambda r: r.tensor_tensor(out=S["t2"], in0=S["lim"], in1=aim, op=ALU.mult))
        dv(lambda r: r.tensor_tensor(out=S["t1"], in0=S["t1"], in1=S["t2"], op=ALU.add))
        dv(lambda r: r.tensor_tensor(out=S["zre"], in0=S["t1"], in1=S["den"], op=ALU.mult))
        dv(lambda r: r.tensor_tensor(out=S["t1"], in0=S["lim"], in1=are, op=ALU.mult))
        dv(lambda r: r.tensor_tensor(out=S["t2"], in0=S["lre"], in1=aim, op=ALU.mult))
        dv(lambda r: r.tensor_tensor(out=S["t1"], in0=S["t1"], in1=S["t2"], op=ALU.subtract))
        dv(lambda r: r.tensor_tensor(out=S["zim"], in0=S["t1"], in1=S["den"], op=ALU.mult))
        K.op(K.dve, lambda r: r.memset(S["cre"], 0.0), allsm + cre_d, allsm + cre_d)
        K.op(K.dve, lambda r: r.memset(S["cim"], 0.0), allsm + cim_d, allsm + cim_d)
        tabs = [Tre, Tim, Cc, Ss, kint] + allsm
        a1, a2 = wk["a1"][0], wk["a2"][0]
        for b in range(8):
            K.op(K.dve, lambda r: r.tensor_scalar(out=a1[:, :], in0=jj[:, :], scalar1=S["th"][:, b:b + 1],
                                                  scalar2=None, op0=ALU.mult), [jj] + tabs, [a1] + tabs)
            for dst, shift in ((Ss, 0.0), (Cc, 0.5 * np.pi)):
                _range_reduce_sin(K, a1[:, :], kint[:, :], a2[:, :], shift, [a1, a2] + tabs)
                K.op(K.act, lambda r: r.activation(out=dst[:, b, :], in_=a2[:, :], func=AF.Sin),
                     [a1, a2] + tabs, [a1, a2] + tabs)
            K.op(K.dve, lambda r: r.tensor_scalar(out=a1[:, :], in0=Cc[:, b, :], scalar1=S["zre"][:, b:b + 1],
                                                  scalar2=None, op0=ALU.mult), [a1] + tabs, [a1] + tabs)
            K.op(K.dve, lambda r: r.scalar_tensor_tensor(out=Tre[:, b, :], in0=Ss[:, b, :],
                                                         scalar=S["zim"][:, b:b + 1], in1=a1[:, :], op0=ALU.mult,
                                                         op1=ALU.add), [a1] + tabs, [a1] + tabs)
            K.op(K.dve, lambda r: r.tensor_scalar(out=a1[:, :], in0=Ss[:, b, :], scalar1=S["zre"][:, b:b + 1],
                                                  scalar2=None, op0=ALU.mult), [a1] + tabs, [a1] + tabs)
            K.op(K.dve, lambda r: r.scalar_tensor_tensor(out=Tim[:, b, :], in0=Cc[:, b, :],
                                                         scalar=S["zim"][:, b:b + 1], in1=a1[:, :], op0=ALU.mult,
                                                         op1=ALU.subtract), [a1] + tabs, [a1] + tabs)
        order = segs if d == 0 else [segs[0]] + segs[1:][::-1]
        rev = d == 1
        for (t0, n, is_ctx) in order:
            sl = it % 2
            it += 1
            u32, ub = u32_[sl], ub_[sl]
            K.dma(u32[:, :, 0:n], G["zF"][0:256, t0:t0 + n].rearrange("(f p) t -> p f t", p=128), writes=[u32])
            K.op(K.act, lambda r: r.copy(out=ub[:, :, 0:n], in_=u32[:, :, 0:n]), [u32], [ub])

            def tv(T, b):
                v = T[:, b, 0:n]
                return v[:, ::-1] if rev else v

            cl = 0 if rev else n - 1
            def blk_gen(b):
                f, b4 = b // 4, b % 4
                py = pY[f]
                q = b % 2
                pre, pim = pP[0][q], pP[1][q]
                K.mm(pre[:, 0:n], BTr[:, b, :], ub[:, f, 0:n], True, True, [BTr, ub], [pre])
                K.mm(pim[:, 0:n], BTi[:, b, :], ub[:, f, 0:n], True, True, [BTi, ub], [pim])
                yield
                A = {k: v[q] for k, v in wk.items()}
                K.op(K.dve, lambda r: r.tensor_tensor(out=A["a1"][:, 0:n], in0=pre[:, 0:n], in1=tv(Tre, b),
                                                      op=ALU.mult), [pre, Tre], [A["a1"]])
                K.op(K.dve, lambda r: r.tensor_tensor(out=A["a2"][:, 0:n], in0=pim[:, 0:n], in1=tv(Tim, b),
                                                      op=ALU.mult), [pim, Tim], [A["a2"]])
                K.op(K.dve, lambda r: r.tensor_tensor(out=A["a3"][:, 0:n], in0=pim[:, 0:n], in1=tv(Tre, b),
                                                      op=ALU.mult), [pim, Tre], [A["a3"]])
                K.op(K.dve, lambda r: r.tensor_tensor(out=A["a4"][:, 0:n], in0=pre[:, 0:n], in1=tv(Tim, b),
                                                      op=ALU.mult), [pre, Tim], [A["a4"]])
                K.op(K.pool, lambda r: r.tensor_tensor(out=A["vre"][:, 0:n], in0=A["a1"][:, 0:n],
                                                       in1=A["a2"][:, 0:n], op=ALU.subtract),
                     [A["a1"], A["a2"]], [A["vre"]])
                K.op(K.pool, lambda r: r.tensor_tensor(out=A["vim"][:, 0:n], in0=A["a3"][:, 0:n],
                                                       in1=A["a4"][:, 0:n], op=ALU.add),
                     [A["a3"], A["a4"]], [A["vim"]])
                yield
                for wn, vn, cn, cd in (("wre", "vre", "cre", cre_d), ("wim", "vim", "cim", cim_d)):
                    o, i1 = A[wn][:, 0:n], A[vn][:, 0:n]
                    if rev:
                        o, i1 = o[:, ::-1], i1[:, ::-1]
                    K.op(K.dve, lambda r: r.tensor_tensor_scan(
                        out=o, data0=S["r"][:, b:b + 1].to_broadcast([128, n]), data1=i1,
                        initial=S[cn][:, b:b + 1], op0=ALU.mult, op1=ALU.add),
                        [A[vn], sm["r"], cd[b]], [A[wn]])
                wr, wi = A["wre"][:, cl:cl + 1], A["wim"][:, cl:cl + 1]
                cc_, ss_ = Cc[:, b, n - 1:n], Ss[:, b, n - 1:n]
                t1_, t2_ = tcs[q][0], tcs[q][1]
                K.op(K.dve, lambda r: r.tensor_tensor(out=t1_[:, :], in0=wi, in1=ss_, op=ALU.mult),
                     [A["wim"], Ss], [t1_])
                K.op(K.dve, lambda r: r.scalar_tensor_tensor(out=S["cre"][:, b:b + 1], in0=wr, scalar=cc_,
                                                             in1=t1_[:, :], op0=ALU.mult, op1=ALU.subtract),
                     [A["wre"], Cc, t1_], [cre_d[b]])
                K.op(K.dve, lambda r: r.tensor_tensor(out=t2_[:, :], in0=wi, in1=cc_, op=ALU.mult),
                     [A["wim"], Cc], [t2_])
                K.op(K.dve, lambda r: r.scalar_tensor_tensor(out=S["cim"][:, b:b + 1], in0=wr, scalar=ss_,
                                                             in1=t2_[:, :], op0=ALU.mult, op1=ALU.add),
                     [A["wre"], Ss, t2_], [cim_d[b]])
                prods = ((Cc, "wre", CTr), (Ss, "wim", CTrn), (Ss, "wre", CTin), (Cc, "wim", CTin))
                xs = []
                for qi, (T, wn, CT) in enumerate(prods):
                    xo = xb[qi][q]
                    K.op(K.dve, lambda r: r.tensor_tensor(out=xo[:, 0:n], in0=A[wn][:, 0:n], in1=tv(T, b),
                                                          op=ALU.mult), [A[wn], T], [xo])
                    xs.append((xo, CT))
                yield
                for qi, (xo, CT) in enumerate(xs):
                    K.mm(py[:, 0:n], CT[:, b, :], xo[:, 0:n], b4 == 0 and qi == 0, b4 == 3 and qi == 3,
                         [CT, xo], [py])

            run_pipelined(range(8), blk_gen, 2)
            for f in range(2):
                py = pY[f]
                key = (f, t0)
                if d == 0:
                    ys = ysb[f]
                    K.op(K.act, lambda r: r.copy(out=ys[:, 0:n], in_=py[:, 0:n]), [py], [ys])
                    ydep[key] = Dep()
                    K.dma(G["ysc"][f * 128:(f + 1) * 128, t0:t0 + n], ys[:, 0:n], reads=[ys], writes=[ydep[key]])
                elif not (last and is_ctx):
                    K.dma(yfw[f][:, 0:n], G["ysc"][f * 128:(f + 1) * 128, t0:t0 + n], reads=[ydep[key]],
                          writes=[yfw[f]])
                    K.op(K.dve, lambda r: r.tensor_tensor(out=ysb[f][:, 0:n], in0=py[:, 0:n], in1=yfw[f][:, 0:n],
                                                          op=ALU.add), [py, yfw[f]], [ysb[f]])
                    K.op(K.dve, lambda r: r.scalar_tensor_tensor(out=ysb[f][:, 0:n], in0=u32[:, f, 0:n],
                                                                 scalar=d2[:, f:f + 1], in1=ysb[f][:, 0:n],
                                                                 op0=ALU.mult, op1=ALU.add),
                         [u32, d2, ysb[f]], [ysb[f]])
                    K.op(K.act, lambda r: r.activation(out=yg[f][:, 0:n], in_=ysb[f][:, 0:n],
                                                       func=AF.Gelu_apprx_tanh), [ysb[f]], [yg[f]])
                    K.op(K.act, lambda r: r.copy(out=ygb[f][:, 0:n], in_=yg[f][:, 0:n]), [yg[f]], [ygb[f]])
            if d == 1 and not (last and is_ctx):
                for fo in range(2):
                    pg = pG[fo]
                    for fi in range(2):
                        K.mm(pg[:, 0:n], wg[:, fi, fo * 128:(fo + 1) * 128], ygb[fi][:, 0:n], fi == 0, fi == 1,
                             [wg, ygb[fi]], [pg])
                    K.op(K.act, lambda r: r.activation(out=sig[fo][:, 0:n], in_=pg[:, 0:n], func=AF.Sigmoid,
                                                       bias=bg[:, fo:fo + 1]), [pg, bg], [sig[fo]])
                    K.op(K.pool, lambda r: r.tensor_tensor(out=mxo[fo][:, 0:n], in0=yg[fo][:, 0:n],
                                                           in1=sig[fo][:, 0:n], op=ALU.mult),
                         [yg[fo], sig[fo]], [mxo[fo]])
                    K.dma(G["mxT"][fo * 128:(fo + 1) * 128, t0:t0 + n], mxo[fo][:, 0:n], reads=[mxo[fo]])
    ph.close()


def phaseB2(K, G, l, segs, last):
    ph = Phase(K)
    L = G["L"]
    mask = ph.sb("g_mask", [128, 2, 128])
    K.dma(mask[:, :, :], G["masks"].rearrange("d s t -> s d t"), writes=[mask])
    rmask = ph.sb("g_rmask", [128, 512])
    K.op(K.dve, lambda r: r.memset(rmask[:, :], 1.0), [], [rmask])
    K.op(K.dve, lambda r: r.memset(rmask[:, :].rearrange("p (c j) -> p c j", j=128)[:, :, 0:1], 0.0), [rmask], [rmask])
    sel = ph.sb("g_sel", [128, 3 * 8 * 96])
    K.dma(sel[:, :], G["gsel"][:, :], writes=[sel])
    wa2f = ph.sb("g_wa2f", [48, 256])
    wa2 = ph.sb("g_wa2", [48, 256], BF16)
    K.dma(wa2f[:, :], G["gla_wa2"][l], writes=[wa2f])
    K.op(K.dve, lambda r: r.tensor_copy(out=wa2[:, :], in_=wa2f[:, :]), [wa2f], [wa2])
    nba = ph.sb("g_nba", [128, 4])
    K.dma(nba[:, :], G["gla_ba"][l], writes=[nba])
    K.op(K.dve, lambda r: r.tensor_scalar(out=nba[:, :], in0=nba[:, :], scalar1=-1.0, scalar2=None, op0=ALU.mult),
         [nba], [nba])
    gbias = ph.sb("g_gbias", [128, 1])
    K.dma(gbias[:, :], G["ml_gb"][l], writes=[gbias])
    grow = ph.sb("g_grow", [128, 768])
    K.dma(grow[:, :], G["gmix"][l:l + 1, :].to_broadcast([128, 768]), writes=[grow])
    identb = ph.sb("g_identb", [128, 128], BF16)
    K.op(K.dve, lambda r: r.tensor_copy(out=identb[:, :], in_=G["identf"][:, :]), [G["identf"]], [identb])
    gqk = ph.sb("g_gqk", [128, 4, 512])
    lrg = ph.sb("g_lrg", [128, 512])
    lrb = ph.sb("g_lrb", [128, 512], BF16)
    spb = ph.sb("g_sp", [128, 2, 512])
    Bp = ph.sb("g_Bp", [128, 2, 512])
    eb = ph.sb("g_eb", [128, 2, 512])
    enb = ph.sb("g_enb", [128, 2, 512])
    qd = ph.sb("g_qd", [128, 2, 512], BF16)
    kd = ph.sb("g_kd", [128, 2, 512], BF16)
    mqk = ph.sb("g_mqk", [128, 8, 512])
    mqd = ph.sb("g_mqd", [128, 4, 512], BF16)
    mkd = ph.sb("g_mkd", [128, 4, 512], BF16)
    Aq = ph.sb("g_Aq", [128, 4, 512])
    Bk = [ph.sb("g_Bk%d" % i, [128, 512]) for i in range(2)]
    gt1 = ph.sb("g_gt1", [128, 512])
    gsp = ph.sb("g_gsp", [128, 512])
    gF = ph.sb("g_gF", [128, 512])
    gtmp = ph.sb("g_gtmp", [128, 512])
    vst = [ph.sb("g_vst%d" % i, [128, NTM]) for i in range(2)]
    Vg = [ph.sb("g_Vg%d" % i, [128, 4, 96], BF16) for i in range(2)]
    Vm = [ph.sb("g_Vm%d" % i, [128, 4, 97], BF16) for i in range(2)]
    for i in range(2):
        K.op(K.dve, lambda r: r.memset(Vm[i][:, :, 96:97], 1.0), [], [Vm[i]])
    attm = [ph.sb("g_attm%d" % i, [128, 128], BF16) for i in range(8)]
    kdT = [ph.sb("g_kdT%d" % i, [128, 128], BF16) for i in range(8)]
    psA_d = [Dep() for _ in range(8)]
    psK_d = [Dep() for _ in range(8)]
    psO_d = [Dep() for _ in range(8)]
    psS_d = [Dep() for _ in range(4)]
    ob_d = [[Dep() for _ in range(8)] for _ in range(2)]
    Sst = ph.sb("g_S", [128, 8, 97])
    Sbf = ph.sb("g_Sbf", [128, 8, 97], BF16)
    Sd = [Dep() for _ in range(8)]
    Sbd = [Dep() for _ in range(8)]
    tmpS = [ph.sb("g_tmpS%d" % i, [128, 97]) for i in range(4)]
    osb = [ph.sb("g_osb%d" % i, [128, 776]) for i in range(2)]
    hsb = [ph.sb("g_hsb%d" % i, [128, 768]) for i in range(2)]
    ofw = [ph.sb("g_ofw%d" % i, [128, 768]) for i in range(2)]
    den = [ph.sb("g_den%d" % i, [128, 4]) for i in range(2)]
    sq = [ph.sb("g_sq%d" % i, [128, 768]) for i in range(2)]
    st1 = [ph.sb("g_st1%d" % i, [128, 8]) for i in range(2)]
    st2 = [ph.sb("g_st2%d" % i, [128, 8]) for i in range(2)]
    st3 = [ph.sb("g_st3%d" % i, [128, 8]) for i in range(2)]
    gact = [ph.sb("g_gact%d" % i, [128, 768]) for i in range(2)]
    mxb = [ph.sb("g_mxb%d" % i, [128, 768], BF16) for i in range(2)]
    mxt = [ph.sb("g_mxt%d" % i, [128, 6, 128], BF16) for i in range(2)]
    psZ = ph.ps("g_psZ")
    psA = [ph.ps("g_psA%d" % i) for i in range(2)]
    psK = ph.ps("g_psK", [128, 1024], BF16)
    psTp = ph.ps("g_psTp", [128, 1024], BF16)
    psO = ph.ps("g_psO", [128, 1024])
    psS = ph.ps("g_psS")
    odep = {}
    ntile = 0
    nhd = 0
    for d in range(2):
        rev = d == 1
        K.op(K.dve, lambda r: r.memset(Sst[:, :, :], 0.0), Sd, Sd)
        K.op(K.pool, lambda r: r.memset(Sbf[:, :, :], 0.0), Sbd, Sbd)
        order = segs if d == 0 else [segs[0]] + segs[1:][::-1]
        for (t0, n, is_ctx) in order:
            nt = n // 128
            K.dma(gqk[:, :, 0:n], G["zF"][256:768, t0:t0 + n].rearrange("(f p) t -> p f t", p=128), writes=[gqk])
            K.dma(lrg[0:80, 0:n], G["zF"][768:848, t0:t0 + n], writes=[lrg])
            K.dma(mqk[0:96, :, 0:n], G["zF"][896:1920, t0:t0 + n].rearrange("(f p) t -> p f t", p=128)[0:96],
                  writes=[mqk])
            K.op(K.act, lambda r: r.copy(out=lrb[0:48, 0:n], in_=lrg[0:48, 0:n]), [lrg], [lrb])
            r0 = 32 * d
            c3 = lambda ap: ap.rearrange("p (c j) -> p c j", j=128)
            for hp in range(2):
                K.mm(psZ[:, 0:n], wa2[r0:r0 + 16, hp * 128:(hp + 1) * 128], lrb[r0:r0 + 16, 0:n], True, True,
                     [wa2, lrb], [psZ])
                K.op(K.act, lambda r: r.activation(out=spb[:, hp, 0:n], in_=psZ[:, 0:n], func=AF.Exp, scale=-1.0,
                                                   bias=nba[:, 2 * d + hp:2 * d + hp + 1]), [psZ, nba], [spb])
                K.op(K.act, lambda r: r.activation(out=spb[:, hp, 0:n], in_=spb[:, hp, 0:n], func=AF.Ln, bias=1.0),
                     [spb], [spb])
                K.op(K.dve, lambda r: r.tensor_tensor_scan(out=Bp[:, hp, 0:n], data0=rmask[:, 0:n],
                                                           data1=spb[:, hp, 0:n], initial=0.0, op0=ALU.mult,
                                                           op1=ALU.add), [spb, rmask], [Bp])
                if rev:
                    K.op(K.dve, lambda r: r.tensor_tensor(out=c3(spb[:, hp, 0:n]), in0=c3(spb[:, hp, 0:n]),
                                                          in1=c3(Bp[:, hp, 0:n]), op=ALU.subtract), [spb, Bp], [spb])
                    K.op(K.dve, lambda r: r.tensor_tensor(
                        out=c3(Bp[:, hp, 0:n]), in0=c3(spb[:, hp, 0:n]),
                        in1=c3(Bp[:, hp, 0:n])[:, :, 127:128].to_broadcast([128, nt, 128]), op=ALU.add),
                        [spb, Bp], [Bp])
                K.op(K.act, lambda r: r.activation(out=eb[:, hp, 0:n], in_=Bp[:, hp, 0:n], func=AF.Exp,
                                                   scale=-1.0 / 16.0), [Bp], [eb])
                K.op(K.act, lambda r: r.activation(out=enb[:, hp, 0:n], in_=Bp[:, hp, 0:n], func=AF.Exp,
                                                   scale=1.0 / 16.0), [Bp], [enb])
                K.op(K.pool, lambda r: r.tensor_tensor(out=qd[:, hp, 0:n], in0=gqk[:, hp, 0:n], in1=eb[:, hp, 0:n],
                                                       op=ALU.mult), [gqk, eb], [qd])
                K.op(K.pool, lambda r: r.tensor_tensor(out=kd[:, hp, 0:n], in0=gqk[:, 2 + hp, 0:n],
                                                       in1=enb[:, hp, 0:n], op=ALU.mult), [gqk, enb], [kd])
            G_ = slice(64, 80)
            K.op(K.dve, lambda r: r.tensor_scalar(out=gt1[G_, 0:n], in0=lrg[G_, 0:n], scalar1=gbias[G_, 0:1],
                                                  scalar2=None, op0=ALU.add), [lrg, gbias], [gt1])
            K.op(K.act, lambda r: r.activation(out=gsp[G_, 0:n], in_=gt1[G_, 0:n], func=AF.Exp, scale=-1.0),
                 [gt1], [gsp])
            K.op(K.act, lambda r: r.activation(out=gsp[G_, 0:n], in_=gsp[G_, 0:n], func=AF.Ln, bias=1.0),
                 [gsp], [gsp])
            K.op(K.dve, lambda r: r.tensor_tensor_scan(out=gF[G_, 0:n], data0=rmask[G_, 0:n], data1=gsp[G_, 0:n],
                                                       initial=0.0, op0=ALU.mult, op1=ALU.add), [gsp, rmask], [gF])
            if rev:
                K.op(K.dve, lambda r: r.tensor_tensor(out=c3(gtmp[G_, 0:n]), in0=c3(gsp[G_, 0:n]),
                                                      in1=c3(gF[G_, 0:n]), op=ALU.subtract), [gsp, gF], [gtmp])
                K.op(K.dve, lambda r: r.tensor_tensor(
                    out=c3(gF[G_, 0:n]), in0=c3(gtmp[G_, 0:n]),
                    in1=c3(gF[G_, 0:n])[:, :, 127:128].to_broadcast([16, nt, 128]), op=ALU.add), [gtmp, gF], [gF])
            for h in range(4):
                sI = sel[G_, ((0 * 2 + d) * 4 + h) * 96:((0 * 2 + d) * 4 + h + 1) * 96]
                sFp = sel[G_, ((1 * 2 + d) * 4 + h) * 96:((1 * 2 + d) * 4 + h + 1) * 96]
                sFn = sel[G_, ((2 * 2 + d) * 4 + h) * 96:((2 * 2 + d) * 4 + h + 1) * 96]
                K.mm(psZ[0:96, 0:n], sFn, gF[G_, 0:n], True, True, [sel, gF], [psZ])
                K.op(K.act, lambda r: r.activation(out=Aq[0:96, h, 0:n], in_=psZ[0:96, 0:n], func=AF.Exp),
                     [psZ], [Aq])
                K.mm(psZ[0:96, 0:n], sI, gt1[G_, 0:n], True, False, [sel, gt1], [psZ])
                K.mm(psZ[0:96, 0:n], sFp, gF[G_, 0:n], False, True, [sel, gF], [psZ])
                bk = Bk[h % 2]
                K.op(K.act, lambda r: r.activation(out=bk[0:96, 0:n], in_=psZ[0:96, 0:n], func=AF.Exp), [psZ], [bk])
                K.op(K.pool, lambda r: r.tensor_tensor(out=mqd[0:96, h, 0:n], in0=mqk[0:96, h, 0:n],
                                                       in1=Aq[0:96, h, 0:n], op=ALU.mult), [mqk, Aq], [mqd])
                K.op(K.pool, lambda r: r.tensor_tensor(out=mkd[0:96, h, 0:n], in0=mqk[0:96, 4 + h, 0:n],
                                                       in1=bk[0:96, 0:n], op=ALU.mult), [mqk, bk], [mkd])
            tiles = list(range(nt))[::-1] if rev else list(range(nt))
            items = []
            for ti in tiles:
                for hd in range(8):
                    items.append(("h", ntile, ti, hd))
                items.append(None)
                items.append(("f", ntile, ti, 0))
                ntile += 1

            def head_gen(item):
                _, tl, ti, hd = item
                sl = tl % 2
                rr = t0 + ti * 128
                cs = slice(ti * 128, (ti + 1) * 128)
                lastc = ti * 128 if rev else ti * 128 + 127
                if hd == 0:
                    K.dma(vst[sl][:, :], G["zT"][rr:rr + 128, :], writes=[vst[sl]])
                    K.op(K.act, lambda r: r.copy(out=Vg[sl][:, :, :],
                                                 in_=vst[sl][:, 0:384].rearrange("p (h v) -> p h v", v=96)),
                         [vst[sl]], [Vg[sl]])
                    K.op(K.act, lambda r: r.copy(out=Vm[sl][:, :, 0:96],
                                                 in_=vst[sl][:, 768:1152].rearrange("p (h v) -> p h v", v=96)),
                         [vst[sl]], [Vm[sl]])
                if hd < 4:
                    DK, DV = 48, 96
                    rows = slice(64 * (hd % 2), 64 * (hd % 2) + 48)
                    qv, kv = qd[rows, hd // 2, cs], kd[rows, hd // 2, cs]
                    ebl = eb[rows, hd // 2, lastc:lastc + 1]
                    V = Vg[sl][:, hd, :]
                    qb, kb, ebb, Vb = qd, kd, eb, Vg[sl]
                else:
                    DK, DV = 96, 97
                    rows = slice(0, 96)
                    qv, kv = mqd[rows, hd - 4, cs], mkd[rows, hd - 4, cs]
                    ebl = Aq[rows, hd - 4, lastc:lastc + 1]
                    V = Vm[sl][:, hd - 4, :]
                    qb, kb, ebb, Vb = mqd, mkd, Aq, Vm[sl]
                a = hd % 4
                a8 = (tl % 2) * 4 + a if False else hd
                pav = psA[hd // 4][:, a * 128:(a + 1) * 128]
                K.mm(pav, kv, qv, True, True, [kb, qb], [psA_d[hd]])
                if hd < 4:
                    pkv = psK[:, hd * 128:(hd + 1) * 128]
                    K.tr(pkv, kd[:, hd // 2, cs], identb[:, :], [kd, identb], [psK_d[hd]])
                    MM = 128
                else:
                    pkv = psK[:, hd * 128: hd * 128 + DK]
                    K.tr(pkv, kv, identb[rows, rows], [kb, identb], [psK_d[hd]])
                    MM = DK
                yield
                am, kt = attm[hd], kdT[hd]
                K.op(K.dve, lambda r: r.tensor_tensor(out=am[:, :], in0=pav, in1=mask[:, d, :], op=ALU.mult),
                     [psA_d[hd], mask], [am])
                K.op(K.act, lambda r: r.copy(out=kt[:, 0:MM], in_=pkv), [psK_d[hd]], [kt])
                yield
                po = psO[:, hd * 128:hd * 128 + DV]
                K.mm(po, am[:, :], V, True, False, [am, Vb], [psO_d[hd]])
                K.mm(po, qv, Sbf[rows, hd, 0:DV], False, True, [qb, Sbd[hd]], [psO_d[hd]])
                psv = psS[rows, a * 128:a * 128 + DV]
                K.mm(psS[0:MM, a * 128:a * 128 + DV], kt[:, 0:MM], V, True, True, [kt, Vb], [psS_d[a]])
                yield
                ts_ = tmpS[a]
                K.op(K.dve, lambda r: r.tensor_tensor(out=ts_[rows, 0:DV], in0=psv, in1=Sst[rows, hd, 0:DV],
                                                      op=ALU.add), [psS_d[a], Sd[hd]], [ts_])
                K.op(K.dve, lambda r: r.tensor_scalar(out=Sst[rows, hd, 0:DV], in0=ts_[rows, 0:DV], scalar1=ebl,
                                                      scalar2=None, op0=ALU.mult), [ts_, ebb], [Sd[hd]])
                K.op(K.act, lambda r: r.copy(out=Sbf[rows, hd, 0:DV], in_=Sst[rows, hd, 0:DV]),
                     [Sd[hd]], [Sbd[hd]])
                ob = osb[sl]
                K.op(K.act, lambda r: r.copy(out=ob[:, hd * 97:hd * 97 + DV], in_=po), [psO_d[hd]], [ob_d[sl][hd]])

            def fin_gen(item):
                _, tl, ti, _ = item
                sl = tl % 2
                rr = t0 + ti * 128
                do_post = rev and not (last and is_ctx)
                ob = osb[sl]
                obd = ob_d[sl]
                o8 = ob[:, :].rearrange("p (h v) -> p h v", v=97)
                K.op(K.act, lambda r: r.activation(out=den[sl][:, :], in_=o8[:, 4:8, 96], func=AF.Abs),
                     obd, [den[sl]])
                K.op(K.dve, lambda r: r.tensor_scalar(out=den[sl][:, :], in0=den[sl][:, :], scalar1=1.0, scalar2=None,
                                                      op0=ALU.max), [den[sl]], [den[sl]])
                K.op(K.dve, lambda r: r.reciprocal(out=den[sl][:, :], in_=den[sl][:, :]), [den[sl]], [den[sl]])
                K.op(K.dve, lambda r: r.tensor_tensor(
                    out=o8[:, 4:8, 0:96], in0=o8[:, 4:8, 0:96],
                    in1=den[sl][:, :].unsqueeze(2).to_broadcast([128, 4, 96]), op=ALU.mult), obd + [den[sl]], obd)
                if not rev:
                    if not (last and is_ctx):
                        odep[rr] = Dep()
                        K.dma(G["oF"][rr:rr + 128, :].rearrange("p (h v) -> p h v", v=96), o8[:, :, 0:96],
                              reads=obd, writes=[odep[rr]])
                    return
                if not do_post:
                    return
                K.dma(ofw[sl][:, :], G["oF"][rr:rr + 128, :], reads=[odep[rr]], writes=[ofw[sl]])
                yield
                hs = hsb[sl]
                h8 = hs[:, :].rearrange("p (h v) -> p h v", v=96)
                K.op(K.pool, lambda r: r.tensor_tensor(out=h8, in0=o8[:, :, 0:96],
                                                       in1=ofw[sl][:, :].rearrange("p (h v) -> p h v", v=96),
                                                       op=ALU.add), obd + [ofw[sl]], [hs])
                yield
                K.op(K.dve, lambda r: r.tensor_reduce(out=st1[sl][:, :], in_=h8, axis=AX.X, op=ALU.add), [hs], [st1[sl]])
                K.op(K.pool, lambda r: r.tensor_tensor(out=sq[sl][:, :], in0=hs[:, :], in1=hs[:, :], op=ALU.mult),
                     [hs], [sq[sl]])
                K.op(K.act, lambda r: r.activation(out=gact[sl][:, 0:384], in_=vst[sl][:, 384:768], func=AF.Silu),
                     [vst[sl]], [gact[sl]])
                K.op(K.act, lambda r: r.activation(out=gact[sl][:, 384:768], in_=vst[sl][:, 1152:1536],
                                                   func=AF.Sigmoid), [vst[sl]], [gact[sl]])
                yield
                K.op(K.dve, lambda r: r.tensor_reduce(out=st2[sl][:, :],
                                                      in_=sq[sl][:, :].rearrange("p (h v) -> p h v", v=96),
                                                      axis=AX.X, op=ALU.add), [sq[sl]], [st2[sl]])
                K.op(K.dve, lambda r: r.tensor_scalar(out=st1[sl][:, :], in0=st1[sl][:, :], scalar1=1.0 / 96.0,
                                                      scalar2=None, op0=ALU.mult), [st1[sl]], [st1[sl]])
                K.op(K.dve, lambda r: r.tensor_tensor(out=st3[sl][:, :], in0=st1[sl][:, :], in1=st1[sl][:, :],
                                                      op=ALU.mult), [st1[sl]], [st3[sl]])
                K.op(K.dve, lambda r: r.scalar_tensor_tensor(out=st2[sl][:, :], in0=st2[sl][:, :], scalar=1.0 / 96.0,
                                                             in1=st3[sl][:, :], op0=ALU.mult, op1=ALU.subtract),
                     [st2[sl], st3[sl]], [st2[sl]])
                K.op(K.act, lambda r: r.activation(out=st2[sl][:, :], in_=st2[sl][:, :], func=AF.Sqrt,
                                                   bias=G_EPS[0][:, 0:1]), [st2[sl], G_EPS[0]], [st2[sl]])
                yield
                K.op(K.dve, lambda r: r.reciprocal(out=st2[sl][:, :], in_=st2[sl][:, :]), [st2[sl]], [st2[sl]])
                K.op(K.dve, lambda r: r.tensor_tensor(out=h8, in0=h8,
                                                      in1=st1[sl][:, :].unsqueeze(2).to_broadcast([128, 8, 96]),
                                                      op=ALU.subtract), [hs, st1[sl]], [hs])
                K.op(K.dve, lambda r: r.tensor_tensor(out=h8, in0=h8,
                                                      in1=st2[sl][:, :].unsqueeze(2).to_broadcast([128, 8, 96]),
                                                      op=ALU.mult), [hs, st2[sl]], [hs])
                K.op(K.pool, lambda r: r.tensor_tensor(out=gact[sl][:, :], in0=gact[sl][:, :], in1=grow[:, :],
                                                       op=ALU.mult), [gact[sl], grow], [gact[sl]])
                yield
                mb = mxb[sl]
                K.op(K.dve, lambda r: r.tensor_tensor(out=mb[:, :], in0=hs[:, :], in1=gact[sl][:, :], op=ALU.mult),
                     [hs, gact[sl]], [mb])
                yield
                for c in range(6):
                    K.tr(psTp[:, c * 128:(c + 1) * 128], mb[:, c * 128:(c + 1) * 128], identb[:, :], [mb, identb],
                         [psTp])
                yield
                mt = mxt[sl]
                K.op(K.act, lambda r: r.copy(out=mt[:, :, :], in_=psTp[:, 0:768].rearrange("p (c t) -> p c t", t=128)),
                     [psTp], [mt])
                K.dma(G["mxT"][256:1024, rr:rr + 128].rearrange("(c p) t -> p c t", p=128), mt[:, :, :], reads=[mt])

            run_pipelined(items, lambda it: head_gen(it) if it[0] == "h" else fin_gen(it), 4)
    ph.close()


def phaseC1a(K, G, l, Hin, segs):
    ph = Phase(K)
    wo = ph.sb("c_wo", [128, 8, D], BF16)
    stage = [ph.sb("c_st%d" % i, [128, 2048]) for i in range(2)]
    load_weight_bf16(K, ph, wo, G["w_out"][l], D, stage)
    mxs = [ph.sb("c_mx%d" % i, [128, 8, 512], BF16) for i in range(2)]
    NS = 4
    hx = [ph.sb("c_hx%d" % i, [128, 1024]) for i in range(NS)]
    tmp = [ph.sb("c_tmp%d" % i, [128, 1024]) for i in range(NS)]
    xm2 = [ph.sb("c_xm2%d" % i, [128, 8, 128], BF16) for i in range(NS)]
    bn = [ph.sb("c_bn%d" % i, [128, 12]) for i in range(NS)]
    mv = [ph.sb("c_mv%d" % i, [128, 2]) for i in range(NS)]
    rs = [ph.sb("c_rs%d" % i, [128, 1]) for i in range(NS)]
    nb = [ph.sb("c_nb%d" % i, [128, 1]) for i in range(NS)]
    psO = [ph.ps("c_psO%d" % i) for i in range(4)]
    psT = [ph.ps("c_psT%d" % i, [128, 1024]) for i in range(2)]
    ident, modF = G["identf"], G["modF"]
    lng = ph.sb("c_lng", [128, 1024])
    lnb = ph.sb("c_lnb", [128, 1024])
    K.dma(lng[:, :], G["lnp"][l, 0:1, :].to_broadcast([128, D]), writes=[lng])
    K.dma(lnb[:, :], G["lnp"][l, 1:2, :].to_broadcast([128, D]), writes=[lnb])
    def ln_a(x, s):
        K.op(K.dve, lambda r: r.bn_stats(out=bn[s][:, 0:6], in_=x[:, 0:512]), [x], [bn[s]])
        K.op(K.dve, lambda r: r.bn_stats(out=bn[s][:, 6:12], in_=x[:, 512:1024]), [x], [bn[s]])
        K.op(K.dve, lambda r: r.bn_aggr(out=mv[s][:, :], in_=bn[s][:, :]), [bn[s]], [mv[s]])
        K.op(K.act, lambda r: r.activation(out=rs[s][:, :], in_=mv[s][:, 1:2], func=AF.Sqrt,
                                           bias=G_EPS[0][:, 0:1]), [mv[s], G_EPS[0]], [rs[s]])

    def ln_b(x, o, s):
        K.op(K.dve, lambda r: r.reciprocal(out=rs[s][:, :], in_=rs[s][:, :]), [rs[s]], [rs[s]])
        K.op(K.dve, lambda r: r.tensor_scalar(out=nb[s][:, :], in0=mv[s][:, 0:1], scalar1=rs[s][:, 0:1],
                                              scalar2=-1.0, op0=ALU.mult, op1=ALU.mult), [mv[s], rs[s]], [nb[s]])
        K.op(K.act, lambda r: r.activation(out=o[:, :], in_=x[:, :], func=AF.Identity, scale=rs[s][:, 0:1],
                                           bias=nb[s][:, 0:1]), [x, rs[s], nb[s]], [o])

    items = []
    for si, (t0, n, is_ctx) in enumerate(segs):
        for ti in range(n // 128):
            items.append((len(items), si, t0, n, is_ctx, ti))

    def tile_gen(item):
        cnt, si, t0, n, is_ctx, ti = item
        j = 1 if is_ctx else 0
        g1 = G["modR"][0][j]
        ms = mxs[si % 2]
        s = cnt % NS
        r0 = t0 + ti * 128
        if ti == 0:
            K.dma(ms[:, :, 0:n], G["mxT"][:, t0:t0 + n].rearrange("(k p) t -> p k t", p=128), writes=[ms])
        K.dma(hx[s][:, :], Hin[r0:r0 + 128, :], writes=[hx[s]])
        ps_ = []
        for hf in range(2):
            p = psO[(2 * cnt + hf) % 4]
            for k in range(8):
                K.mm(p[:, :], ms[:, k, ti * 128:(ti + 1) * 128], wo[:, k, hf * 512:(hf + 1) * 512],
                     k == 0, k == 7, [ms, wo], [p])
            ps_.append(p)
        yield
        for hf in range(2):
            p = ps_[hf]
            K.op(K.dve, lambda r: r.tensor_tensor(out=tmp[s][:, hf * 512:(hf + 1) * 512], in0=p[:, :],
                                                  in1=g1[:, hf * 512:(hf + 1) * 512], op=ALU.mult),
                 [p, g1], [tmp[s]])
        K.op(K.dve, lambda r: r.scalar_tensor_tensor(out=hx[s][:, :], in0=hx[s][:, :], scalar=ALPHA,
                                                     in1=tmp[s][:, :], op0=ALU.mult, op1=ALU.add),
             [hx[s], tmp[s]], [hx[s]])
        ln_a(hx[s], s)
        yield
        ln_b(hx[s], tmp[s], s)
        yield
        K.op(K.pool, lambda r: r.tensor_tensor(out=tmp[s][:, :], in0=tmp[s][:, :], in1=lng[:, :],
                                               op=ALU.mult), [tmp[s], lng], [tmp[s]])
        yield
        K.op(K.dve, lambda r: r.tensor_tensor(out=hx[s][:, :], in0=tmp[s][:, :], in1=lnb[:, :],
                                              op=ALU.add), [tmp[s], lnb], [hx[s]])
        K.dma(G["H1"][r0:r0 + 128, :], hx[s][:, :], reads=[hx[s]])
        ln_a(hx[s], s)
        yield
        ln_b(hx[s], tmp[s], s)
        yield
        pt = psT[cnt % 2]
        for k in range(8):
            K.tr(pt[:, k * 128:(k + 1) * 128], tmp[s][:, k * 128:(k + 1) * 128], ident[:, :],
                 [tmp[s], ident], [pt])
        yield
        for k in range(8):
            src = pt[:, k * 128:(k + 1) * 128]
            dst = xm2[s][:, k, :]
            if k % 2 == 0:
                K.op(K.act, lambda r: r.activation(out=dst, in_=src, func=AF.Identity,
                                                   scale=modF[:, 32 + k, j:j + 1], bias=modF[:, 24 + k, j:j + 1]),
                     [pt, modF], [xm2[s]])
            else:
                K.op(K.dve, lambda r: r.tensor_scalar(out=dst, in0=src, scalar1=modF[:, 32 + k, j:j + 1],
                                                      scalar2=modF[:, 24 + k, j:j + 1], op0=ALU.mult,
                                                      op1=ALU.add), [pt, modF], [xm2[s]])
        K.dma(G["xm2T"][:, r0:r0 + 128].rearrange("(k p) t -> p k t", p=128), xm2[s][:, :, :], reads=[xm2[s]])

    run_pipelined(items, tile_gen, 2)
    ph.close()


def phaseC1b(K, G, l, segs):
    ph = Phase(K)
    wu = ph.sb("u_wu", [128, 8, 2 * D_FF], BF16)
    stage = [ph.sb("u_st%d" % i, [128, 2048]) for i in range(2)]
    load_weight_bf16(K, ph, wu, G["w_up"][l], 2 * D_FF, stage)
    xs = [ph.sb("u_x%d" % i, [128, 8, 512], BF16) for i in range(2)]
    sa = [ph.sb("u_sa%d" % i, [128, 512]) for i in range(3)]
    sv = [ph.sb("u_sv%d" % i, [128, 512], BF16) for i in range(3)]
    psU = [ph.ps("u_ps%d" % i) for i in range(6)]
    ne = 0
    for si, (t0, n, is_ctx) in enumerate(segs):
        x = xs[si % 2]
        K.dma(x[:, :, 0:n], G["xm2T"][:, t0:t0 + n].rearrange("(k p) t -> p k t", p=128), writes=[x])
        for m in range(44):
            p = psU[m % 6]
            for k in range(8):
                K.mm(p[:, 0:n], wu[:, k, m * 128:(m + 1) * 128], x[:, k, 0:n], k == 0, k == 7, [wu, x], [p])
            if m < 22:
                s = sa[m % 3]
                _evac(K, ne, s[:, 0:n], p[:, 0:n], [p], [s])
                K.dma(G["aT"][m * 128:(m + 1) * 128, t0:t0 + n], s[:, 0:n], reads=[s])
            else:
                s = sv[m % 3]
                _evac(K, ne, s[:, 0:n], p[:, 0:n], [p], [s])
                K.dma(G["vT"][(m - 22) * 128:(m - 21) * 128, t0:t0 + n], s[:, 0:n], reads=[s])
            ne += 1
    ph.close()


def phaseC2(K, G, l, Hout, segs, last):
    ph = Phase(K)
    L = G["L"]
    wd = ph.sb("d_wd", [128, 22, D], BF16)
    stage = [ph.sb("d_st%d" % i, [128, 2048]) for i in range(2)]
    load_weight_bf16(K, ph, wd, G["w_down"][l], D, stage)
    wdc = ph.sb("d_wdc", [128, 22, 9])
    bdc = ph.sb("d_bdc", [128, 22])
    for t in range(9):
        K.dma(wdc[:, :, t], G["w_dconv"][l, t].rearrange("(c p) -> p c", p=128), writes=[wdc],
              allow_slow_non_contiguous=True)
    K.dma(bdc[:, :], G["b_dconv"][l].rearrange("(c p) -> p c", p=128), writes=[bdc], allow_slow_non_contiguous=True)
    ab = [ph.sb("d_a%d" % i, [128, 640]) for i in range(4)]
    vb = [ph.sb("d_v%d" % i, [128, 512], BF16) for i in range(4)]
    acc = [ph.sb("d_acc%d" % i, [128, 512]) for i in range(4)]
    hm = [ph.sb("d_hm%d" % i, [128, 22, 512], BF16) for i in range(2)]
    h1 = [ph.sb("d_h1%d" % i, [128, 1024]) for i in range(2)]
    tmp = [ph.sb("d_tmp%d" % i, [128, 1024]) for i in range(2)]
    bn = [ph.sb("d_bn%d" % i, [128, 12]) for i in range(2)]
    mv = [ph.sb("d_mv%d" % i, [128, 2]) for i in range(2)]
    rs = [ph.sb("d_rs%d" % i, [128, 1]) for i in range(2)]
    nb = [ph.sb("d_nb%d" % i, [128, 1]) for i in range(2)]
    psD = [ph.ps("d_ps%d" % i) for i in range(4)]
    lng = ph.sb("d_lng", [128, 1024])
    lnb = ph.sb("d_lnb", [128, 1024])
    K.dma(lng[:, :], G["lnp"][l, 2:3, :].to_broadcast([128, D]), writes=[lng])
    K.dma(lnb[:, :], G["lnp"][l, 3:4, :].to_broadcast([128, D]), writes=[lnb])
    ctm = [ph.sb("d_ctm%d" % i, [128, 512]) for i in range(3)]
    ntm = 0
    items = []
    nchunk = 0
    ntl = 0
    for si, (t0, n, is_ctx) in enumerate(segs):
        for cidx in range(22):
            items.append(("c", nchunk, si, t0, n, is_ctx, cidx))
            nchunk += 1
        items.append(None)
        for ti in range(n // 128):
            items.append(("t", ntl, si, t0, n, is_ctx, ti))
            ntl += 1
    ntm_ = [0]

    def chunk_gen(item):
        _, nch, si, t0, n, is_ctx, c = item
        hms = hm[si % 2]
        sl = nch % 4
        eng = K.pool if c % 3 == 2 else K.dve
        a, v, ac = ab[sl], vb[sl], acc[sl]
        if is_ctx:
            W = n
            K.dma(a[:, 64:64 + n], G["aT"][c * 128:(c + 1) * 128, t0:t0 + n], writes=[a])
            taps = [(0, -1), (0, 1)]
        else:
            W = GRID_W
            lo = t0 - 64 if t0 - 64 >= CTX else t0
            hi = t0 + n + 64 if t0 + n + 64 <= CTX + L else t0 + n
            if lo == t0:
                K.op(eng, lambda r: r.memset(a[:, 0:64], 0.0), [], [a])
            if hi == t0 + n:
                K.op(eng, lambda r: r.memset(a[:, 64 + n:128 + n], 0.0), [], [a])
            K.dma(a[:, 64 - (t0 - lo):64 + n + (hi - t0 - n)], G["aT"][c * 128:(c + 1) * 128, lo:hi], writes=[a])
            taps = [(dy, dx) for dy in (-1, 0, 1) for dx in (-1, 0, 1) if (dy, dx) != (0, 0)]
        K.dma(v[:, 0:n], G["vT"][c * 128:(c + 1) * 128, t0:t0 + n], writes=[v])
        yield
        K.op(eng, lambda r: r.tensor_scalar(out=ac[:, 0:n], in0=a[:, 64:64 + n], scalar1=wdc[:, c, 4:5],
                                            scalar2=bdc[:, c:c + 1], op0=ALU.mult, op1=ALU.add),
             [a, wdc, bdc], [ac])
        for (dy, dx) in taps:
            tap = (dy + 1) * 3 + (dx + 1)
            x0, x1 = max(0, -dx), W - max(0, dx)
            o3 = ac[:, 0:n].rearrange("p (r w) -> p r w", w=W)[:, :, x0:x1]
            base = 64 + dy * 64 if not is_ctx else 64
            i3 = a[:, base:base + n].rearrange("p (r w) -> p r w", w=W)[:, :, x0 + dx:x1 + dx]
            if eng is K.dve:
                K.op(eng, lambda r: r.scalar_tensor_tensor(out=o3, in0=i3, scalar=wdc[:, c, tap:tap + 1], in1=o3,
                                                           op0=ALU.mult, op1=ALU.add), [a, wdc, ac], [ac])
            else:
                tm = ctm[ntm_[0] % 3]
                ntm_[0] += 1
                t3 = tm[:, 0:n].rearrange("p (r w) -> p r w", w=W)[:, :, x0:x1]
                K.op(K.act, lambda r: r.activation(out=t3, in_=i3, func=AF.Copy, scale=wdc[:, c, tap:tap + 1]),
                     [a, wdc], [tm])
                K.op(K.pool, lambda r: r.tensor_tensor(out=o3, in0=o3, in1=t3, op=ALU.add), [tm, ac], [ac])
        yield
        K.op(K.act, lambda r: r.activation(out=ac[:, 0:n], in_=ac[:, 0:n], func=AF.Gelu_apprx_tanh), [ac], [ac])
        yield
        K.op(eng, lambda r: r.tensor_tensor(out=hms[:, c, 0:n], in0=ac[:, 0:n], in1=v[:, 0:n], op=ALU.mult),
             [ac, v], [hms])

    def tile_gen(item):
        _, cnt, si, t0, n, is_ctx, ti = item
        j = 1 if is_ctx else 0
        g2 = G["modR"][1][j]
        hms = hm[si % 2]
        s = cnt % 2
        r0 = t0 + ti * 128
        K.dma(h1[s][:, :], G["H1"][r0:r0 + 128, :], writes=[h1[s]])
        ps_ = []
        for hf in range(2):
            p = psD[(2 * cnt + hf) % 4]
            for c in range(22):
                K.mm(p[:, :], hms[:, c, ti * 128:(ti + 1) * 128], wd[:, c, hf * 512:(hf + 1) * 512],
                     c == 0, c == 21, [hms, wd], [p])
            ps_.append(p)
        yield
        for hf in range(2):
            p = ps_[hf]
            K.op(K.dve, lambda r: r.tensor_tensor(out=tmp[s][:, hf * 512:(hf + 1) * 512], in0=p[:, :],
                                                  in1=g2[:, hf * 512:(hf + 1) * 512], op=ALU.mult),
                 [p, g2], [tmp[s]])
        K.op(K.dve, lambda r: r.scalar_tensor_tensor(out=h1[s][:, :], in0=h1[s][:, :], scalar=ALPHA,
                                                     in1=tmp[s][:, :], op0=ALU.mult, op1=ALU.add),
             [h1[s], tmp[s]], [h1[s]])
        K.op(K.dve, lambda r: r.bn_stats(out=bn[s][:, 0:6], in_=h1[s][:, 0:512]), [h1[s]], [bn[s]])
        K.op(K.dve, lambda r: r.bn_stats(out=bn[s][:, 6:12], in_=h1[s][:, 512:1024]), [h1[s]], [bn[s]])
        K.op(K.dve, lambda r: r.bn_aggr(out=mv[s][:, :], in_=bn[s][:, :]), [bn[s]], [mv[s]])
        K.op(K.act, lambda r: r.activation(out=rs[s][:, :], in_=mv[s][:, 1:2], func=AF.Sqrt,
                                           bias=G_EPS[0][:, 0:1]), [mv[s], G_EPS[0]], [rs[s]])
        yield
        K.op(K.dve, lambda r: r.reciprocal(out=rs[s][:, :], in_=rs[s][:, :]), [rs[s]], [rs[s]])
        K.op(K.dve, lambda r: r.tensor_scalar(out=nb[s][:, :], in0=mv[s][:, 0:1], scalar1=rs[s][:, 0:1],
                                              scalar2=-1.0, op0=ALU.mult, op1=ALU.mult), [mv[s], rs[s]], [nb[s]])
        K.op(K.act, lambda r: r.activation(out=tmp[s][:, :], in_=h1[s][:, :], func=AF.Identity,
                                           scale=rs[s][:, 0:1], bias=nb[s][:, 0:1]),
             [h1[s], rs[s], nb[s]], [tmp[s]])
        yield
        K.op(K.pool, lambda r: r.tensor_tensor(out=tmp[s][:, :], in0=tmp[s][:, :], in1=lng[:, :],
                                               op=ALU.mult), [tmp[s], lng], [tmp[s]])
        yield
        K.op(K.dve, lambda r: r.tensor_tensor(out=h1[s][:, :], in0=tmp[s][:, :], in1=lnb[:, :],
                                              op=ALU.add), [tmp[s], lnb], [h1[s]])
        if last:
            K.dma(Hout[r0 - CTX:r0 - CTX + 128, :], h1[s][:, :], reads=[h1[s]])
        else:
            K.dma(Hout[r0:r0 + 128, :], h1[s][:, :], reads=[h1[s]])

    run_pipelined(items, lambda it: chunk_gen(it) if it[0] == "c" else tile_gen(it), 2)
    ph.close()
def build(L, nl=2, stop_after=None, mode=None):
    K = Kern()
    TALL = CTX + L
    G = {}
    G["L"], G["TALL"] = L, TALL
    G["_ext"] = {}

    def ext(name, shape, dt=F32):
        G["_ext"][name] = list(shape)
        return K.dram(name, shape, dt, "ExternalInput")
    G["h0"] = ext("h0", [TALL, D])
    G["cvec"] = ext("cvec", [128, 16])
    G["w_ada"] = ext("w_ada", [nl, D, 6 * D])
    G["b_adaF"] = ext("b_adaF", [nl, 128, 48])
    G["b_ada"] = ext("b_ada", [nl, 6 * D])
    G["wF"] = ext("wF", [nl, D, NFM * 128])
    G["wT"] = ext("wT", [nl, D, NTM])
    G["ident"] = ext("ident", [128, 128])
    G["w_out"] = ext("w_out", [nl, D, D])
    G["w_up"] = ext("w_up", [nl, D, 2 * D_FF])
    G["w_down"] = ext("w_down", [nl, D_FF, D])
    G["w_dconv"] = ext("w_dconv", [nl, 9, D_FF])
    G["b_dconv"] = ext("b_dconv", [nl, D_FF])
    G["lnp"] = ext("lnp", [nl, 4, D])
    G["jjrow"] = ext("jjrow", [128, 512])
    G["s5_par"] = ext("s5_par", [nl, 2, 128, 24])
    G["s5_BT"] = ext("s5_BT", [nl, 2, 2, 128, 1024])
    G["s5_CT"] = ext("s5_CT", [nl, 2, 2, 128, 1024])
    G["s5_d2"] = ext("s5_d2", [nl, 128, 2])
    G["s5_bg2"] = ext("s5_bg2", [nl, 128, 2])
    G["s5_w_glu"] = ext("s5_w_glu", [nl, 256, 256])
    G["ysc"] = K.dram("ysc", [256, TALL], F32)
    G["gla_wa2"] = ext("gla_wa2", [nl, 48, 256])
    G["gla_ba"] = ext("gla_ba", [nl, 128, 4])
    G["ml_gb"] = ext("ml_gb", [nl, 128, 1])
    G["gsel"] = ext("gsel", [128, 3 * 8 * 96])
    G["masks"] = ext("masks", [2, 128, 128])
    G["gmix"] = ext("gmix", [nl, 768])
    G["oF"] = K.dram("oF", [TALL, 768], F32)
    dbgA = stop_after == "A"
    G["zF"] = K.dram("zF", [NFM * 128, TALL], F32, "ExternalOutput" if dbgA else ("ExternalInput" if mode == "testB" else "Internal"))
    G["zT"] = K.dram("zT", [TALL, NTM], F32, "ExternalOutput" if dbgA else ("ExternalInput" if mode == "testB" else "Internal"))
    G["mxT"] = K.dram("mxT", [D, TALL], BF16, "ExternalInput" if mode == "testC" else ("ExternalOutput" if mode == "testB" else "Internal"))
    G["H1"] = K.dram("H1", [TALL, D], F32)
    G["Hs"] = K.dram("Hs", [TALL, D], F32, "ExternalOutput" if mode == "testC" else "Internal")
    G["xm2T"] = K.dram("xm2T", [D, TALL], BF16)
    G["aT"] = K.dram("aT", [D_FF, TALL], F32)
    G["vT"] = K.dram("vT", [D_FF, TALL], BF16)
    G["out"] = K.dram("out", [L, D], F32, "ExternalOutput")
    G["modF"] = K.sb("modF", [128, 48, 2])
    G["modR"] = [[K.sb("modR%d%d" % (w, j), [128, 1024]) for j in range(2)] for w in range(2)]
    G_EPS[0] = K.sb("eps", [128, 1])
    K.op(K.dve, lambda r: r.memset(G_EPS[0][:, :], LN_EPS), [], [G_EPS[0]])
    idf = K.sb("identf", [128, 128])
    G["identf"] = idf
    K.dma(idf[:, :], G["ident"][:, :], writes=[idf])
    segs = make_segs(L)
    Hin = G["h0"]
    if mode == "testB":
        phaseB1(K, G, 0, segs, False)
        phaseB2(K, G, 0, segs, False)
        K.finish()
        return K, G
    for l in range(nl):
        last = l == nl - 1
        phase0(K, G, l)
        if mode != "testC":
            phaseA(K, G, l, Hin, segs)
            if not dbgA:
                phaseB1(K, G, l, segs, last)
                phaseB2(K, G, l, segs, last)
        if dbgA:
            dm = K.dram("dbg_modF", [128, 96], F32, "ExternalOutput")
            K.dma(dm[:, :], G["modF"][:, :, :].rearrange("p c j -> p (c j)"), reads=[G["modF"]])
            break
        segsC = segs[1:] if (last and mode != "testC") else segs
        phaseC1a(K, G, l, Hin, segsC)
        phaseC1b(K, G, l, segsC)
        Hout = G["out"] if (last and mode != "testC") else G["Hs"]
        phaseC2(K, G, l, Hout, segsC, last and mode != "testC")
        Hin = G["Hs"]
    K.finish()
    return K, G


_CACHE = {}


def kernel(**inp):
    x = np.asarray(inp["x"], np.float32)
    B, L, _ = x.shape
    nl = inp["w_in"].shape[0]
    f32 = lambda a: np.ascontiguousarray(np.asarray(a, np.float32))
    wF, wT = _layout_w_in(f32(inp["w_in"]))
    shared = dict(
        w_ada=f32(inp["w_ada"]), b_ada=f32(inp["b_ada"]),
        b_adaF=f32(np.asarray(inp["b_ada"]).reshape(nl, 48, 128).transpose(0, 2, 1)),
        wF=wF, wT=wT, ident=np.eye(128, dtype=np.float32),
        w_out=f32(inp["w_out"]), w_up=f32(inp["w_up"]), w_down=f32(inp["w_down"]),
        w_dconv=f32(np.asarray(inp["w_dconv"]).reshape(nl, 9, D_FF)), b_dconv=f32(inp["b_dconv"]),
        lnp=f32(np.stack([inp["ln1_g"], inp["ln1_b"], inp["ln2_g"], inp["ln2_b"]], axis=1)),
    )
    shared.update(mixer_inputs({k: np.asarray(v, np.float32) for k, v in inp.items()
                                if k.startswith(("s5_", "gla_", "ml_"))}))
    if L not in _CACHE:
        _CACHE[L] = build(L, nl=nl)
    K, G = _CACHE[L]
    cc = np.asarray(inp["c_ctx"], np.float32).reshape(8, 128).T
    in_maps = []
    for b in range(B):
        cb = np.asarray(inp["c"][b], np.float32).reshape(8, 128).T
        m = dict(shared)
        m["h0"] = f32(np.concatenate([inp["ctx"][b], x[b]], axis=0))
        m["cvec"] = f32(np.stack([cb, cc], axis=2).reshape(128, 16))
        in_maps.append(m)
    res = run_bass_kernel_spmd(K.nc, in_maps, core_ids=list(range(B)))
    return np.stack([np.asarray(r["out"], np.float32) for r in res.results], axis=0)
```

```python
import bisect
import numpy as np
import concourse.bass as bass
import concourse.mybir as mybir
from concourse.bass_utils import run_bass_kernel_spmd

F32, BF16 = mybir.dt.float32, mybir.dt.bfloat16
AF = mybir.ActivationFunctionType
ALU = mybir.AluOpType
AX = mybir.AxisListType

D = 1024
CTX = 256
GRID_W = 64
S5_W, S5_G, S5_P, S5_HC = 256, 16, 64, 16
GLA_H, GLA_DK, GLA_DV, GLA_RANK = 4, 48, 96, 16
ML_H, ML_D = 4, 96
D_FF = 2816
DEPTH = 2
ALPHA = (2.0 * DEPTH) ** 0.25
LN_EPS = 1e-5
IN_SPLITS = (256, 192, 192, 384, 384, 32, 384, 384, 384, 384, 8, 8)
IN_OFF = [0] + list(np.cumsum(IN_SPLITS))
NFM = 15
NTM = 1536
TWO_PI = 2.0 * np.pi
S5W = 2
B2W = 4


class Dep:
    __slots__ = ("w", "r")

    def __init__(self):
        self.w = None
        self.r = {}


class Buf:
    def __init__(self, t):
        self.t = t
        self.d = Dep()

    def __getitem__(self, k):
        return self.t[k]


class Eng:
    def __init__(self, K, name, raw, is_pe=False):
        self.K, self.name, self.raw, self.is_pe = K, name, raw, is_pe
        self.sem = K.nc.semaphore(name + "_sem").__enter__()
        self.seq = 0
        self.count = 0
        self.inc_seq = []
        self.inc_cnt = []
        self.last = None
        self.last_seq = -1
        self.waited = {}

    def resolve(self, seq):
        i = bisect.bisect_left(self.inc_seq, seq)
        if i < len(self.inc_seq):
            return self.inc_cnt[i]
        assert self.last is not None and self.last_seq >= seq
        self.count += 1
        self.last.then_inc(self.sem, 1)
        self.inc_seq.append(self.last_seq)
        self.inc_cnt.append(self.count)
        return self.count

    def wait_tok(self, tok):
        if tok[0] == "c":
            e = tok[1]
            val = e.resolve(tok[2])
            key, sem = e.name, e.sem
        else:
            key, sem, val = ("dma", tok[1]), self.K.dsem[tok[1]], tok[2]
        if self.waited.get(key, 0) >= val:
            return
        self.raw.wait_ge(sem, val)
        self.waited[key] = val


class Kern:
    NDSEM = 24

    def __init__(self):
        self.nc = bass.Bass("TRN2", target_bir_lowering=False)
        nc = self.nc
        self.pe = Eng(self, "pe", nc.tensor, True)
        self.act = Eng(self, "act", nc.scalar)
        self.dve = Eng(self, "dve", nc.vector)
        self.pool = self.dve
        self.sp = Eng(self, "sp", nc.sync)
        self.engs = [self.pe, self.act, self.dve, self.sp]
        self.dsem = [nc.semaphore("dsem%d" % i).__enter__() for i in range(self.NDSEM)]
        self.dcnt = [0] * self.NDSEM
        self.dnext = 0
        self.sb_bytes = 0
        self.n_instr = 0

    def sb(self, name, shape, dt=F32):
        t = self.nc.sbuf_tensor(name, list(shape), dt).__enter__()
        n = 1
        for s in shape[1:]:
            n *= s
        self.sb_bytes += n * (4 if dt == F32 else 2)
        return Buf(t)

    def ps(self, name, shape, dt=F32):
        return Buf(self.nc.psum_tensor(name, list(shape), dt).__enter__())

    def dram(self, name, shape, dt=F32, kind="Internal"):
        return self.nc.dram_tensor(name, list(shape), dt, kind=kind).ap()

    def _deps(self, reads, writes):
        toks = []
        for d in reads:
            d = d.d if isinstance(d, Buf) else d
            if d.w is not None:
                toks.append(d.w)
        for d in writes:
            d = d.d if isinstance(d, Buf) else d
            if d.w is not None:
                toks.append(d.w)
            toks.extend(d.r.values())
        return toks

    def op(self, eng, fn, reads=(), writes=()):
        for t in self._deps(reads, writes):
            if eng.is_pe and t[0] == "c" and t[1] is eng:
                continue
            eng.wait_tok(t)
        ins = fn(eng.raw)
        seq = eng.seq
        eng.seq += 1
        eng.last, eng.last_seq = ins, seq
        tok = ("c", eng, seq)
        for d in reads:
            d = d.d if isinstance(d, Buf) else d
            d.r[eng.name] = tok
        for d in writes:
            d = d.d if isinstance(d, Buf) else d
            d.w = tok
            d.r = {}
        self.n_instr += 1
        return ins

    def dma(self, out, in_, reads=(), writes=(), q=None, **kw):
        q = q or self.sp
        for t in self._deps(reads, writes):
            q.wait_tok(t)
        s = self.dnext
        self.dnext = (self.dnext + 1) % self.NDSEM
        if self.dcnt[s] > 0:
            q.wait_tok(("d", s, 16 * self.dcnt[s]))
        self.dcnt[s] += 1
        q.raw.dma_start(out=out, in_=in_, **kw).then_inc(self.dsem[s], 16)
        tok = ("d", s, 16 * self.dcnt[s])
        for d in reads:
            d = d.d if isinstance(d, Buf) else d
            d.r[("dma", s, self.dcnt[s])] = tok
        for d in writes:
            d = d.d if isinstance(d, Buf) else d
            d.w = tok
            d.r = {}
        self.n_instr += 1
        return tok

    def finish(self):
        for e in self.engs:
            if e is self.sp or e.last is None:
                continue
            self.sp.wait_tok(("c", e, e.last_seq))
        for s in range(self.NDSEM):
            if self.dcnt[s] > 0:
                self.sp.wait_tok(("d", s, 16 * self.dcnt[s]))

    def mm(self, out, lhsT, rhs, start, stop, reads, writes):
        return self.op(self.pe, lambda e: e.matmul(out, lhsT=lhsT, rhs=rhs, start=start, stop=stop),
                       reads, writes)

    def tr(self, out, in_, ident, reads, writes):
        return self.op(self.pe, lambda e: e.transpose(out, in_, ident), reads, writes)


def _layout_w_in(w_in):
    nl = w_in.shape[0]
    wF = np.zeros((nl, w_in.shape[1], NFM * 128), np.float32)
    o = IN_OFF
    wF[:, :, 0:256] = w_in[:, :, 0:256]
    for h in range(4):
        t, s = 2 + h // 2, 64 * (h % 2)
        wF[:, :, t * 128 + s: t * 128 + s + 48] = w_in[:, :, o[1] + 48 * h: o[1] + 48 * (h + 1)]
        t = 4 + h // 2
        wF[:, :, t * 128 + s: t * 128 + s + 48] = w_in[:, :, o[2] + 48 * h: o[2] + 48 * (h + 1)]
    b = 6 * 128
    wF[:, :, b + 0: b + 16] = w_in[:, :, o[5]: o[5] + 16]
    wF[:, :, b + 32: b + 48] = w_in[:, :, o[5] + 16: o[5] + 32]
    wF[:, :, b + 64: b + 72] = w_in[:, :, o[10]: o[10] + 8]
    wF[:, :, b + 72: b + 80] = w_in[:, :, o[11]: o[11] + 8]
    for h in range(4):
        wF[:, :, (7 + h) * 128: (7 + h) * 128 + 96] = w_in[:, :, o[6] + 96 * h: o[6] + 96 * (h + 1)]
        wF[:, :, (11 + h) * 128: (11 + h) * 128 + 96] = w_in[:, :, o[7] + 96 * h: o[7] + 96 * (h + 1)]
    wT = np.concatenate([w_in[:, :, o[3]:o[4]], w_in[:, :, o[4]:o[5]],
                         w_in[:, :, o[8]:o[9]], w_in[:, :, o[9]:o[10]]], axis=2)
    return np.ascontiguousarray(wF), np.ascontiguousarray(wT)


FM_ROWS = [128, 128, 128, 128, 128, 128, 80] + [96] * 8
FM_SCALE = [1.0, 1.0, GLA_DK ** -0.5, GLA_DK ** -0.5, 1.0, 1.0, 1.0] + [1.0] * 4 + [ML_D ** -0.5] * 4


import contextlib


class Phase:
    def __init__(self, K):
        self.K = K
        self.stack = contextlib.ExitStack()

    UID = [0]

    def sb(self, name, shape, dt=F32):
        Phase.UID[0] += 1
        t = self.stack.enter_context(self.K.nc.sbuf_tensor("%s_u%d" % (name, Phase.UID[0]), list(shape), dt))
        return Buf(t)

    def ps(self, name, shape=(128, 512), dt=F32):
        Phase.UID[0] += 1
        t = self.stack.enter_context(self.K.nc.psum_tensor("%s_u%d" % (name, Phase.UID[0]), list(shape), dt))
        return Buf(t)

    def close(self):
        self.K.barrier()
        self.stack.close()


def _barrier(K):
    toks = []
    for e in K.engs:
        if e.last is not None:
            toks.append(("c", e, e.last_seq))
    for s in range(K.NDSEM):
        if K.dcnt[s] > 0:
            toks.append(("d", s, 16 * K.dcnt[s]))
    for e in K.engs:
        for t in toks:
            if t[0] == "c" and t[1] is e:
                continue
            e.wait_tok(t)


Kern.barrier = _barrier


def _cvt(K, i, out, in_, reads, writes, scale=None):
    e = (K.act, K.dve, K.pool)[i % 3]
    if e is K.act:
        if scale is None:
            return K.op(e, lambda r: r.copy(out=out, in_=in_), reads, writes)
        return K.op(e, lambda r: r.mul(out=out, in_=in_, mul=scale), reads, writes)
    if scale is None:
        return K.op(e, lambda r: r.tensor_copy(out=out, in_=in_), reads, writes)
    return K.op(e, lambda r: r.tensor_scalar(out=out, in0=in_, scalar1=scale, scalar2=None, op0=ALU.mult),
                reads, writes)


def _evac(K, i, out, in_, reads, writes, scale=None):
    e = (K.act, K.dve)[i % 2]
    if e is K.act:
        if scale is None:
            return K.op(e, lambda r: r.copy(out=out, in_=in_), reads, writes)
        return K.op(e, lambda r: r.mul(out=out, in_=in_, mul=scale), reads, writes)
    if scale is None:
        return K.op(e, lambda r: r.tensor_copy(out=out, in_=in_), reads, writes)
    return K.op(e, lambda r: r.tensor_scalar(out=out, in0=in_, scalar1=scale, scalar2=None, op0=ALU.mult),
                reads, writes)


def load_weight_bf16(K, ph, dst, src, ncols, stage, cnt=[0]):
    for k in range(src.shape[0] // 128):
        sw = stage[0].t.shape[1]
        for c0 in range(0, ncols, sw):
            c1 = min(ncols, c0 + sw)
            st = stage[cnt[0] % len(stage)]
            K.dma(st[:, 0:c1 - c0], src[k * 128:(k + 1) * 128, c0:c1], writes=[st])
            _cvt(K, cnt[0], dst[:, k, c0:c1], st[:, 0:c1 - c0], [st], [dst])
            cnt[0] += 1


def phase0(K, G, l):
    ph = Phase(K)
    sc = ph.sb("p0_sc", [128, 16])
    screp = ph.sb("p0_screp", [128, 16, 128])
    cv = ph.sb("p0_cv", [128, 16])
    bF = ph.sb("p0_bF", [128, 48])
    wst = [ph.sb("p0_w%d" % i, [128, 6144]) for i in range(2)]
    brow = ph.sb("p0_brow", [128, 1024])
    psF = ph.ps("p0_psF")
    psR = [[ph.ps("p0_psR%d%d" % (j, hf)) for hf in range(2)] for j in range(2)]
    K.dma(cv[:, :], G["cvec"][:, :], writes=[cv])
    K.dma(bF[:, :], G["b_adaF"][l], writes=[bF])
    K.op(K.act, lambda r: r.activation(out=sc[:, :], in_=cv[:, :], func=AF.Silu), [cv], [sc])
    for i in range(16):
        K.op(K.dve, lambda r: r.tensor_copy(out=screp[:, i, :], in_=sc[:, i:i + 1].to_broadcast([128, 128])),
             [sc], [screp])
    n = 0
    for cb in range(8):
        st = wst[n % 2]
        n += 1
        K.dma(st[:, :].rearrange("p (k c) -> p k c", k=8),
              G["w_ada"][l, :, cb * 768:(cb + 1) * 768].rearrange("(k p) c -> p k c", p=128), writes=[st])
        for c in range(6):
            cc = cb * 6 + c
            for k in range(8):
                K.mm(psF[:, 2 * cc:2 * cc + 2], st[:, k * 768 + c * 128: k * 768 + (c + 1) * 128],
                     sc[:, 2 * k:2 * k + 2], k == 0, k == 7, [st, sc], [psF])
    modF = G["modF"]
    K.op(K.dve, lambda r: r.tensor_tensor(
        out=modF[:, :, :], in0=psF[:, 0:96].rearrange("p (c j) -> p c j", j=2),
        in1=bF[:, :].unsqueeze(2).to_broadcast([128, 48, 2]), op=ALU.add), [psF, bF], [modF])
    for c0 in (8, 32):
        K.op(K.dve, lambda r: r.tensor_scalar(out=modF[:, c0:c0 + 8, :], in0=modF[:, c0:c0 + 8, :],
                                              scalar1=1.0, scalar2=None, op0=ALU.add), [modF], [modF])
    for w, c0 in enumerate((2048, 5120)):
        K.dma(brow[:, :], G["b_ada"][l:l + 1, c0:c0 + 1024].to_broadcast([128, 1024]), writes=[brow])
        for k in range(8):
            st = wst[n % 2]
            n += 1
            K.dma(st[:, 0:1024], G["w_ada"][l, k * 128:(k + 1) * 128, c0:c0 + 1024], writes=[st])
            for j in range(2):
                for hf in range(2):
                    K.mm(psR[j][hf][:, :], screp[:, 2 * k + j, :], st[:, hf * 512:(hf + 1) * 512],
                         k == 0, k == 7, [st, screp], [psR[j][hf]])
        for j in range(2):
            for hf in range(2):
                dst = G["modR"][w][j]
                K.op(K.dve, lambda r: r.tensor_tensor(out=dst[:, hf * 512:(hf + 1) * 512], in0=psR[j][hf][:, :],
                                                      in1=brow[:, hf * 512:(hf + 1) * 512], op=ALU.add),
                     [psR[j][hf], brow], [dst])
    ph.close()


G_EPS = [None]


def ln_stats(K, xi, bn, mv, rs, nb):
    K.op(K.dve, lambda r: r.bn_stats(out=bn[:, 0:6], in_=xi[:, 0:512]), [xi], [bn])
    K.op(K.dve, lambda r: r.bn_stats(out=bn[:, 6:12], in_=xi[:, 512:1024]), [xi], [bn])
    K.op(K.dve, lambda r: r.bn_aggr(out=mv[:, :], in_=bn[:, :]), [bn], [mv])
    K.op(K.act, lambda r: r.activation(out=rs[:, :], in_=mv[:, 1:2], func=AF.Sqrt, bias=G_EPS[0][:, 0:1]),
         [mv, G_EPS[0]], [rs])
    K.op(K.dve, lambda r: r.reciprocal(out=rs[:, :], in_=rs[:, :]), [rs], [rs])
    K.op(K.dve, lambda r: r.tensor_scalar(out=nb[:, :], in0=mv[:, 0:1], scalar1=rs[:, 0:1], scalar2=-1.0,
                                          op0=ALU.mult, op1=ALU.mult), [mv, rs], [nb])


def phaseA(K, G, l, Hin, segs):
    ph = Phase(K)
    wFb = ph.sb("a_wF", [128, 8, NFM * 128], BF16)
    wTb = ph.sb("a_wT", [128, 8, NTM], BF16)
    stage = [ph.sb("a_st%d" % i, [128, 2048]) for i in range(2)]
    load_weight_bf16(K, ph, wFb, G["wF"][l], NFM * 128, stage)
    load_weight_bf16(K, ph, wTb, G["wT"][l], NTM, stage)
    xin = [ph.sb("a_xin%d" % i, [128, 1024]) for i in range(2)]
    xn = [ph.sb("a_xn%d" % i, [128, 1024]) for i in range(2)]
    bn = [ph.sb("a_bn%d" % i, [128, 12]) for i in range(2)]
    mv = [ph.sb("a_mv%d" % i, [128, 2]) for i in range(2)]
    rs = [ph.sb("a_rs%d" % i, [128, 1]) for i in range(2)]
    nb = [ph.sb("a_nb%d" % i, [128, 1]) for i in range(2)]
    fenceA = [ph.sb("a_fence%d" % i, [128, 1]) for i in range(2)]
    xmT = ph.sb("a_xmT", [128, 8, 512], BF16)
    stF = [ph.sb("a_stF%d" % i, [128, 512]) for i in range(3)]
    stT = [ph.sb("a_stT%d" % i, [128, NTM]) for i in range(2)]
    psT = [ph.ps("a_psT%d" % i, [128, 1024]) for i in range(2)]
    psM = [ph.ps("a_psM%d" % i) for i in range(4)]
    ident = G["identf"]
    modF = G["modF"]
    cnt = 0
    ne = 0
    for (t0, n, is_ctx) in segs:
        j = 1 if is_ctx else 0
        nt = n // 128
        for ti in range(nt):
            s = cnt % 2
            cnt += 1
            K.dma(xin[s][:, :], Hin[t0 + 128 * ti: t0 + 128 * (ti + 1), :], writes=[xin[s]])
            ln_stats(K, xin[s], bn[s], mv[s], rs[s], nb[s])
            K.op(K.dve, lambda r: r.memset(fenceA[s][:, :], 0.0), [rs[s], nb[s]], [fenceA[s]])
            K.op(K.act, lambda r: r.activation(out=xn[s][:, :], in_=xin[s][:, :], func=AF.Identity,
                                               scale=rs[s][:, 0:1], bias=nb[s][:, 0:1]),
                 [xin[s], rs[s], nb[s], fenceA[s]], [xn[s]])
            pt = psT[s % 2]
            for k in range(8):
                K.tr(pt[:, k * 128:(k + 1) * 128], xn[s][:, k * 128:(k + 1) * 128], ident[:, :],
                     [xn[s], ident], [pt])
            for k in range(8):
                src = pt[:, k * 128:(k + 1) * 128]
                dst = xmT[:, k, ti * 128:(ti + 1) * 128]
                if k % 2 == 0:
                    K.op(K.act, lambda r: r.activation(out=dst, in_=src, func=AF.Identity,
                                                       scale=modF[:, 8 + k, j:j + 1], bias=modF[:, k, j:j + 1]),
                         [pt, modF], [xmT])
                else:
                    K.op(K.dve, lambda r: r.tensor_scalar(out=dst, in0=src, scalar1=modF[:, 8 + k, j:j + 1],
                                                          scalar2=modF[:, k, j:j + 1], op0=ALU.mult, op1=ALU.add),
                         [pt, modF], [xmT])
        for m in range(NFM):
            M = FM_ROWS[m]
            p = psM[m % 4]
            for k in range(8):
                K.mm(p[0:M, 0:n], wFb[:, k, m * 128: m * 128 + M], xmT[:, k, 0:n], k == 0, k == 7, [wFb, xmT], [p])
            s = stF[m % 3]
            _evac(K, ne, s[0:M, 0:n], p[0:M, 0:n], [p], [s], None if FM_SCALE[m] == 1.0 else FM_SCALE[m])
            ne += 1
            K.dma(G["zF"][m * 128: m * 128 + M, t0:t0 + n], s[0:M, 0:n], reads=[s])
        for ti in range(nt):
            s = stT[ti % 2]
            for c in range(3):
                p = psM[(c + ti * 3 + 3) % 4]
                for k in range(8):
                    K.mm(p[:, :], xmT[:, k, ti * 128:(ti + 1) * 128], wTb[:, k, c * 512:(c + 1) * 512],
                         k == 0, k == 7, [wTb, xmT], [p])
                _evac(K, ne, s[:, c * 512:(c + 1) * 512], p[:, :], [p], [s])
                ne += 1
            K.dma(G["zT"][t0 + ti * 128: t0 + (ti + 1) * 128, :], s[:, :], reads=[s])
    ph.close()


def make_segs(L):
    return [(0, CTX, True)] + [(CTX + 512 * i, 512, False) for i in range(L // 512)]


def _layout_s5(a_re, a_im, log_dt, b_re, b_im, c_re, c_im):
    nl = a_re.shape[0]
    par = np.zeros((nl, 2, 128, 24), np.float32)
    BT = np.zeros((nl, 2, 2, 128, 8, 128), np.float32)
    CT = np.zeros((nl, 2, 2, 128, 8, 128), np.float32)
    for b in range(8):
        for gl in range(2):
            g = 2 * b + gl
            g8 = g % 8
            par[:, :, gl * 64:(gl + 1) * 64, b] = a_re[:, :, g, :]
            par[:, :, gl * 64:(gl + 1) * 64, 8 + b] = a_im[:, :, g, :]
            par[:, :, gl * 64:(gl + 1) * 64, 16 + b] = log_dt[:, :, g][:, :, None]
            for x, (bb, cc) in enumerate(((b_re, c_re), (b_im, c_im))):
                BT[:, :, x, g8 * 16:(g8 + 1) * 16, b, gl * 64:(gl + 1) * 64] = bb[:, :, g].transpose(0, 1, 3, 2)
                CT[:, :, x, gl * 64:(gl + 1) * 64, b, g8 * 16:(g8 + 1) * 16] = cc[:, :, g].transpose(0, 1, 3, 2)
    return par, BT.reshape(nl, 2, 2, 128, 1024), CT.reshape(nl, 2, 2, 128, 1024)


def mixer_inputs(p):
    par, BT, CT = _layout_s5(p["s5_a_re"], p["s5_a_im"], p["s5_log_dt"], p["s5_b_re"], p["s5_b_im"],
                             p["s5_c_re"], p["s5_c_im"])
    nl = par.shape[0]
    out = dict(s5_par=par, s5_BT=BT, s5_CT=CT,
               s5_d2=np.ascontiguousarray(p["s5_d"].reshape(nl, 2, 128).transpose(0, 2, 1)),
               s5_bg2=np.ascontiguousarray(p["s5_b_glu"].reshape(nl, 2, 128).transpose(0, 2, 1)),
               s5_w_glu=np.ascontiguousarray(p["s5_w_glu"]),
               jjrow=np.ascontiguousarray(np.broadcast_to(np.arange(1, 513, dtype=np.float32), (128, 512))))
    wa2 = np.zeros((nl, 48, 256), np.float32)
    ba = np.zeros((nl, 128, 4), np.float32)
    for d in range(2):
        for h in range(4):
            wa2[:, 32 * d:32 * d + 16, 64 * h:64 * h + 48] = p["gla_w_a2"][:, d, :, 48 * h:48 * (h + 1)]
            ba[:, 64 * (h % 2):64 * (h % 2) + 48, 2 * d + h // 2] = p["gla_b_a"][:, d, 48 * h:48 * (h + 1)]
    gb = np.zeros((nl, 128, 1), np.float32)
    gb[:, 64:72, 0] = p["ml_i_bias"].reshape(nl, 8)
    gb[:, 72:80, 0] = p["ml_f_bias"].reshape(nl, 8)
    gsel = np.zeros((128, 3, 2, 4, 96), np.float32)
    for d in range(2):
        for h in range(4):
            gsel[64 + 4 * d + h, 0, d, h, :] = 1.0
            gsel[72 + 4 * d + h, 1, d, h, :] = 1.0
            gsel[72 + 4 * d + h, 2, d, h, :] = -1.0
    s_, t_ = np.arange(128)[:, None], np.arange(128)[None, :]
    masks = np.stack([(s_ <= t_), (s_ >= t_)]).astype(np.float32)
    out.update(gla_wa2=wa2, gla_ba=ba, ml_gb=gb, gsel=gsel.reshape(128, -1), masks=masks,
               gmix=np.concatenate([p["gla_g"], p["ml_g"]], axis=1))
    return {k: np.ascontiguousarray(v, dtype=np.float32) for k, v in out.items()}


DBG = {"on": False, "n": 0}


def _dbg(K, name, buf, ap, shape, dt=F32):
    if not DBG["on"]:
        return
    t = K.dram("dbg_%s_%d" % (name, DBG["n"]), list(shape), dt, "ExternalOutput")
    DBG["n"] += 1
    K.dma(t, ap, reads=[buf])


def run_pipelined(items, fn, width, stagger=False):
    active = []
    items = list(items)
    pos = 0
    while pos < len(items) or active:
        while pos < len(items) and len(active) < width:
            if items[pos] is None:
                if active:
                    break
                pos += 1
                continue
            active.append(fn(items[pos]))
            pos += 1
            if stagger:
                break
        for g in list(active):
            try:
                next(g)
            except StopIteration:
                active.remove(g)


def _range_reduce_sin(K, u, kint, tmp, shift, deps):
    if shift != 0.0:
        K.op(K.dve, lambda r: r.tensor_scalar(out=tmp, in0=u, scalar1=shift, scalar2=None, op0=ALU.add), deps, deps)
        src = tmp
    else:
        src = u
    K.op(K.dve, lambda r: r.tensor_scalar(out=kint, in0=src, scalar1=1.0 / TWO_PI, scalar2=None, op0=ALU.mult),
         deps, deps)
    K.op(K.dve, lambda r: r.scalar_tensor_tensor(out=tmp, in0=kint, scalar=-TWO_PI, in1=src, op0=ALU.mult,
                                                 op1=ALU.add), deps, deps)
    K.op(K.dve, lambda r: r.tensor_scalar(out=tmp, in0=tmp, scalar1=-3.1415925, scalar2=3.1415925, op0=ALU.max,
                                          op1=ALU.min), deps, deps)


def phaseB1(K, G, l, segs, last):
    ph = Phase(K)
    I32 = mybir.dt.int32
    jj = ph.sb("s_jj", [128, 512])
    d2 = ph.sb("s_d2", [128, 2])
    bg = ph.sb("s_bg", [128, 2])
    wg = ph.sb("s_wg", [128, 2, 256], BF16)
    stage = [ph.sb("s_st%d" % i, [128, 1024]) for i in range(2)]
    K.dma(jj[:, :], G["jjrow"][:, :], writes=[jj])
    K.dma(d2[:, :], G["s5_d2"][l], writes=[d2])
    K.dma(bg[:, :], G["s5_bg2"][l], writes=[bg])
    load_weight_bf16(K, ph, wg, G["s5_w_glu"][l], 256, stage)
    BTr = ph.sb("s_BTr", [128, 8, 128], BF16)
    BTi = ph.sb("s_BTi", [128, 8, 128], BF16)
    CTr = ph.sb("s_CTr", [128, 8, 128], BF16)
    CTrn = ph.sb("s_CTrn", [128, 8, 128], BF16)
    CTin = ph.sb("s_CTin", [128, 8, 128], BF16)
    par = ph.sb("s_par", [128, 24])
    Tre = ph.sb("s_Tre", [128, 8, 512])
    Tim = ph.sb("s_Tim", [128, 8, 512])
    Cc = ph.sb("s_Cc", [128, 8, 512])
    Ss = ph.sb("s_Ss", [128, 8, 512])
    sm = {nm: ph.sb("s_" + nm, [128, 8]) for nm in
          ("dt", "r", "th", "sin", "cos", "lre", "lim", "den", "t1", "t2", "zre", "zim", "cre", "cim")}
    smi = ph.sb("s_smi", [128, 8], I32)
    kint = ph.sb("s_kint", [128, 512], I32)
    ub_ = [ph.sb("s_ub%d" % i, [128, 2, 512], BF16) for i in range(2)]
    u32_ = [ph.sb("s_u32%d" % i, [128, 2, 512]) for i in range(2)]
    wk = {nm: [ph.sb("s_%s%d" % (nm, i), [128, 512]) for i in range(2)] for nm in
          ("a1", "a2", "a3", "a4", "vre", "vim", "wre", "wim")}
    xb = [[ph.sb("s_xb%d%d" % (q, i), [128, 512], BF16) for i in range(2)] for q in range(4)]
    tcs = [[ph.sb("s_tc%d%d" % (i, j), [128, 1]) for j in range(2)] for i in range(2)]
    cre_d = [Dep() for _ in range(8)]
    cim_d = [Dep() for _ in range(8)]
    ysb = [ph.sb("s_ysb%d" % i, [128, 512]) for i in range(2)]
    yfw = [ph.sb("s_yfw%d" % i, [128, 512]) for i in range(2)]
    yg = [ph.sb("s_yg%d" % i, [128, 512]) for i in range(2)]
    ygb = [ph.sb("s_ygb%d" % i, [128, 512], BF16) for i in range(2)]
    sig = [ph.sb("s_sig%d" % i, [128, 512]) for i in range(2)]
    mxo = [ph.sb("s_mxo%d" % i, [128, 512], BF16) for i in range(2)]
    pP = [[ph.ps("s_pP%d%d" % (x, i)) for i in range(2)] for x in range(2)]
    pY = [ph.ps("s_pY%d" % i) for i in range(2)]
    pG = [ph.ps("s_pG%d" % i) for i in range(2)]
    ydep = {}
    it = 0
    for d in range(2):
        K.dma(par[:, :], G["s5_par"][l, d], writes=[par])
        for x, (dstb, scale) in enumerate(((BTr, None), (BTi, None))):
            st = stage[x]
            K.dma(st[:, 0:1024], G["s5_BT"][l, d, x], writes=[st])
            _cvt(K, x, dstb[:, :, :].rearrange("p b m -> p (b m)"), st[:, 0:1024], [st], [dstb])
        st = stage[0]
        K.dma(st[:, 0:1024], G["s5_CT"][l, d, 0], writes=[st])
        _cvt(K, 1, CTr[:, :, :].rearrange("p b m -> p (b m)"), st[:, 0:1024], [st], [CTr])
        _cvt(K, 1, CTrn[:, :, :].rearrange("p b m -> p (b m)"), st[:, 0:1024], [st], [CTrn], scale=-1.0)
        st = stage[1]
        K.dma(st[:, 0:1024], G["s5_CT"][l, d, 1], writes=[st])
        _cvt(K, 1, CTin[:, :, :].rearrange("p b m -> p (b m)"), st[:, 0:1024], [st], [CTin], scale=-1.0)
        are, aim, ldt = par[:, 0:8], par[:, 8:16], par[:, 16:24]
        S = {k: v[:, :] for k, v in sm.items()}
        allsm = list(sm.values()) + [smi, par]

        def dv(fn):
            K.op(K.dve, fn, allsm, allsm)

        K.op(K.act, lambda r: r.activation(out=S["dt"], in_=ldt, func=AF.Exp), allsm, allsm)
        dv(lambda r: r.tensor_tensor(out=S["t1"], in0=are, in1=S["dt"], op=ALU.mult))
        K.op(K.act, lambda r: r.activation(out=S["r"], in_=S["t1"], func=AF.Exp), allsm, allsm)
        dv(lambda r: r.tensor_tensor(out=S["th"], in0=aim, in1=S["dt"], op=ALU.mult))
        for nm, shift in (("sin", 0.0), ("cos", 0.5 * np.pi)):
            _range_reduce_sin(K, S["th"], smi[:, :], S["t2"], shift, allsm)
            K.op(K.act, lambda r: r.activation(out=S[nm], in_=S["t2"], func=AF.Sin), allsm, allsm)
        dv(lambda r: r.tensor_tensor(out=S["lre"], in0=S["r"], in1=S["cos"], op=ALU.mult))
        dv(lambda r: r.tensor_tensor(out=S["lim"], in0=S["r"], in1=S["sin"], op=ALU.mult))
        dv(lambda r: r.tensor_tensor(out=S["den"], in0=are, in1=are, op=ALU.mult))
        dv(lambda r: r.tensor_tensor(out=S["t1"], in0=aim, in1=aim, op=ALU.mult))
        dv(lambda r: r.tensor_tensor(out=S["den"], in0=S["den"], in1=S["t1"], op=ALU.add))
        dv(lambda r: r.reciprocal(out=S["den"], in_=S["den"]))
        dv(lambda r: r.tensor_scalar(out=S["lre"], in0=S["lre"], scalar1=-1.0, scalar2=None, op0=ALU.add))
        dv(lambda r: r.tensor_tensor(out=S["t1"], in0=S["lre"], in1=are, op=ALU.mult))
        dv(lambda r: r.tensor_tensor(out=S["t2"], in0=S["lim"], in1=aim, op=ALU.mult))
        dv(lambda r: r.tensor_tensor(out=S["t1"], in0=S["t1"], in1=S["t2"], op=ALU.add))
        dv(lambda r: r.tensor_tensor(out=S["zre"], in0=S["t1"], in1=S["den"], op=ALU.mult))
        dv(lambda r: r.tensor_tensor(out=S["t1"], in0=S["lim"], in1=are, op=ALU.mult))
        dv(lambda r: r.tensor_tensor(out=S["t2"], in0=S["lre"], in1=aim, op=ALU.mult))
        dv(lambda r: r.tensor_tensor(out=S["t1"], in0=S["t1"], in1=S["t2"], op=ALU.subtract))
        dv(lambda r: r.tensor_tensor(out=S["zim"], in0=S["t1"], in1=S["den"], op=ALU.mult))
        K.op(K.dve, lambda r: r.memset(S["cre"], 0.0), allsm + cre_d, allsm + cre_d)
        K.op(K.dve, lambda r: r.memset(S["cim"], 0.0), allsm + cim_d, allsm + cim_d)
        tabs = [Tre, Tim, Cc, Ss, kint] + allsm
        a1, a2 = wk["a1"][0], wk["a2"][0]
        for b in range(8):
            K.op(K.dve, lambda r: r.tensor_scalar(out=a1[:, :], in0=jj[:, :], scalar1=S["th"][:, b:b + 1],
                                                  scalar2=None, op0=ALU.mult), [jj] + tabs, [a1] + tabs)
            for dst, shift in ((Ss, 0.0), (Cc, 0.5 * np.pi)):
                _range_reduce_sin(K, a1[:, :], kint[:, :], a2[:, :], shift, [a1, a2] + tabs)
                K.op(K.act, lambda r: r.activation(out=dst[:, b, :], in_=a2[:, :], func=AF.Sin),
                     [a1, a2] + tabs, [a1, a2] + tabs)
            K.op(K.dve, lambda r: r.tensor_scalar(out=a1[:, :], in0=Cc[:, b, :], scalar1=S["zre"][:, b:b + 1],
                                                  scalar2=None, op0=ALU.mult), [a1] + tabs, [a1] + tabs)
            K.op(K.dve, lambda r: r.scalar_tensor_tensor(out=Tre[:, b, :], in0=Ss[:, b, :],
                                                         scalar=S["zim"][:, b:b + 1], in1=a1[:, :], op0=ALU.mult,
                                                         op1=ALU.add), [a1] + tabs, [a1] + tabs)
            K.op(K.dve, lambda r: r.tensor_scalar(out=a1[:, :], in0=Ss[:, b, :], scalar1=S["zre"][:, b:b + 1],
                                                  scalar2=None, op0=ALU.mult), [a1] + tabs, [a1] + tabs)
            K.op(K.dve, lambda r: r.scalar_tensor_tensor(out=Tim[:, b, :], in0=Cc[:, b, :],
                                                         scalar=S["zim"][:, b:b + 1], in1=a1[:, :], op0=ALU.mult,
                                                         op1=ALU.subtract), [a1] + tabs, [a1] + tabs)
        order = segs if d == 0 else [segs[0]] + segs[1:][::-1]
        rev = d == 1
        for (t0, n, is_ctx) in order:
            sl = it % 2
            it += 1
            u32, ub = u32_[sl], ub_[sl]
            K.dma(u32[:, :, 0:n], G["zF"][0:256, t0:t0 + n].rearrange("(f p) t -> p f t", p=128), writes=[u32])
            K.op(K.act, lambda r: r.copy(out=ub[:, :, 0:n], in_=u32[:, :, 0:n]), [u32], [ub])

            def tv(T, b):
                v = T[:, b, 0:n]
                return v[:, ::-1] if rev else v

            cl = 0 if rev else n - 1
            def blk_gen(b):
                f, b4 = b // 4, b % 4
                py = pY[f]
                q = b % 2
                pre, pim = pP[0][q], pP[1][q]
                K.mm(pre[:, 0:n], BTr[:, b, :], ub[:, f, 0:n], True, True, [BTr, ub], [pre])
                K.mm(pim[:, 0:n], BTi[:, b, :], ub[:, f, 0:n], True, True, [BTi, ub], [pim])
                yield
                A = {k: v[q] for k, v in wk.items()}
                K.op(K.dve, lambda r: r.tensor_tensor(out=A["a1"][:, 0:n], in0=pre[:, 0:n], in1=tv(Tre, b),
                                                      op=ALU.mult), [pre, Tre], [A["a1"]])
                K.op(K.dve, lambda r: r.tensor_tensor(out=A["a2"][:, 0:n], in0=pim[:, 0:n], in1=tv(Tim, b),
                                                      op=ALU.mult), [pim, Tim], [A["a2"]])
                K.op(K.dve, lambda r: r.tensor_tensor(out=A["a3"][:, 0:n], in0=pim[:, 0:n], in1=tv(Tre, b),
                                                      op=ALU.mult), [pim, Tre], [A["a3"]])
                K.op(K.dve, lambda r: r.tensor_tensor(out=A["a4"][:, 0:n], in0=pre[:, 0:n], in1=tv(Tim, b),
                                                      op=ALU.mult), [pre, Tim], [A["a4"]])
                K.op(K.pool, lambda r: r.tensor_tensor(out=A["vre"][:, 0:n], in0=A["a1"][:, 0:n],
                                                       in1=A["a2"][:, 0:n], op=ALU.subtract),
                     [A["a1"], A["a2"]], [A["vre"]])
                K.op(K.pool, lambda r: r.tensor_tensor(out=A["vim"][:, 0:n], in0=A["a3"][:, 0:n],
                                                       in1=A["a4"][:, 0:n], op=ALU.add),
                     [A["a3"], A["a4"]], [A["vim"]])
                yield
                for wn, vn, cn, cd in (("wre", "vre", "cre", cre_d), ("wim", "vim", "cim", cim_d)):
                    o, i1 = A[wn][:, 0:n], A[vn][:, 0:n]
                    if rev:
                        o, i1 = o[:, ::-1], i1[:, ::-1]
                    K.op(K.dve, lambda r: r.tensor_tensor_scan(
                        out=o, data0=S["r"][:, b:b + 1].to_broadcast([128, n]), data1=i1,
                        initial=S[cn][:, b:b + 1], op0=ALU.mult, op1=ALU.add),
                        [A[vn], sm["r"], cd[b]], [A[wn]])
                wr, wi = A["wre"][:, cl:cl + 1], A["wim"][:, cl:cl + 1]
                cc_, ss_ = Cc[:, b, n - 1:n], Ss[:, b, n - 1:n]
                t1_, t2_ = tcs[q][0], tcs[q][1]
                K.op(K.dve, lambda r: r.tensor_tensor(out=t1_[:, :], in0=wi, in1=ss_, op=ALU.mult),
                     [A["wim"], Ss], [t1_])
                K.op(K.dve, lambda r: r.scalar_tensor_tensor(out=S["cre"][:, b:b + 1], in0=wr, scalar=cc_,
                                                             in1=t1_[:, :], op0=ALU.mult, op1=ALU.subtract),
                     [A["wre"], Cc, t1_], [cre_d[b]])
                K.op(K.dve, lambda r: r.tensor_tensor(out=t2_[:, :], in0=wi, in1=cc_, op=ALU.mult),
                     [A["wim"], Cc], [t2_])
                K.op(K.dve, lambda r: r.scalar_tensor_tensor(out=S["cim"][:, b:b + 1], in0=wr, scalar=ss_,
                                                             in1=t2_[:, :], op0=ALU.mult, op1=ALU.add),
                     [A["wre"], Ss, t2_], [cim_d[b]])
                prods = ((Cc, "wre", CTr), (Ss, "wim", CTrn), (Ss, "wre", CTin), (Cc, "wim", CTin))
                xs = []
                for qi, (T, wn, CT) in enumerate(prods):
                    xo = xb[qi][q]
                    K.op(K.dve, lambda r: r.tensor_tensor(out=xo[:, 0:n], in0=A[wn][:, 0:n], in1=tv(T, b),
                                                          op=ALU.mult), [A[wn], T], [xo])
                    xs.append((xo, CT))
                yield
                for qi, (xo, CT) in enumerate(xs):
                    K.mm(py[:, 0:n], CT[:, b, :], xo[:, 0:n], b4 == 0 and qi == 0, b4 == 3 and qi == 3,
                         [CT, xo], [py])

            run_pipelined(range(8), blk_gen, S5W)
            for f in range(2):
                py = pY[f]
                key = (f, t0)
                if d == 0:
                    ys = ysb[f]
                    K.op(K.act, lambda r: r.copy(out=ys[:, 0:n], in_=py[:, 0:n]), [py], [ys])
                    ydep[key] = Dep()
                    K.dma(G["ysc"][f * 128:(f + 1) * 128, t0:t0 + n], ys[:, 0:n], reads=[ys], writes=[ydep[key]])
                elif not (last and is_ctx):
                    K.dma(yfw[f][:, 0:n], G["ysc"][f * 128:(f + 1) * 128, t0:t0 + n], reads=[ydep[key]],
                          writes=[yfw[f]])
                    K.op(K.dve, lambda r: r.tensor_tensor(out=ysb[f][:, 0:n], in0=py[:, 0:n], in1=yfw[f][:, 0:n],
                                                          op=ALU.add), [py, yfw[f]], [ysb[f]])
                    K.op(K.dve, lambda r: r.scalar_tensor_tensor(out=ysb[f][:, 0:n], in0=u32[:, f, 0:n],
                                                                 scalar=d2[:, f:f + 1], in1=ysb[f][:, 0:n],
                                                                 op0=ALU.mult, op1=ALU.add),
                         [u32, d2, ysb[f]], [ysb[f]])
                    K.op(K.act, lambda r: r.activation(out=yg[f][:, 0:n], in_=ysb[f][:, 0:n],
                                                       func=AF.Gelu_apprx_tanh), [ysb[f]], [yg[f]])
                    K.op(K.act, lambda r: r.copy(out=ygb[f][:, 0:n], in_=yg[f][:, 0:n]), [yg[f]], [ygb[f]])
            if d == 1 and not (last and is_ctx):
                for fo in range(2):
                    pg = pG[fo]
                    for fi in range(2):
                        K.mm(pg[:, 0:n], wg[:, fi, fo * 128:(fo + 1) * 128], ygb[fi][:, 0:n], fi == 0, fi == 1,
                             [wg, ygb[fi]], [pg])
                    K.op(K.act, lambda r: r.activation(out=sig[fo][:, 0:n], in_=pg[:, 0:n], func=AF.Sigmoid,
                                                       bias=bg[:, fo:fo + 1]), [pg, bg], [sig[fo]])
                    K.op(K.pool, lambda r: r.tensor_tensor(out=mxo[fo][:, 0:n], in0=yg[fo][:, 0:n],
                                                           in1=sig[fo][:, 0:n], op=ALU.mult),
                         [yg[fo], sig[fo]], [mxo[fo]])
                    K.dma(G["mxT"][fo * 128:(fo + 1) * 128, t0:t0 + n], mxo[fo][:, 0:n], reads=[mxo[fo]])
    ph.close()


def phaseB2(K, G, l, segs, last):
    ph = Phase(K)
    L = G["L"]
    mask = ph.sb("g_mask", [128, 2, 128])
    K.dma(mask[:, :, :], G["masks"].rearrange("d s t -> s d t"), writes=[mask])
    rmask = ph.sb("g_rmask", [128, 512])
    K.op(K.dve, lambda r: r.memset(rmask[:, :], 1.0), [], [rmask])
    K.op(K.dve, lambda r: r.memset(rmask[:, :].rearrange("p (c j) -> p c j", j=128)[:, :, 0:1], 0.0), [rmask], [rmask])
    sel = ph.sb("g_sel", [128, 3 * 8 * 96])
    K.dma(sel[:, :], G["gsel"][:, :], writes=[sel])
    wa2f = ph.sb("g_wa2f", [48, 256])
    wa2 = ph.sb("g_wa2", [48, 256], BF16)
    K.dma(wa2f[:, :], G["gla_wa2"][l], writes=[wa2f])
    K.op(K.dve, lambda r: r.tensor_copy(out=wa2[:, :], in_=wa2f[:, :]), [wa2f], [wa2])
    nba = ph.sb("g_nba", [128, 4])
    K.dma(nba[:, :], G["gla_ba"][l], writes=[nba])
    K.op(K.dve, lambda r: r.tensor_scalar(out=nba[:, :], in0=nba[:, :], scalar1=-1.0, scalar2=None, op0=ALU.mult),
         [nba], [nba])
    gbias = ph.sb("g_gbias", [128, 1])
    K.dma(gbias[:, :], G["ml_gb"][l], writes=[gbias])
    grow = ph.sb("g_grow", [128, 768])
    K.dma(grow[:, :], G["gmix"][l:l + 1, :].to_broadcast([128, 768]), writes=[grow])
    identb = ph.sb("g_identb", [128, 128], BF16)
    K.op(K.dve, lambda r: r.tensor_copy(out=identb[:, :], in_=G["identf"][:, :]), [G["identf"]], [identb])
    gqk = ph.sb("g_gqk", [128, 4, 512])
    lrg = ph.sb("g_lrg", [128, 512])
    lrb = ph.sb("g_lrb", [128, 512], BF16)
    spb = ph.sb("g_sp", [128, 2, 512])
    Bp = ph.sb("g_Bp", [128, 2, 512])
    eb = ph.sb("g_eb", [128, 2, 512])
    enb = ph.sb("g_enb", [128, 2, 512])
    qd = ph.sb("g_qd", [128, 2, 512], BF16)
    kd = ph.sb("g_kd", [128, 2, 512], BF16)
    mqk = ph.sb("g_mqk", [128, 8, 512])
    mqd = ph.sb("g_mqd", [128, 4, 512], BF16)
    mkd = ph.sb("g_mkd", [128, 4, 512], BF16)
    Aq = ph.sb("g_Aq", [128, 4, 512])
    Bk = [ph.sb("g_Bk%d" % i, [128, 512]) for i in range(2)]
    gt1 = ph.sb("g_gt1", [128, 512])
    gsp = ph.sb("g_gsp", [128, 512])
    gF = ph.sb("g_gF", [128, 512])
    gtmp = ph.sb("g_gtmp", [128, 512])
    vst = [ph.sb("g_vst%d" % i, [128, NTM]) for i in range(2)]
    Vg = [ph.sb("g_Vg%d" % i, [128, 4, 96], BF16) for i in range(2)]
    Vm = [ph.sb("g_Vm%d" % i, [128, 4, 97], BF16) for i in range(2)]
    for i in range(2):
        K.op(K.dve, lambda r: r.memset(Vm[i][:, :, 96:97], 1.0), [], [Vm[i]])
    attm = [ph.sb("g_attm%d" % i, [128, 128], BF16) for i in range(8)]
    kdT = [ph.sb("g_kdT%d" % i, [128, 128], BF16) for i in range(8)]
    psA_d = [Dep() for _ in range(8)]
    psK_d = [Dep() for _ in range(8)]
    psO_d = [Dep() for _ in range(8)]
    psS_d = [Dep() for _ in range(4)]
    ob_d = [[Dep() for _ in range(8)] for _ in range(2)]
    Sst = ph.sb("g_S", [128, 8, 97])
    Sbf = ph.sb("g_Sbf", [128, 8, 97], BF16)
    Sd = [Dep() for _ in range(8)]
    Sbd = [Dep() for _ in range(8)]
    tmpS = [ph.sb("g_tmpS%d" % i, [128, 97]) for i in range(4)]
    osb = [ph.sb("g_osb%d" % i, [128, 776]) for i in range(2)]
    hsb = [ph.sb("g_hsb%d" % i, [128, 768]) for i in range(2)]
    ofw = [ph.sb("g_ofw%d" % i, [128, 768]) for i in range(2)]
    den = [ph.sb("g_den%d" % i, [128, 4]) for i in range(2)]
    sq = [ph.sb("g_sq%d" % i, [128, 768]) for i in range(2)]
    st1 = [ph.sb("g_st1%d" % i, [128, 8]) for i in range(2)]
    st2 = [ph.sb("g_st2%d" % i, [128, 8]) for i in range(2)]
    st3 = [ph.sb("g_st3%d" % i, [128, 8]) for i in range(2)]
    gact = [ph.sb("g_gact%d" % i, [128, 768]) for i in range(2)]
    mxb = [ph.sb("g_mxb%d" % i, [128, 768], BF16) for i in range(2)]
    mxt = [ph.sb("g_mxt%d" % i, [128, 6, 128], BF16) for i in range(2)]
    psZ = ph.ps("g_psZ")
    psA = [ph.ps("g_psA%d" % i) for i in range(2)]
    psK = ph.ps("g_psK", [128, 1024], BF16)
    psTp = ph.ps("g_psTp", [128, 1024], BF16)
    psO = ph.ps("g_psO", [128, 1024])
    psS = ph.ps("g_psS")
    odep = {}
    ntile = 0
    nhd = 0
    for d in range(2):
        rev = d == 1
        K.op(K.dve, lambda r: r.memset(Sst[:, :, :], 0.0), Sd, Sd)
        K.op(K.pool, lambda r: r.memset(Sbf[:, :, :], 0.0), Sbd, Sbd)
        order = segs if d == 0 else [segs[0]] + segs[1:][::-1]
        for (t0, n, is_ctx) in order:
            nt = n // 128
            K.dma(gqk[:, :, 0:n], G["zF"][256:768, t0:t0 + n].rearrange("(f p) t -> p f t", p=128), writes=[gqk])
            K.dma(lrg[0:80, 0:n], G["zF"][768:848, t0:t0 + n], writes=[lrg])
            K.dma(mqk[0:96, :, 0:n], G["zF"][896:1920, t0:t0 + n].rearrange("(f p) t -> p f t", p=128)[0:96],
                  writes=[mqk])
            K.op(K.act, lambda r: r.copy(out=lrb[0:48, 0:n], in_=lrg[0:48, 0:n]), [lrg], [lrb])
            r0 = 32 * d
            c3 = lambda ap: ap.rearrange("p (c j) -> p c j", j=128)
            for hp in range(2):
                K.mm(psZ[:, 0:n], wa2[r0:r0 + 16, hp * 128:(hp + 1) * 128], lrb[r0:r0 + 16, 0:n], True, True,
                     [wa2, lrb], [psZ])
                K.op(K.act, lambda r: r.activation(out=spb[:, hp, 0:n], in_=psZ[:, 0:n], func=AF.Exp, scale=-1.0,
                                                   bias=nba[:, 2 * d + hp:2 * d + hp + 1]), [psZ, nba], [spb])
                K.op(K.act, lambda r: r.activation(out=spb[:, hp, 0:n], in_=spb[:, hp, 0:n], func=AF.Ln, bias=1.0),
                     [spb], [spb])
                K.op(K.dve, lambda r: r.tensor_tensor_scan(out=Bp[:, hp, 0:n], data0=rmask[:, 0:n],
                                                           data1=spb[:, hp, 0:n], initial=0.0, op0=ALU.mult,
                                                           op1=ALU.add), [spb, rmask], [Bp])
                if rev:
                    K.op(K.dve, lambda r: r.tensor_tensor(out=c3(spb[:, hp, 0:n]), in0=c3(spb[:, hp, 0:n]),
                                                          in1=c3(Bp[:, hp, 0:n]), op=ALU.subtract), [spb, Bp], [spb])
                    K.op(K.dve, lambda r: r.tensor_tensor(
                        out=c3(Bp[:, hp, 0:n]), in0=c3(spb[:, hp, 0:n]),
                        in1=c3(Bp[:, hp, 0:n])[:, :, 127:128].to_broadcast([128, nt, 128]), op=ALU.add),
                        [spb, Bp], [Bp])
                K.op(K.act, lambda r: r.activation(out=eb[:, hp, 0:n], in_=Bp[:, hp, 0:n], func=AF.Exp,
                                                   scale=-1.0 / 16.0), [Bp], [eb])
                K.op(K.act, lambda r: r.activation(out=enb[:, hp, 0:n], in_=Bp[:, hp, 0:n], func=AF.Exp,
                                                   scale=1.0 / 16.0), [Bp], [enb])
                K.op(K.pool, lambda r: r.tensor_tensor(out=qd[:, hp, 0:n], in0=gqk[:, hp, 0:n], in1=eb[:, hp, 0:n],
                                                       op=ALU.mult), [gqk, eb], [qd])
                K.op(K.pool, lambda r: r.tensor_tensor(out=kd[:, hp, 0:n], in0=gqk[:, 2 + hp, 0:n],
                                                       in1=enb[:, hp, 0:n], op=ALU.mult), [gqk, enb], [kd])
            G_ = slice(64, 80)
            K.op(K.dve, lambda r: r.tensor_scalar(out=gt1[G_, 0:n], in0=lrg[G_, 0:n], scalar1=gbias[G_, 0:1],
                                                  scalar2=None, op0=ALU.add), [lrg, gbias], [gt1])
            K.op(K.act, lambda r: r.activation(out=gsp[G_, 0:n], in_=gt1[G_, 0:n], func=AF.Exp, scale=-1.0),
                 [gt1], [gsp])
            K.op(K.act, lambda r: r.activation(out=gsp[G_, 0:n], in_=gsp[G_, 0:n], func=AF.Ln, bias=1.0),
                 [gsp], [gsp])
            K.op(K.dve, lambda r: r.tensor_tensor_scan(out=gF[G_, 0:n], data0=rmask[G_, 0:n], data1=gsp[G_, 0:n],
                                                       initial=0.0, op0=ALU.mult, op1=ALU.add), [gsp, rmask], [gF])
            if rev:
                K.op(K.dve, lambda r: r.tensor_tensor(out=c3(gtmp[G_, 0:n]), in0=c3(gsp[G_, 0:n]),
                                                      in1=c3(gF[G_, 0:n]), op=ALU.subtract), [gsp, gF], [gtmp])
                K.op(K.dve, lambda r: r.tensor_tensor(
                    out=c3(gF[G_, 0:n]), in0=c3(gtmp[G_, 0:n]),
                    in1=c3(gF[G_, 0:n])[:, :, 127:128].to_broadcast([16, nt, 128]), op=ALU.add), [gtmp, gF], [gF])
            for h in range(4):
                sI = sel[G_, ((0 * 2 + d) * 4 + h) * 96:((0 * 2 + d) * 4 + h + 1) * 96]
                sFp = sel[G_, ((1 * 2 + d) * 4 + h) * 96:((1 * 2 + d) * 4 + h + 1) * 96]
                sFn = sel[G_, ((2 * 2 + d) * 4 + h) * 96:((2 * 2 + d) * 4 + h + 1) * 96]
                K.mm(psZ[0:96, 0:n], sFn, gF[G_, 0:n], True, True, [sel, gF], [psZ])
                K.op(K.act, lambda r: r.activation(out=Aq[0:96, h, 0:n], in_=psZ[0:96, 0:n], func=AF.Exp),
                     [psZ], [Aq])
                K.mm(psZ[0:96, 0:n], sI, gt1[G_, 0:n], True, False, [sel, gt1], [psZ])
                K.mm(psZ[0:96, 0:n], sFp, gF[G_, 0:n], False, True, [sel, gF], [psZ])
                bk = Bk[h % 2]
                K.op(K.act, lambda r: r.activation(out=bk[0:96, 0:n], in_=psZ[0:96, 0:n], func=AF.Exp), [psZ], [bk])
                K.op(K.pool, lambda r: r.tensor_tensor(out=mqd[0:96, h, 0:n], in0=mqk[0:96, h, 0:n],
                                                       in1=Aq[0:96, h, 0:n], op=ALU.mult), [mqk, Aq], [mqd])
                K.op(K.pool, lambda r: r.tensor_tensor(out=mkd[0:96, h, 0:n], in0=mqk[0:96, 4 + h, 0:n],
                                                       in1=bk[0:96, 0:n], op=ALU.mult), [mqk, bk], [mkd])
            tiles = list(range(nt))[::-1] if rev else list(range(nt))
            items = []
            for ti in tiles:
                for hd in range(8):
                    items.append(("h", ntile, ti, hd))
                items.append(None)
                items.append(("f", ntile, ti, 0))
                ntile += 1

            def head_gen(item):
                _, tl, ti, hd = item
                sl = tl % 2
                rr = t0 + ti * 128
                cs = slice(ti * 128, (ti + 1) * 128)
                lastc = ti * 128 if rev else ti * 128 + 127
                if hd == 0:
                    K.dma(vst[sl][:, :], G["zT"][rr:rr + 128, :], writes=[vst[sl]])
                    K.op(K.act, lambda r: r.copy(out=Vg[sl][:, :, :],
                                                 in_=vst[sl][:, 0:384].rearrange("p (h v) -> p h v", v=96)),
                         [vst[sl]], [Vg[sl]])
                    K.op(K.act, lambda r: r.copy(out=Vm[sl][:, :, 0:96],
                                                 in_=vst[sl][:, 768:1152].rearrange("p (h v) -> p h v", v=96)),
                         [vst[sl]], [Vm[sl]])
                if hd < 4:
                    DK, DV = 48, 96
                    rows = slice(64 * (hd % 2), 64 * (hd % 2) + 48)
                    qv, kv = qd[rows, hd // 2, cs], kd[rows, hd // 2, cs]
                    ebl = eb[rows, hd // 2, lastc:lastc + 1]
                    V = Vg[sl][:, hd, :]
                    qb, kb, ebb, Vb = qd, kd, eb, Vg[sl]
                else:
                    DK, DV = 96, 97
                    rows = slice(0, 96)
                    qv, kv = mqd[rows, hd - 4, cs], mkd[rows, hd - 4, cs]
                    ebl = Aq[rows, hd - 4, lastc:lastc + 1]
                    V = Vm[sl][:, hd - 4, :]
                    qb, kb, ebb, Vb = mqd, mkd, Aq, Vm[sl]
                a = hd % 4
                a8 = (tl % 2) * 4 + a if False else hd
                pav = psA[hd // 4][:, a * 128:(a + 1) * 128]
                K.mm(pav, kv, qv, True, True, [kb, qb], [psA_d[hd]])
                if hd < 4:
                    pkv = psK[:, hd * 128:(hd + 1) * 128]
                    K.tr(pkv, kd[:, hd // 2, cs], identb[:, :], [kd, identb], [psK_d[hd]])
                    MM = 128
                else:
                    pkv = psK[:, hd * 128: hd * 128 + DK]
                    K.tr(pkv, kv, identb[rows, rows], [kb, identb], [psK_d[hd]])
                    MM = DK
                yield
                am, kt = attm[hd], kdT[hd]
                K.op(K.dve, lambda r: r.tensor_tensor(out=am[:, :], in0=pav, in1=mask[:, d, :], op=ALU.mult),
                     [psA_d[hd], mask], [am])
                K.op(K.act, lambda r: r.copy(out=kt[:, 0:MM], in_=pkv), [psK_d[hd]], [kt])
                yield
                po = psO[:, hd * 128:hd * 128 + DV]
                K.mm(po, am[:, :], V, True, False, [am, Vb], [psO_d[hd]])
                K.mm(po, qv, Sbf[rows, hd, 0:DV], False, True, [qb, Sbd[hd]], [psO_d[hd]])
                psv = psS[rows, a * 128:a * 128 + DV]
                K.mm(psS[0:MM, a * 128:a * 128 + DV], kt[:, 0:MM], V, True, True, [kt, Vb], [psS_d[a]])
                yield
                ts_ = tmpS[a]
                K.op(K.dve, lambda r: r.tensor_tensor(out=ts_[rows, 0:DV], in0=psv, in1=Sst[rows, hd, 0:DV],
                                                      op=ALU.add), [psS_d[a], Sd[hd]], [ts_])
                K.op(K.dve, lambda r: r.tensor_scalar(out=Sst[rows, hd, 0:DV], in0=ts_[rows, 0:DV], scalar1=ebl,
                                                      scalar2=None, op0=ALU.mult), [ts_, ebb], [Sd[hd]])
                K.op(K.act, lambda r: r.copy(out=Sbf[rows, hd, 0:DV], in_=Sst[rows, hd, 0:DV]),
                     [Sd[hd]], [Sbd[hd]])
                ob = osb[sl]
                K.op(K.act, lambda r: r.copy(out=ob[:, hd * 97:hd * 97 + DV], in_=po), [psO_d[hd]], [ob_d[sl][hd]])

            def fin_gen(item):
                _, tl, ti, _ = item
                sl = tl % 2
                rr = t0 + ti * 128
                do_post = rev and not (last and is_ctx)
                ob = osb[sl]
                obd = ob_d[sl]
                o8 = ob[:, :].rearrange("p (h v) -> p h v", v=97)
                K.op(K.act, lambda r: r.activation(out=den[sl][:, :], in_=o8[:, 4:8, 96], func=AF.Abs),
                     obd, [den[sl]])
                K.op(K.dve, lambda r: r.tensor_scalar(out=den[sl][:, :], in0=den[sl][:, :], scalar1=1.0, scalar2=None,
                                                      op0=ALU.max), [den[sl]], [den[sl]])
                K.op(K.dve, lambda r: r.reciprocal(out=den[sl][:, :], in_=den[sl][:, :]), [den[sl]], [den[sl]])
                K.op(K.dve, lambda r: r.tensor_tensor(
                    out=o8[:, 4:8, 0:96], in0=o8[:, 4:8, 0:96],
                    in1=den[sl][:, :].unsqueeze(2).to_broadcast([128, 4, 96]), op=ALU.mult), obd + [den[sl]], obd)
                if not rev:
                    if not (last and is_ctx):
                        odep[rr] = Dep()
                        K.dma(G["oF"][rr:rr + 128, :].rearrange("p (h v) -> p h v", v=96), o8[:, :, 0:96],
                              reads=obd, writes=[odep[rr]])
                    return
                if not do_post:
                    return
                K.dma(ofw[sl][:, :], G["oF"][rr:rr + 128, :], reads=[odep[rr]], writes=[ofw[sl]])
                yield
                hs = hsb[sl]
                h8 = hs[:, :].rearrange("p (h v) -> p h v", v=96)
                K.op(K.pool, lambda r: r.tensor_tensor(out=h8, in0=o8[:, :, 0:96],
                                                       in1=ofw[sl][:, :].rearrange("p (h v) -> p h v", v=96),
                                                       op=ALU.add), obd + [ofw[sl]], [hs])
                yield
                K.op(K.dve, lambda r: r.tensor_reduce(out=st1[sl][:, :], in_=h8, axis=AX.X, op=ALU.add), [hs], [st1[sl]])
                K.op(K.pool, lambda r: r.tensor_tensor(out=sq[sl][:, :], in0=hs[:, :], in1=hs[:, :], op=ALU.mult),
                     [hs], [sq[sl]])
                K.op(K.act, lambda r: r.activation(out=gact[sl][:, 0:384], in_=vst[sl][:, 384:768], func=AF.Silu),
                     [vst[sl]], [gact[sl]])
                K.op(K.act, lambda r: r.activation(out=gact[sl][:, 384:768], in_=vst[sl][:, 1152:1536],
                                                   func=AF.Sigmoid), [vst[sl]], [gact[sl]])
                yield
                K.op(K.dve, lambda r: r.tensor_reduce(out=st2[sl][:, :],
                                                      in_=sq[sl][:, :].rearrange("p (h v) -> p h v", v=96),
                                                      axis=AX.X, op=ALU.add), [sq[sl]], [st2[sl]])
                K.op(K.dve, lambda r: r.tensor_scalar(out=st1[sl][:, :], in0=st1[sl][:, :], scalar1=1.0 / 96.0,
                                                      scalar2=None, op0=ALU.mult), [st1[sl]], [st1[sl]])
                K.op(K.dve, lambda r: r.tensor_tensor(out=st3[sl][:, :], in0=st1[sl][:, :], in1=st1[sl][:, :],
                                                      op=ALU.mult), [st1[sl]], [st3[sl]])
                K.op(K.dve, lambda r: r.scalar_tensor_tensor(out=st2[sl][:, :], in0=st2[sl][:, :], scalar=1.0 / 96.0,
                                                             in1=st3[sl][:, :], op0=ALU.mult, op1=ALU.subtract),
                     [st2[sl], st3[sl]], [st2[sl]])
                K.op(K.act, lambda r: r.activation(out=st2[sl][:, :], in_=st2[sl][:, :], func=AF.Sqrt,
                                                   bias=G_EPS[0][:, 0:1]), [st2[sl], G_EPS[0]], [st2[sl]])
                yield
                K.op(K.dve, lambda r: r.reciprocal(out=st2[sl][:, :], in_=st2[sl][:, :]), [st2[sl]], [st2[sl]])
                K.op(K.dve, lambda r: r.tensor_tensor(out=h8, in0=h8,
                                                      in1=st1[sl][:, :].unsqueeze(2).to_broadcast([128, 8, 96]),
                                                      op=ALU.subtract), [hs, st1[sl]], [hs])
                K.op(K.dve, lambda r: r.tensor_tensor(out=h8, in0=h8,
                                                      in1=st2[sl][:, :].unsqueeze(2).to_broadcast([128, 8, 96]),
                                                      op=ALU.mult), [hs, st2[sl]], [hs])
                K.op(K.pool, lambda r: r.tensor_tensor(out=gact[sl][:, :], in0=gact[sl][:, :], in1=grow[:, :],
                                                       op=ALU.mult), [gact[sl], grow], [gact[sl]])
                yield
                mb = mxb[sl]
                K.op(K.dve, lambda r: r.tensor_tensor(out=mb[:, :], in0=hs[:, :], in1=gact[sl][:, :], op=ALU.mult),
                     [hs, gact[sl]], [mb])
                yield
                for c in range(6):
                    K.tr(psTp[:, c * 128:(c + 1) * 128], mb[:, c * 128:(c + 1) * 128], identb[:, :], [mb, identb],
                         [psTp])
                yield
                mt = mxt[sl]
                K.op(K.act, lambda r: r.copy(out=mt[:, :, :], in_=psTp[:, 0:768].rearrange("p (c t) -> p c t", t=128)),
                     [psTp], [mt])
                K.dma(G["mxT"][256:1024, rr:rr + 128].rearrange("(c p) t -> p c t", p=128), mt[:, :, :], reads=[mt])

            run_pipelined(items, lambda it: head_gen(it) if it[0] == "h" else fin_gen(it), B2W)
    ph.close()


def phaseC1a(K, G, l, Hin, segs):
    ph = Phase(K)
    wo = ph.sb("c_wo", [128, 8, D], BF16)
    stage = [ph.sb("c_st%d" % i, [128, 2048]) for i in range(2)]
    load_weight_bf16(K, ph, wo, G["w_out"][l], D, stage)
    mxs = [ph.sb("c_mx%d" % i, [128, 8, 512], BF16) for i in range(2)]
    NS = 4
    hx = [ph.sb("c_hx%d" % i, [128, 1024]) for i in range(NS)]
    tmp = [ph.sb("c_tmp%d" % i, [128, 1024]) for i in range(NS)]
    xm2 = [ph.sb("c_xm2%d" % i, [128, 8, 128], BF16) for i in range(NS)]
    bn = [ph.sb("c_bn%d" % i, [128, 12]) for i in range(NS)]
    mv = [ph.sb("c_mv%d" % i, [128, 2]) for i in range(NS)]
    rs = [ph.sb("c_rs%d" % i, [128, 1]) for i in range(NS)]
    nb = [ph.sb("c_nb%d" % i, [128, 1]) for i in range(NS)]
    fence = [ph.sb("c_fence%d" % i, [128, 1]) for i in range(NS)]
    ph1 = Phase(K)
    psO = [ph1.ps("c_psO%d" % i) for i in range(8)]
    ident, modF = G["identf"], G["modF"]
    lng = ph.sb("c_lng", [128, 1024])
    lnb = ph.sb("c_lnb", [128, 1024])
    K.dma(lng[:, :], G["lnp"][l, 0:1, :].to_broadcast([128, D]), writes=[lng])
    K.dma(lnb[:, :], G["lnp"][l, 1:2, :].to_broadcast([128, D]), writes=[lnb])
    def ln_a(x, s):
        K.op(K.dve, lambda r: r.bn_stats(out=bn[s][:, 0:6], in_=x[:, 0:512]), [x], [bn[s]])
        K.op(K.dve, lambda r: r.bn_stats(out=bn[s][:, 6:12], in_=x[:, 512:1024]), [x], [bn[s]])
        K.op(K.dve, lambda r: r.bn_aggr(out=mv[s][:, :], in_=bn[s][:, :]), [bn[s]], [mv[s]])
        K.op(K.act, lambda r: r.activation(out=rs[s][:, :], in_=mv[s][:, 1:2], func=AF.Sqrt,
                                           bias=G_EPS[0][:, 0:1]), [mv[s], G_EPS[0]], [rs[s]])

    def ln_b(x, o, s):
        K.op(K.dve, lambda r: r.reciprocal(out=rs[s][:, :], in_=rs[s][:, :]), [rs[s]], [rs[s]])
        K.op(K.dve, lambda r: r.tensor_scalar(out=nb[s][:, :], in0=mv[s][:, 0:1], scalar1=rs[s][:, 0:1],
                                              scalar2=-1.0, op0=ALU.mult, op1=ALU.mult), [mv[s], rs[s]], [nb[s]])
        K.op(K.dve, lambda r: r.memset(fence[s][:, :], 0.0), [rs[s], nb[s]], [fence[s]])
        K.op(K.act, lambda r: r.activation(out=o[:, :], in_=x[:, :], func=AF.Identity, scale=rs[s][:, 0:1],
                                           bias=nb[s][:, 0:1]), [x, rs[s], nb[s], fence[s]], [o])

    items = []
    for si, (t0, n, is_ctx) in enumerate(segs):
        for ti in range(n // 128):
            items.append((len(items), si, t0, n, is_ctx, ti))

    def tile1(item):
        cnt, si, t0, n, is_ctx, ti = item
        j = 1 if is_ctx else 0
        g1 = G["modR"][0][j]
        ms = mxs[si % 2]
        s = cnt % NS
        r0 = t0 + ti * 128
        if ti == 0:
            K.dma(ms[:, :, 0:n], G["mxT"][:, t0:t0 + n].rearrange("(k p) t -> p k t", p=128), writes=[ms])
        K.dma(hx[s][:, :], Hin[r0:r0 + 128, :], writes=[hx[s]])
        ps_ = []
        for hf in range(2):
            p = psO[(2 * cnt + hf) % 8]
            for k in range(8):
                K.mm(p[:, :], ms[:, k, ti * 128:(ti + 1) * 128], wo[:, k, hf * 512:(hf + 1) * 512],
                     k == 0, k == 7, [ms, wo], [p])
            ps_.append(p)
        yield
        for hf in range(2):
            p = ps_[hf]
            K.op(K.dve, lambda r: r.tensor_tensor(out=tmp[s][:, hf * 512:(hf + 1) * 512], in0=p[:, :],
                                                  in1=g1[:, hf * 512:(hf + 1) * 512], op=ALU.mult),
                 [p, g1], [tmp[s]])
        K.op(K.dve, lambda r: r.scalar_tensor_tensor(out=hx[s][:, :], in0=hx[s][:, :], scalar=ALPHA,
                                                     in1=tmp[s][:, :], op0=ALU.mult, op1=ALU.add),
             [hx[s], tmp[s]], [hx[s]])
        ln_a(hx[s], s)
        yield
        ln_b(hx[s], tmp[s], s)
        yield
        K.op(K.pool, lambda r: r.tensor_tensor(out=tmp[s][:, :], in0=tmp[s][:, :], in1=lng[:, :],
                                               op=ALU.mult), [tmp[s], lng], [tmp[s]])
        yield
        K.op(K.dve, lambda r: r.tensor_tensor(out=hx[s][:, :], in0=tmp[s][:, :], in1=lnb[:, :],
                                              op=ALU.add), [tmp[s], lnb], [hx[s]])
        h1dep[r0] = Dep()
        K.dma(G["H1"][r0:r0 + 128, :], hx[s][:, :], reads=[hx[s]], writes=[h1dep[r0]])

    def tile2(item):
        cnt, si, t0, n, is_ctx, ti = item
        j = 1 if is_ctx else 0
        s = cnt % NS
        r0 = t0 + ti * 128
        K.dma(hx[s][:, :], G["H1"][r0:r0 + 128, :], reads=[h1dep[r0]], writes=[hx[s]])
        yield
        ln_a(hx[s], s)
        yield
        ln_b(hx[s], tmp[s], s)
        yield
        pt = psT[cnt % 4]
        for k in range(8):
            K.tr(pt[:, k * 128:(k + 1) * 128], tmp[s][:, k * 128:(k + 1) * 128], ident[:, :],
                 [tmp[s], ident], [pt])
        yield
        for k in range(8):
            src = pt[:, k * 128:(k + 1) * 128]
            dst = xm2[s][:, k, :]
            if k % 2 == 0:
                K.op(K.act, lambda r: r.activation(out=dst, in_=src, func=AF.Identity,
                                                   scale=modF[:, 32 + k, j:j + 1], bias=modF[:, 24 + k, j:j + 1]),
                     [pt, modF], [xm2[s]])
            else:
                K.op(K.dve, lambda r: r.tensor_scalar(out=dst, in0=src, scalar1=modF[:, 32 + k, j:j + 1],
                                                      scalar2=modF[:, 24 + k, j:j + 1], op0=ALU.mult,
                                                      op1=ALU.add), [pt, modF], [xm2[s]])
        K.dma(G["xm2T"][:, r0:r0 + 128].rearrange("(k p) t -> p k t", p=128), xm2[s][:, :, :], reads=[xm2[s]])

    h1dep = {}
    run_pipelined(items, tile1, 4, stagger=True)
    ph1.close()
    psT = [ph.ps("c_psT%d" % i, [128, 1024]) for i in range(4)]
    run_pipelined(items, tile2, 4, stagger=True)
    ph.close()


def phaseC1b(K, G, l, segs):
    ph = Phase(K)
    wu = ph.sb("u_wu", [128, 8, 2 * D_FF], BF16)
    stage = [ph.sb("u_st%d" % i, [128, 2048]) for i in range(2)]
    load_weight_bf16(K, ph, wu, G["w_up"][l], 2 * D_FF, stage)
    xs = [ph.sb("u_x%d" % i, [128, 8, 512], BF16) for i in range(2)]
    sa = [ph.sb("u_sa%d" % i, [128, 512]) for i in range(3)]
    sv = [ph.sb("u_sv%d" % i, [128, 512], BF16) for i in range(3)]
    psU = [ph.ps("u_ps%d" % i) for i in range(6)]
    ne = 0
    for si, (t0, n, is_ctx) in enumerate(segs):
        x = xs[si % 2]
        K.dma(x[:, :, 0:n], G["xm2T"][:, t0:t0 + n].rearrange("(k p) t -> p k t", p=128), writes=[x])
        for m in range(44):
            p = psU[m % 6]
            for k in range(8):
                K.mm(p[:, 0:n], wu[:, k, m * 128:(m + 1) * 128], x[:, k, 0:n], k == 0, k == 7, [wu, x], [p])
            if m < 22:
                s = sa[m % 3]
                _evac(K, ne, s[:, 0:n], p[:, 0:n], [p], [s])
                K.dma(G["aT"][m * 128:(m + 1) * 128, t0:t0 + n], s[:, 0:n], reads=[s])
            else:
                s = sv[m % 3]
                _evac(K, ne, s[:, 0:n], p[:, 0:n], [p], [s])
                K.dma(G["vT"][(m - 22) * 128:(m - 21) * 128, t0:t0 + n], s[:, 0:n], reads=[s])
            ne += 1
    ph.close()


def phaseC2(K, G, l, Hout, segs, last):
    ph = Phase(K)
    L = G["L"]
    wd = ph.sb("d_wd", [128, 22, D], BF16)
    stage = [ph.sb("d_st%d" % i, [128, 2048]) for i in range(2)]
    load_weight_bf16(K, ph, wd, G["w_down"][l], D, stage)
    wdc = ph.sb("d_wdc", [128, 22, 9])
    bdc = ph.sb("d_bdc", [128, 22])
    for t in range(9):
        K.dma(wdc[:, :, t], G["w_dconv"][l, t].rearrange("(c p) -> p c", p=128), writes=[wdc],
              allow_slow_non_contiguous=True)
    K.dma(bdc[:, :], G["b_dconv"][l].rearrange("(c p) -> p c", p=128), writes=[bdc], allow_slow_non_contiguous=True)
    ab = [ph.sb("d_a%d" % i, [128, 640]) for i in range(4)]
    vb = [ph.sb("d_v%d" % i, [128, 512], BF16) for i in range(4)]
    acc = [ph.sb("d_acc%d" % i, [128, 512]) for i in range(4)]
    hm = [ph.sb("d_hm%d" % i, [128, 22, 512], BF16) for i in range(2)]
    h1 = [ph.sb("d_h1%d" % i, [128, 1024]) for i in range(2)]
    tmp = [ph.sb("d_tmp%d" % i, [128, 1024]) for i in range(2)]
    bn = [ph.sb("d_bn%d" % i, [128, 12]) for i in range(2)]
    mv = [ph.sb("d_mv%d" % i, [128, 2]) for i in range(2)]
    rs = [ph.sb("d_rs%d" % i, [128, 1]) for i in range(2)]
    nb = [ph.sb("d_nb%d" % i, [128, 1]) for i in range(2)]
    fence = [ph.sb("d_fence%d" % i, [128, 1]) for i in range(2)]
    psD = [ph.ps("d_ps%d" % i) for i in range(4)]
    lng = ph.sb("d_lng", [128, 1024])
    lnb = ph.sb("d_lnb", [128, 1024])
    K.dma(lng[:, :], G["lnp"][l, 2:3, :].to_broadcast([128, D]), writes=[lng])
    K.dma(lnb[:, :], G["lnp"][l, 3:4, :].to_broadcast([128, D]), writes=[lnb])
    ctm = [ph.sb("d_ctm%d" % i, [128, 512]) for i in range(3)]
    ntm = 0
    items = []
    nchunk = 0
    ntl = 0
    for si, (t0, n, is_ctx) in enumerate(segs):
        for cidx in range(22):
            items.append(("c", nchunk, si, t0, n, is_ctx, cidx))
            nchunk += 1
        items.append(None)
        for ti in range(n // 128):
            items.append(("t", ntl, si, t0, n, is_ctx, ti))
            ntl += 1
    ntm_ = [0]

    def chunk_gen(item):
        _, nch, si, t0, n, is_ctx, c = item
        hms = hm[si % 2]
        sl = nch % 4
        eng = K.pool if c % 3 == 2 else K.dve
        a, v, ac = ab[sl], vb[sl], acc[sl]
        if is_ctx:
            W = n
            K.dma(a[:, 64:64 + n], G["aT"][c * 128:(c + 1) * 128, t0:t0 + n], writes=[a])
            taps = [(0, -1), (0, 1)]
        else:
            W = GRID_W
            lo = t0 - 64 if t0 - 64 >= CTX else t0
            hi = t0 + n + 64 if t0 + n + 64 <= CTX + L else t0 + n
            if lo == t0:
                K.op(eng, lambda r: r.memset(a[:, 0:64], 0.0), [], [a])
            if hi == t0 + n:
                K.op(eng, lambda r: r.memset(a[:, 64 + n:128 + n], 0.0), [], [a])
            K.dma(a[:, 64 - (t0 - lo):64 + n + (hi - t0 - n)], G["aT"][c * 128:(c + 1) * 128, lo:hi], writes=[a])
            taps = [(dy, dx) for dy in (-1, 0, 1) for dx in (-1, 0, 1) if (dy, dx) != (0, 0)]
        K.dma(v[:, 0:n], G["vT"][c * 128:(c + 1) * 128, t0:t0 + n], writes=[v])
        yield
        K.op(eng, lambda r: r.tensor_scalar(out=ac[:, 0:n], in0=a[:, 64:64 + n], scalar1=wdc[:, c, 4:5],
                                            scalar2=bdc[:, c:c + 1], op0=ALU.mult, op1=ALU.add),
             [a, wdc, bdc], [ac])
        for (dy, dx) in taps:
            tap = (dy + 1) * 3 + (dx + 1)
            x0, x1 = max(0, -dx), W - max(0, dx)
            o3 = ac[:, 0:n].rearrange("p (r w) -> p r w", w=W)[:, :, x0:x1]
            base = 64 + dy * 64 if not is_ctx else 64
            i3 = a[:, base:base + n].rearrange("p (r w) -> p r w", w=W)[:, :, x0 + dx:x1 + dx]
            if eng is K.dve:
                K.op(eng, lambda r: r.scalar_tensor_tensor(out=o3, in0=i3, scalar=wdc[:, c, tap:tap + 1], in1=o3,
                                                           op0=ALU.mult, op1=ALU.add), [a, wdc, ac], [ac])
            else:
                tm = ctm[ntm_[0] % 3]
                ntm_[0] += 1
                t3 = tm[:, 0:n].rearrange("p (r w) -> p r w", w=W)[:, :, x0:x1]
                K.op(K.act, lambda r: r.activation(out=t3, in_=i3, func=AF.Copy, scale=wdc[:, c, tap:tap + 1]),
                     [a, wdc], [tm])
                K.op(K.pool, lambda r: r.tensor_tensor(out=o3, in0=o3, in1=t3, op=ALU.add), [tm, ac], [ac])
        yield
        K.op(K.act, lambda r: r.activation(out=ac[:, 0:n], in_=ac[:, 0:n], func=AF.Gelu_apprx_tanh), [ac], [ac])
        yield
        K.op(eng, lambda r: r.tensor_tensor(out=hms[:, c, 0:n], in0=ac[:, 0:n], in1=v[:, 0:n], op=ALU.mult),
             [ac, v], [hms])

    def tile_gen(item):
        _, cnt, si, t0, n, is_ctx, ti = item
        j = 1 if is_ctx else 0
        g2 = G["modR"][1][j]
        hms = hm[si % 2]
        s = cnt % 2
        r0 = t0 + ti * 128
        K.dma(h1[s][:, :], G["H1"][r0:r0 + 128, :], writes=[h1[s]])
        ps_ = []
        for hf in range(2):
            p = psD[(2 * cnt + hf) % 4]
            for c in range(22):
                K.mm(p[:, :], hms[:, c, ti * 128:(ti + 1) * 128], wd[:, c, hf * 512:(hf + 1) * 512],
                     c == 0, c == 21, [hms, wd], [p])
            ps_.append(p)
        yield
        for hf in range(2):
            p = ps_[hf]
            K.op(K.dve, lambda r: r.tensor_tensor(out=tmp[s][:, hf * 512:(hf + 1) * 512], in0=p[:, :],
                                                  in1=g2[:, hf * 512:(hf + 1) * 512], op=ALU.mult),
                 [p, g2], [tmp[s]])
        K.op(K.dve, lambda r: r.scalar_tensor_tensor(out=h1[s][:, :], in0=h1[s][:, :], scalar=ALPHA,
                                                     in1=tmp[s][:, :], op0=ALU.mult, op1=ALU.add),
             [h1[s], tmp[s]], [h1[s]])
        K.op(K.dve, lambda r: r.bn_stats(out=bn[s][:, 0:6], in_=h1[s][:, 0:512]), [h1[s]], [bn[s]])
        K.op(K.dve, lambda r: r.bn_stats(out=bn[s][:, 6:12], in_=h1[s][:, 512:1024]), [h1[s]], [bn[s]])
        K.op(K.dve, lambda r: r.bn_aggr(out=mv[s][:, :], in_=bn[s][:, :]), [bn[s]], [mv[s]])
        K.op(K.act, lambda r: r.activation(out=rs[s][:, :], in_=mv[s][:, 1:2], func=AF.Sqrt,
                                           bias=G_EPS[0][:, 0:1]), [mv[s], G_EPS[0]], [rs[s]])
        yield
        K.op(K.dve, lambda r: r.reciprocal(out=rs[s][:, :], in_=rs[s][:, :]), [rs[s]], [rs[s]])
        K.op(K.dve, lambda r: r.tensor_scalar(out=nb[s][:, :], in0=mv[s][:, 0:1], scalar1=rs[s][:, 0:1],
                                              scalar2=-1.0, op0=ALU.mult, op1=ALU.mult), [mv[s], rs[s]], [nb[s]])
        K.op(K.dve, lambda r: r.memset(fence[s][:, :], 0.0), [rs[s], nb[s]], [fence[s]])
        K.op(K.act, lambda r: r.activation(out=tmp[s][:, :], in_=h1[s][:, :], func=AF.Identity,
                                           scale=rs[s][:, 0:1], bias=nb[s][:, 0:1]),
             [h1[s], rs[s], nb[s], fence[s]], [tmp[s]])
        yield
        K.op(K.pool, lambda r: r.tensor_tensor(out=tmp[s][:, :], in0=tmp[s][:, :], in1=lng[:, :],
                                               op=ALU.mult), [tmp[s], lng], [tmp[s]])
        yield
        K.op(K.dve, lambda r: r.tensor_tensor(out=h1[s][:, :], in0=tmp[s][:, :], in1=lnb[:, :],
                                              op=ALU.add), [tmp[s], lnb], [h1[s]])
        if last:
            K.dma(Hout[r0 - CTX:r0 - CTX + 128, :], h1[s][:, :], reads=[h1[s]])
        else:
            K.dma(Hout[r0:r0 + 128, :], h1[s][:, :], reads=[h1[s]])

    run_pipelined(items, lambda it: chunk_gen(it) if it[0] == "c" else tile_gen(it), 2, stagger=True)
    ph.close()
def build(L, nl=2, stop_after=None, mode=None):
    K = Kern()
    TALL = CTX + L
    G = {}
    G["L"], G["TALL"] = L, TALL
    G["_ext"] = {}

    def ext(name, shape, dt=F32):
        G["_ext"][name] = list(shape)
        return K.dram(name, shape, dt, "ExternalInput")
    G["h0"] = ext("h0", [TALL, D])
    G["cvec"] = ext("cvec", [128, 16])
    G["w_ada"] = ext("w_ada", [nl, D, 6 * D])
    G["b_adaF"] = ext("b_adaF", [nl, 128, 48])
    G["b_ada"] = ext("b_ada", [nl, 6 * D])
    G["wF"] = ext("wF", [nl, D, NFM * 128])
    G["wT"] = ext("wT", [nl, D, NTM])
    G["ident"] = ext("ident", [128, 128])
    G["w_out"] = ext("w_out", [nl, D, D])
    G["w_up"] = ext("w_up", [nl, D, 2 * D_FF])
    G["w_down"] = ext("w_down", [nl, D_FF, D])
    G["w_dconv"] = ext("w_dconv", [nl, 9, D_FF])
    G["b_dconv"] = ext("b_dconv", [nl, D_FF])
    G["lnp"] = ext("lnp", [nl, 4, D])
    G["jjrow"] = ext("jjrow", [128, 512])
    G["s5_par"] = ext("s5_par", [nl, 2, 128, 24])
    G["s5_BT"] = ext("s5_BT", [nl, 2, 2, 128, 1024])
    G["s5_CT"] = ext("s5_CT", [nl, 2, 2, 128, 1024])
    G["s5_d2"] = ext("s5_d2", [nl, 128, 2])
    G["s5_bg2"] = ext("s5_bg2", [nl, 128, 2])
    G["s5_w_glu"] = ext("s5_w_glu", [nl, 256, 256])
    G["ysc"] = K.dram("ysc", [256, TALL], F32)
    G["gla_wa2"] = ext("gla_wa2", [nl, 48, 256])
    G["gla_ba"] = ext("gla_ba", [nl, 128, 4])
    G["ml_gb"] = ext("ml_gb", [nl, 128, 1])
    G["gsel"] = ext("gsel", [128, 3 * 8 * 96])
    G["masks"] = ext("masks", [2, 128, 128])
    G["gmix"] = ext("gmix", [nl, 768])
    G["oF"] = K.dram("oF", [TALL, 768], F32)
    dbgA = stop_after == "A"
    G["zF"] = K.dram("zF", [NFM * 128, TALL], F32, "ExternalOutput" if dbgA else ("ExternalInput" if mode == "testB" else "Internal"))
    G["zT"] = K.dram("zT", [TALL, NTM], F32, "ExternalOutput" if dbgA else ("ExternalInput" if mode == "testB" else "Internal"))
    G["mxT"] = K.dram("mxT", [D, TALL], BF16, "ExternalInput" if mode == "testC" else ("ExternalOutput" if mode == "testB" else "Internal"))
    G["H1"] = K.dram("H1", [TALL, D], F32)
    G["Hs"] = K.dram("Hs", [TALL, D], F32, "ExternalOutput" if mode == "testC" else "Internal")
    G["xm2T"] = K.dram("xm2T", [D, TALL], BF16)
    G["aT"] = K.dram("aT", [D_FF, TALL], F32)
    G["vT"] = K.dram("vT", [D_FF, TALL], BF16)
    G["out"] = K.dram("out", [L, D], F32, "ExternalOutput")
    G["modF"] = K.sb("modF", [128, 48, 2])
    G["modR"] = [[K.sb("modR%d%d" % (w, j), [128, 1024]) for j in range(2)] for w in range(2)]
    G_EPS[0] = K.sb("eps", [128, 1])
    K.op(K.dve, lambda r: r.memset(G_EPS[0][:, :], LN_EPS), [], [G_EPS[0]])
    idf = K.sb("identf", [128, 128])
    G["identf"] = idf
    K.dma(idf[:, :], G["ident"][:, :], writes=[idf])
    segs = make_segs(L)
    Hin = G["h0"]
    if mode == "testB":
        phaseB1(K, G, 0, segs, False)
        phaseB2(K, G, 0, segs, False)
        K.finish()
        return K, G
    for l in range(nl):
        last = l == nl - 1
        phase0(K, G, l)
        if mode != "testC":
            phaseA(K, G, l, Hin, segs)
            if not dbgA:
                phaseB1(K, G, l, segs, last)
                phaseB2(K, G, l, segs, last)
        if dbgA:
            dm = K.dram("dbg_modF", [128, 96], F32, "ExternalOutput")
            K.dma(dm[:, :], G["modF"][:, :, :].rearrange("p c j -> p (c j)"), reads=[G["modF"]])
            break
        segsC = segs[1:] if (last and mode != "testC") else segs
        phaseC1a(K, G, l, Hin, segsC)
        phaseC1b(K, G, l, segsC)
        Hout = G["out"] if (last and mode != "testC") else G["Hs"]
        phaseC2(K, G, l, Hout, segsC, last and mode != "testC")
        Hin = G["Hs"]
    K.finish()
    return K, G


_CACHE = {}


def kernel(**inp):
    x = np.asarray(inp["x"], np.float32)
    B, L, _ = x.shape
    nl = inp["w_in"].shape[0]
    f32 = lambda a: np.ascontiguousarray(np.asarray(a, np.float32))
    wF, wT = _layout_w_in(f32(inp["w_in"]))
    shared = dict(
        w_ada=f32(inp["w_ada"]), b_ada=f32(inp["b_ada"]),
        b_adaF=f32(np.asarray(inp["b_ada"]).reshape(nl, 48, 128).transpose(0, 2, 1)),
        wF=wF, wT=wT, ident=np.eye(128, dtype=np.float32),
        w_out=f32(inp["w_out"]), w_up=f32(inp["w_up"]), w_down=f32(inp["w_down"]),
        w_dconv=f32(np.asarray(inp["w_dconv"]).reshape(nl, 9, D_FF)), b_dconv=f32(inp["b_dconv"]),
        lnp=f32(np.stack([inp["ln1_g"], inp["ln1_b"], inp["ln2_g"], inp["ln2_b"]], axis=1)),
    )
    shared.update(mixer_inputs({k: np.asarray(v, np.float32) for k, v in inp.items()
                                if k.startswith(("s5_", "gla_", "ml_"))}))
    if L not in _CACHE:
        _CACHE[L] = build(L, nl=nl)
    K, G = _CACHE[L]
    cc = np.asarray(inp["c_ctx"], np.float32).reshape(8, 128).T
    in_maps = []
    for b in range(B):
        cb = np.asarray(inp["c"][b], np.float32).reshape(8, 128).T
        m = dict(shared)
        m["h0"] = f32(np.concatenate([inp["ctx"][b], x[b]], axis=0))
        m["cvec"] = f32(np.stack([cb, cc], axis=2).reshape(128, 16))
        in_maps.append(m)
    res = run_bass_kernel_spmd(K.nc, in_maps, core_ids=list(range(B)))
    return np.stack([np.asarray(r["out"], np.float32) for r in res.results], axis=0)
```
